# Optimizing a Trainium2 kernel written in Bass

```python
import jax, jax.numpy as jnp
from jax import lax
import numpy as np

D_MODEL = 2048
BATCH = 2
SEQ = 8192
DEPTH = 2
DEC_BATCH = 32
DEC_SEQ = 64
PAST_LEN = 1024

CHUNK = 64
CHUNK_MLP = 128
D_A = 1024
G_A = 8
DG_A = D_A // G_A
D_B = 1024
HEAD_B = 64
H_B = D_B // HEAD_B
LORA_W = 64
LORA_A = 64
LORA_G = 160
D_SHIFT = 3 * D_B + LORA_W + LORA_A + LORA_G
D_IN = 2 * D_A + D_SHIFT
SPLITS_B = (D_B, 2 * D_B, 3 * D_B, 3 * D_B + LORA_W, 3 * D_B + LORA_W + LORA_A)
D_FF = 4 * D_MODEL
RMS_EPS = 1e-5
LN_EPS = 1e-5
GN_EPS = 64e-5

kernel_name = 'hybrid_gmlp_rwkv7_stream_step'


def rmsnorm(x, g):
    xf = x.astype(jnp.float32)
    y = xf * lax.rsqrt(jnp.mean(xf * xf, axis=-1, keepdims=True) + RMS_EPS)
    return (y * g.astype(jnp.float32)).astype(x.dtype)


def layernorm(x, g, b, eps):
    xf = x.astype(jnp.float32)
    mu = jnp.mean(xf, axis=-1, keepdims=True)
    var = jnp.mean(jnp.square(xf - mu), axis=-1, keepdims=True)
    y = (xf - mu) * lax.rsqrt(var + eps) * g.astype(jnp.float32) + b.astype(jnp.float32)
    return y.astype(x.dtype)


def spatial_gating(u, v, ln_g, ln_b, w_s, b_s):
    bsz, t, _ = v.shape
    vn = layernorm(v, ln_g, ln_b, LN_EPS)
    n_blk = -(-t // CHUNK_MLP)
    vp = jnp.pad(vn, ((0, 0), (0, n_blk * CHUNK_MLP - t), (0, 0)))
    vp = vp.reshape(bsz, n_blk, CHUNK_MLP, G_A, DG_A)
    causal = jnp.tril(jnp.ones((CHUNK_MLP, CHUNK_MLP), dtype=bool))
    w = jnp.where(causal[None], w_s, jnp.zeros_like(w_s)).astype(vp.dtype)
    s = jnp.einsum('gqk,bnkgd->bnqgd', w, vp) + b_s.T.astype(vp.dtype)[None, None, :, :, None]
    s = s.reshape(bsz, n_blk * CHUNK_MLP, D_A)[:, :t]
    return u * s, vn


def wkv7_step(S, inp):
    r_t, w_t, k_t, v_t, kk_t, a_t = inp
    sa = jnp.einsum('bhij,bhj->bhi', S, kk_t)
    S = (S * w_t[:, :, None, :]
         - sa[..., None] * (kk_t * a_t)[:, :, None, :]
         + v_t[..., None] * k_t[:, :, None, :])
    o = jnp.einsum('bhij,bhj->bhi', S, r_t)
    return S, o


def wkv7_scan(s0, r, w, k, v, kk, a):
    xs = tuple(jnp.moveaxis(z, 1, 0) for z in (r, w, k, v, kk, a))
    s_fin, o = lax.scan(wkv7_step, s0, xs)
    return s_fin, jnp.moveaxis(o, 0, 1)


def rwkv7_branch(p, shift_prev, s0, mu, w0, w2, a0, a2, g2, k_k, k_a, r_k, gn_g, gn_b):
    f32 = jnp.float32
    bsz, t, _ = p.shape
    prev = jnp.concatenate([shift_prev[:, None, :].astype(p.dtype), p[:, :-1]], axis=1)
    xs = p + (prev - p) * mu
    r, k, v, xw, xa, xg = jnp.split(xs, SPLITS_B, axis=-1)
    logw = -jax.nn.softplus(-(w0 + jnp.tanh(xw) @ w2).astype(f32)) - 0.5
    decay = jnp.exp(-jnp.exp(logw))
    a = jax.nn.sigmoid((a0 + xa @ a2).astype(f32))
    g = jax.nn.sigmoid(xg) @ g2

    def heads(z):
        return z.astype(f32).reshape(bsz, t, H_B, HEAD_B)

    r_h, k_h, v_h, a_h, w_h = heads(r), heads(k), heads(v), heads(a), heads(decay)
    kk = k_h * k_k.astype(f32).reshape(H_B, HEAD_B)
    kk = kk * lax.rsqrt(jnp.maximum(jnp.sum(kk * kk, axis=-1, keepdims=True), 1e-24))
    k_h = k_h * (1 + (a_h - 1) * k_a.astype(f32).reshape(H_B, HEAD_B))
    s_new, o = wkv7_scan(s0.astype(f32), r_h, w_h, k_h, v_h, kk, a_h)
    mean = jnp.mean(o, axis=-1, keepdims=True)
    var = jnp.mean(jnp.square(o - mean), axis=-1, keepdims=True)
    on = ((o - mean) * lax.rsqrt(var + GN_EPS)).reshape(bsz, t, D_B)
    on = on * gn_g.astype(f32) + gn_b.astype(f32)
    bonus = jnp.sum(r_h * k_h * r_k.astype(f32), axis=-1, keepdims=True) * v_h
    y = ((on + bonus.reshape(bsz, t, D_B)) * g.astype(f32)).astype(p.dtype)
    return y, p[:, -1], s_new.astype(s0.dtype)


def trunk_layer(x, shift_prev, s0, prm):
    h = rmsnorm(x, prm['norm1'])
    proj = h @ prm['w_in']
    u = jax.nn.gelu(proj[..., :D_A])
    v = jax.nn.gelu(proj[..., D_A:2 * D_A])
    y_a, v_rows = spatial_gating(u, v, prm['ln_v_g'], prm['ln_v_b'], prm['w_s'], prm['b_s'])
    y_b, new_shift, s_new = rwkv7_branch(
        proj[..., 2 * D_A:], shift_prev, s0, prm['mu_shift'], prm['w0'], prm['w2'],
        prm['a0'], prm['a2'], prm['g2'], prm['k_k'], prm['k_a'], prm['r_k'],
        prm['gn_g'], prm['gn_b'])
    gates = jax.nn.sigmoid(h @ prm['w_gate'])
    mixed = (gates[..., :D_MODEL] * (y_a @ prm['w_pa'])
             + gates[..., D_MODEL:] * (y_b @ prm['w_pb']))
    x = x + mixed @ prm['w_o']
    h2 = rmsnorm(x, prm['norm2'])
    x = x + jnp.square(jax.nn.relu(h2 @ prm['w_up'])) @ prm['w_down']
    return x, v_rows, new_shift, s_new


def setup_inputs(seed: int = 0) -> dict:
    key = jax.random.key(seed)
    ks = iter(jax.random.split(key, 40))
    f32 = jnp.float32

    def nrm(shape, scale):
        return jax.random.normal(next(ks), shape, f32) * scale

    def unif(shape, lo, hi):
        return jax.random.uniform(next(ks), shape, f32, lo, hi)

    L = DEPTH
    return {
        'x_prompt': nrm((BATCH, SEQ, D_MODEL), 1.0),
        'x_sample': nrm((DEC_BATCH, DEC_SEQ, D_MODEL), 1.0),
        'state_tshift': nrm((L, DEC_BATCH, D_SHIFT), 1.0),
        'state_wkv': nrm((L, DEC_BATCH, H_B, HEAD_B, HEAD_B), 0.3),
        'norm1': 1.0 + nrm((L, D_MODEL), 0.02),
        'w_in': nrm((L, D_MODEL, D_IN), D_MODEL ** -0.5),
        'ln_v_g': 1.0 + nrm((L, D_A), 0.02),
        'ln_v_b': nrm((L, D_A), 0.02),
        'w_s': nrm((L, G_A, CHUNK_MLP, CHUNK_MLP), CHUNK_MLP ** -0.5),
        'b_s': 1.0 + nrm((L, G_A, CHUNK_MLP), 0.02),
        'mu_shift': unif((L, D_SHIFT), 0.0, 1.0),
        'w0': unif((L, D_B), -4.0, 0.0),
        'w2': nrm((L, LORA_W, D_B), 0.1 * LORA_W ** -0.5),
        'a0': nrm((L, D_B), 0.1),
        'a2': nrm((L, LORA_A, D_B), LORA_A ** -0.5),
        'g2': nrm((L, LORA_G, D_B), LORA_G ** -0.5),
        'k_k': 0.85 + nrm((L, D_B), 0.02),
        'k_a': 1.0 + nrm((L, D_B), 0.02),
        'r_k': nrm((L, H_B, HEAD_B), 0.1),
        'gn_g': 1.0 + nrm((L, D_B), 0.02),
        'gn_b': nrm((L, D_B), 0.02),
        'w_gate': nrm((L, D_MODEL, 2 * D_MODEL), D_MODEL ** -0.5),
        'w_pa': nrm((L, D_A, D_MODEL), D_A ** -0.5),
        'w_pb': nrm((L, D_B, D_MODEL), D_B ** -0.5),
        'w_o': nrm((L, D_MODEL, D_MODEL), D_MODEL ** -0.5),
        'norm2': 1.0 + nrm((L, D_MODEL), 0.02),
        'w_up': nrm((L, D_MODEL, D_FF), D_MODEL ** -0.5),
        'w_down': nrm((L, D_FF, D_MODEL), D_FF ** -0.5),
        'norm_f': 1.0 + nrm((D_MODEL,), 0.02),
    }


def reference(x_prompt, x_sample, state_tshift, state_wkv, norm1, w_in, ln_v_g, ln_v_b,
              w_s, b_s, mu_shift, w0, w2, a0, a2, g2, k_k, k_a, r_k, gn_g, gn_b,
              w_gate, w_pa, w_pb, w_o, norm2, w_up, w_down, norm_f):
    xp, xs = x_prompt, x_sample
    bp = x_prompt.shape[0]
    zero_shift = jnp.zeros((bp, D_SHIFT), x_prompt.dtype)
    zero_wkv = jnp.zeros((bp, H_B, HEAD_B, HEAD_B), state_wkv.dtype)
    tsh_p, wkv_p, tsh_s, wkv_s, vrow_s = [], [], [], [], []
    for l in range(DEPTH):
        prm = {
            'norm1': norm1[l], 'w_in': w_in[l], 'ln_v_g': ln_v_g[l], 'ln_v_b': ln_v_b[l],
            'w_s': w_s[l], 'b_s': b_s[l], 'mu_shift': mu_shift[l], 'w0': w0[l], 'w2': w2[l],
            'a0': a0[l], 'a2': a2[l], 'g2': g2[l], 'k_k': k_k[l], 'k_a': k_a[l], 'r_k': r_k[l],
            'gn_g': gn_g[l], 'gn_b': gn_b[l], 'w_gate': w_gate[l], 'w_pa': w_pa[l],
            'w_pb': w_pb[l], 'w_o': w_o[l], 'norm2': norm2[l], 'w_up': w_up[l],
            'w_down': w_down[l],
        }
        xp, _, shp, wkp = trunk_layer(xp, zero_shift, zero_wkv, prm)
        xs, vrows, shs, wks = trunk_layer(xs, state_tshift[l], state_wkv[l], prm)
        tsh_p.append(shp)
        wkv_p.append(wkp)
        tsh_s.append(shs)
        wkv_s.append(wks)
        vrow_s.append(vrows)
    y_prompt = rmsnorm(xp, norm_f)
    y_sample = rmsnorm(xs, norm_f)
    return (y_prompt, y_sample, jnp.stack(tsh_p), jnp.stack(wkv_p), jnp.stack(tsh_s),
            jnp.stack(wkv_s), jnp.stack(vrow_s))
```

```python
import contextlib
import types
import numpy as np
import concourse.bass as bass
import concourse.mybir as mybir
from concourse.bass_utils import run_bass_kernel_spmd

F32 = mybir.dt.float32
BF16 = mybir.dt.bfloat16
ALU = mybir.AluOpType
AF = mybir.ActivationFunctionType

D = 2048
DA = 1024
DB = 1024
DIN = 5408
DFF = 8192
NL = 2
PC0 = 2048
DSH = 3360
NPCH = 28
C0 = 0.6065306597126334
RMS_EPS = 1e-5
LN_EPS = 1e-5
GN_EPS = 64e-5
ENGS = ("pe", "act", "dve", "pool", "sp")


class Res:
    __slots__ = ("name", "w", "r", "rd")

    def __init__(self, name=""):
        self.name = name
        self.w = None
        self.r = {}
        self.rd = []


class Op:
    __slots__ = ("eng", "fn", "deps", "dma", "sig", "sigval", "dsem", "dval", "idx")


def _freeze(fn):
    if fn is None or fn.__closure__ is None:
        return fn
    cells = []
    for c in fn.__closure__:
        try:
            cells.append(types.CellType(c.cell_contents))
        except ValueError:
            cells.append(c)
    return types.FunctionType(fn.__code__, fn.__globals__, fn.__name__, fn.__defaults__, tuple(cells))


class Sched:
    def __init__(self, nc, n_dma_sems=20):
        self.nc = nc
        self.ops = []
        self.n_dma_sems = n_dma_sems
        self.dma_count = {e: 0 for e in ENGS}
        self.last = {}
        self.last_dma = {}
        self.cap = None

    def capture(self, gen):
        old = self.cap
        self.cap = []
        gen()
        out = self.cap
        self.cap = old
        return out

    def replay(self, *streams):
        streams = [st_ for st_ in streams if st_]
        pos = [0] * len(streams)
        tot = sum(len(st_) for st_ in streams)
        for _ in range(tot):
            best, bi = None, 0
            for i, st_ in enumerate(streams):
                if pos[i] < len(st_):
                    frac = pos[i] / len(st_)
                    if best is None or frac < best:
                        best, bi = frac, i
            a = streams[bi][pos[bi]]
            pos[bi] += 1
            self.op(*a[:2], reads=a[2], writes=a[3], dma=a[4], extra_deps=a[5], frozen=True)

    def op(self, eng, fn, reads=(), writes=(), dma=False, extra_deps=(), frozen=False):
        if not frozen:
            fn = _freeze(fn)
        if self.cap is not None:
            self.cap.append((eng, fn, list(reads), list(writes), dma, tuple(extra_deps)))
            return None
        o = Op()
        o.eng, o.fn, o.dma = eng, fn, dma
        o.idx = len(self.ops)
        o.sig = False
        o.sigval = 0
        deps = set(extra_deps)
        for r in reads:
            if r.w is not None:
                deps.add(r.w)
        for r in writes:
            if r.w is not None:
                deps.add(r.w)
            deps.update(r.r.values())
            deps.update(r.rd)
        ops = self.ops
        if eng == "pe" and not dma:
            deps = {d for d in deps if not (ops[d].eng == "pe" and not ops[d].dma)}
        o.deps = deps
        for r in reads:
            if dma:
                r.rd.append(o.idx)
            else:
                r.r[eng] = o.idx
        for r in writes:
            r.w = o.idx
            r.r = {}
            r.rd = []
        if dma:
            n = self.dma_count[eng]
            self.dma_count[eng] = n + 1
            o.dsem = (eng, n % self.n_dma_sems)
            o.dval = 16 * (n // self.n_dma_sems + 1)
            if o.dsem in self.last_dma:
                o.deps.add(self.last_dma[o.dsem])
            self.last_dma[o.dsem] = o.idx
        else:
            self.last[eng] = o.idx
        self.ops.append(o)
        return o

    def barrier(self, engines=("pe", "act", "dve", "pool")):
        deps = set(self.last.values()) | set(self.last_dma.values())
        for e in engines:
            self.op(e, None, extra_deps=deps)

    def emit(self, final_wait_eng="sp"):
        nc = self.nc
        ops = self.ops
        for o in ops:
            for d in o.deps:
                if not ops[d].dma:
                    ops[d].sig = True
        cnt = {e: 0 for e in ENGS}
        for o in ops:
            if o.sig and not o.dma:
                if o.fn is None:
                    o.sig = False
                    continue
                cnt[o.eng] += 1
                o.sigval = cnt[o.eng]
        last_dma = {}
        for o in ops:
            if o.dma:
                last_dma[o.dsem] = max(last_dma.get(o.dsem, 0), o.dval)
        with contextlib.ExitStack() as st:
            esem = {e: st.enter_context(nc.semaphore("s_" + e)) for e in ENGS if e != "sp"}
            dsem = {}
            for e in ENGS:
                for i in range(min(self.n_dma_sems, self.dma_count[e])):
                    dsem[(e, i)] = st.enter_context(nc.semaphore("d_%s%d" % (e, i)))
            block = st.enter_context(nc.Block())
            per_eng = {e: [o for o in ops if o.eng == e] for e in ENGS}

            def run(e, engobj):
                frontier = {}
                for o in per_eng[e]:
                    need = {}
                    for d in o.deps:
                        p = ops[d]
                        if p.dma:
                            key, val = ("d",) + p.dsem, p.dval
                        else:
                            if p.fn is None:
                                continue
                            key, val = ("e", p.eng), p.sigval
                        if val > need.get(key, 0):
                            need[key] = val
                    for key, val in need.items():
                        if frontier.get(key, 0) >= val:
                            continue
                        frontier[key] = val
                        s = dsem[key[1:]] if key[0] == "d" else esem[key[1]]
                        engobj.wait_ge(s, val)
                    if o.fn is None:
                        continue
                    ins = o.fn(engobj)
                    if o.dma:
                        ins.then_inc(dsem[o.dsem], 16)
                    elif o.sig:
                        ins.then_inc(esem[o.eng], 1)
                if e == final_wait_eng:
                    for key, val in last_dma.items():
                        engobj.wait_ge(dsem[key], val)

            @block.tensor
            def _(eng):
                run("pe", eng)

            @block.scalar
            def _(eng):
                run("act", eng)

            @block.vector
            def _(eng):
                run("dve", eng)

            @block.gpsimd
            def _(eng):
                run("pool", eng)

            @block.sync
            def _(eng):
                run("sp", eng)


class Buf:
    def __init__(self, t, name=""):
        self.t = t
        self.r = Res(name)


def _consts():
    p = np.arange(128)[:, None]
    c = np.arange(128)[None, :]
    same = (p // 64) == (c // 64)
    ident = (p == c).astype(np.float32)
    J = (c == (p + 64) % 128).astype(np.float32)
    su = (same & ((p % 64) < (c % 64))).astype(np.float32)
    ui = (same & ((p % 64) <= (c % 64))).astype(np.float32)
    sl = (same & ((p % 64) > (c % 64))).astype(np.float32)
    mgram = np.concatenate([su, ui, su, ui], axis=1)
    bones = same.astype(np.float32)
    triu = (p <= c).astype(np.float32)
    cm = np.ones((128, 512), np.float32)
    cm[:, ::64] = 0.0
    ones = np.ones((128, 128), np.float32)
    parts = [("ident", ident), ("J", J), ("mgram", mgram), ("msl", sl), ("bones", bones),
             ("triu", triu), ("cmask", cm), ("ones", ones)]
    off = {}
    o = 0
    for k, v in parts:
        off[k] = (o, v.shape[1])
        o += v.shape[1]
    return np.concatenate([v for _, v in parts], axis=1), off


CST, CST_OFF = _consts()

PV = {}
_o = 0
for _l in range(NL):
    for _nm, _n in (("norm1", 16), ("norm2", 16), ("mu", NPCH), ("w0", 8), ("a0", 8), ("k_k", 8),
                    ("k_a", 8), ("r_k", 8), ("gn_g", 8), ("gn_b", 8)):
        PV[(_nm, _l)] = _o
        _o += _n
PV[("norm_f", 0)] = _o
_o += 16
NPV = _o

LORA = [(24, 3072, 64), (25, 3136, 64), (26, 3200, 128), (27, 3328, 32)]


def _blocks():
    bl = []
    bl.append(("A0", 16, 288, [("w_in", 0, PC0 + 3072, 288, 0)]))
    for hp in range(8):
        bl.append(("Ahp%d" % hp, 16, 384, [("w_in", 0, PC0 + j * 1024 + hp * 128, 128, j * 128) for j in range(3)]))
    for j in range(2):
        bl.append(("Cv%d" % j, 16, 512, [("w_in", 0, 1024 + 512 * j, 512, 0)]))
    for j in range(2):
        bl.append(("Cu%d" % j, 16, 512, [("w_in", 0, 512 * j, 512, 0)]))
    for mg in range(8):
        bl.append(("G%d" % mg, 16, 512, [("w_gate", 0, 256 * mg, 256, 0), ("w_gate", 0, 2048 + 256 * mg, 256, 256)]))
        bl.append(("P%d" % mg, 8, 512, [("w_pa", 0, 256 * mg, 256, 0), ("w_pb", 0, 256 * mg, 256, 256)]))
    for j in range(4):
        bl.append(("O%d" % j, 16, 512, [("w_o", 0, 512 * j, 512, 0)]))
    for q in range(4):
        for j in range(4):
            bl.append(("U%d_%d" % (q, j), 16, 512, [("w_up", 0, 2048 * q + 512 * j, 512, 0)]))
        for j in range(4):
            bl.append(("D%d_%d" % (q, j), 16, 512, [("w_down", 2048 * q, 512 * j, 512, 0)]))
    return bl


BLOCKS = _blocks()
WSHAPES = {"w_in": (D, DIN), "w_gate": (D, 2 * D), "w_pa": (DA, D), "w_pb": (DB, D), "w_o": (D, D),
           "w_up": (D, DFF), "w_down": (DFF, D)}


class Tile:
    def __init__(self, c0, w, nseq, kind, first, last, idx):
        self.c0, self.w, self.nseq, self.kind = c0, w, nseq, kind
        self.Lq = w // nseq
        self.first, self.last, self.idx = first, last, idx
        self.nch = w // 64


def build_program(NPT=4, NS=4, GS=4, dbg=False, n_layers=NL):
    NP = NPT * 512
    NSW = NS * 64
    NTOK = NP + NSW
    tiles = [Tile(512 * i, 512, 1, "P", i == 0, i == NPT - 1, i) for i in range(NPT)]
    if NS:
        tiles.append(Tile(NP, NSW, NS, "S", False, False, NPT))
    NT = len(tiles)
    nc = bass.Bass("TRN2", target_bir_lowering=False)

    def din(name, shape, dt=F32):
        return nc.dram_tensor(name, list(shape), dt, kind="ExternalInput").ap()

    def dout(name, shape, dt=F32):
        return nc.dram_tensor(name, list(shape), dt, kind="ExternalOutput").ap()

    def dscr(name, shape, dt=F32):
        return nc.dram_tensor(name, list(shape), dt, kind="Internal").ap()

    xT_in = din("xT", [D, NTOK])
    xprev_in = din("xprev", [128, 16])
    pvec_in = din("pvec", [128, NPV])
    cst_in = din("cst", [128, CST.shape[1]])
    lng_in = din("ln_v_g", [NL, DA])
    lnb_in = din("ln_v_b", [NL, DA])
    bs_in = din("b_s", [NL, 1, 1024])
    wsT_in = din("w_sT", [NL, 128, 8, 128])
    w2_in = din("w2", [NL, 64, DB])
    a2_in = din("a2", [NL, 64, DB])
    g2_in = din("g2", [NL, 160, DB])
    tsh_in = din("tsh_s", [NL, 128, NPCH, max(NS, 1)])
    wkv_in = din("wkv_s", [NL, max(NS, 1), 8, 128, 128])
    W_in = {k: din(k, [NL] + list(v)) for k, v in WSHAPES.items()}

    yT_out = dout("yT", [D, NTOK])
    tsh_out = dout("tsh_o", [NL, 128, NPCH, 1 + max(NS, 1)])
    wkv_out = dout("wkv_o", [NL, 1 + max(NS, 1), 8, 128, 128])
    vn_out = dout("vn_o", [NL, max(NSW, 128) // 128, 128, DA])
    dbg_out = dout("dbg", [32, 128, 512]) if dbg else None

    x1_scr = dscr("x1_scr", [D, NTOK])
    o_sp = dscr("o_sp", [NT, 8, 128, 512])
    od_sp = dscr("od_sp", [NT, 8, 128, 1024], BF16)
    bg_sp = dscr("bg_sp", [NT, 8, 128, 512])
    g_sp = dscr("g_sp", [NT, 8, 128, 512])
    wscr = {}
    for l in range(n_layers):
        for (nm, kc, ncol, pieces) in BLOCKS:
            wscr[(l, nm)] = dscr("ws_%d_%s" % (l, nm), [128, kc, ncol], BF16)

    st = contextlib.ExitStack()
    with st:
        S = Sched(nc)
        BLK = {b[0]: b for b in BLOCKS}

        ARENA_BYTES = 130 * 1024
        arena = st.enter_context(nc.sbuf_tensor("arena", [128, ARENA_BYTES // 4], F32))
        aoff = {"A": 0, "C": 0}

        def sb(name, shape, dt=F32, reg="S"):
            shape = list(shape)
            if reg == "S":
                return Buf(st.enter_context(nc.sbuf_tensor("sb_" + name, shape, dt)), name)
            esz = 4 if dt == F32 else 2
            n = 1
            for d_ in shape[1:]:
                n *= d_
            nbytes = (n * esz + 31) // 32 * 32
            o = aoff[reg]
            aoff[reg] = o + nbytes
            assert aoff[reg] <= ARENA_BYTES, (name, reg, aoff[reg])
            ap = arena[0:shape[0], o // 4:(o + nbytes) // 4]
            if dt != F32:
                ap = ap.bitcast(dt)
            ap = ap[:, 0:n]
            if len(shape) == 3:
                ap = ap.rearrange("p (a b) -> p a b", a=shape[1])
            b = Buf(ap, name)
            b.rows = shape[0]
            return b

        def ps(name):
            return Buf(st.enter_context(nc.psum_tensor(name, [128, 512], F32)), name)

        PSB = [ps("ps%d" % i) for i in range(8)]

        pvec = sb("pvec", [128, NPV])
        omk = sb("omk", [128, NL * 8])
        cstb = sb("cstb", [128, CST.shape[1]], BF16)
        cst = cstb

        def cb(name):
            o, n = CST_OFF[name]
            return cstb.t[:, o:o + n]

        cf = cb
        S.op("act", lambda e: e.dma_start(out=pvec.t[:], in_=pvec_in), writes=[pvec.r], dma=True)
        S.op("pool", lambda e: e.dma_start(out=cstb.t[:], in_=cst_in), writes=[cstb.r], dma=True)
        for l in range(NL):
            ka = PV[("k_a", l)]
            S.op("dve", lambda e, l=l, ka=ka: e.tensor_scalar(out=omk.t[:, l * 8:(l + 1) * 8], in0=pvec.t[:, ka:ka + 8],
                                                              scalar1=-1.0, scalar2=1.0, op0=ALU.mult, op1=ALU.add),
                 reads=[pvec.r], writes=[omk.r])

        def pv(name, l, c=0, rows=128):
            o = PV[(name, l)] + c
            return pvec.t[0:rows, o:o + 1]

        wres = {}
        conv_order = [(l, b[0]) for l in range(n_layers) for b in BLOCKS]
        conv_pos = [0]

        def conv_next():
            if conv_pos[0] >= len(conv_order):
                return
            l, nm = conv_order[conv_pos[0]]
            conv_pos[0] += 1
            _, kc, ncol, pieces = BLK[nm]
            rl = []
            for (wn, r0, c0, n, d0) in pieces:
                src = W_in[wn][l, r0:r0 + kc * 128, c0:c0 + n].rearrange("(k p) c -> p k c", p=128)
                dst = wscr[(l, nm)][:, :, d0:d0 + n]
                r = Res("cv")
                S.op("pool", lambda e, s=src, d=dst: e.dma_start(out=d, in_=s), writes=[r], dma=True)
                rl.append(r)
            wres[(l, nm)] = rl

        NSLOT = 3
        slots = [sb("wslot%d" % i, [128, 16 * 512], BF16) for i in range(NSLOT)]
        wctr = [0]

        def wget(l, nm):
            _, kc, ncol, _ = BLK[nm]
            while (l, nm) not in wres:
                conv_next()
            conv_next()
            sl = slots[wctr[0] % NSLOT]
            wctr[0] += 1
            view = sl.t[:, 0:kc * ncol].rearrange("p (k c) -> p k c", k=kc)
            src = wscr[(l, nm)]
            S.op("sp", lambda e, v=view, s=src: e.dma_start(out=v, in_=s), reads=wres[(l, nm)], writes=[sl.r], dma=True)
            return view, sl.r

        xT = sb("xT", [128, 16, 512], reg="C")
        xTr = [Res("xT%d" % c) for c in range(16)]
        xstg = [sb("xstg%d" % i, [128, 2, 512], reg="A") for i in range(2)]
        hT = sb("hT", [128, 16, 512], BF16)
        hprev = sb("hprev", [128, 16], BF16)
        xprev = sb("xprev", [128, 40])
        sq = [sb("sq%d" % i, [128, 512], BF16) for i in range(2)]
        rstd = sb("rstd", [128, 512])
        gctr = [0]
        gbanks = [list(range(0, 3)), list(range(0, 8))]
        phase = [0]

        def gbank():
            bl = gbanks[phase[0]]
            b = PSB[bl[gctr[0] % len(bl)]]
            gctr[0] += 1
            return b

        ectr = [0]

        def evac_eng():
            ectr[0] += 1
            return "act" if ectr[0] % 2 else "dve"

        def copy_op(eng, out, in_, reads, writes):
            if eng == "act":
                S.op("act", lambda e: e.activation(out=out, in_=in_, func=AF.Copy), reads=reads, writes=writes)
            else:
                S.op(eng, lambda e: e.tensor_copy(out=out, in_=in_), reads=reads, writes=writes)

        dbg_i = [0]
        dbgbufs = []

        def dump(buf, ap, rows=128, cols=512):
            if not dbg:
                return
            i = dbg_i[0]
            dbg_i[0] += 1
            if not dbgbufs:
                dbgbufs.extend([sb("dbgt%d" % j, [128, 512]) for j in range(2)])
            tmp = dbgbufs[i % 2]
            S.op("dve", lambda e: e.memset(tmp.t[:], 0.0), writes=[tmp.r])
            S.op("dve", lambda e: e.tensor_copy(out=tmp.t[0:rows, 0:cols], in_=ap), reads=[buf.r], writes=[tmp.r])
            S.op("act", lambda e: e.dma_start(out=dbg_out[i], in_=tmp.t[:]), reads=[tmp.r], dma=True)

        def load_x(l, T):
            src = (xT_in if l == 0 else x1_scr).rearrange("(c p) t -> p c t", p=128)
            for c in range(16):
                rd = [] if l == 0 else [x1res[(T.idx, c)]]
                S.op("act", lambda e, c=c: e.dma_start(out=xT.t[:, c, 0:T.w], in_=src[:, c, T.c0:T.c0 + T.w]), reads=rd, writes=[xTr[c]], dma=True)

        def rmsnorm(xres, xap_fn, w, gname, l, out_fn, ores):
            pb = gbank()
            for c in range(16):
                s = sq[c % 2]
                S.op("act", lambda e, c=c, s=s: e.activation(out=s.t[:, 0:w], in_=xap_fn(c), func=AF.Square),
                     reads=[xres[c]], writes=[s.r])
                S.op("pe", lambda e, c=c, s=s: e.matmul(pb.t[:, 0:w], cb("ones"), s.t[:, 0:w], start=(c == 0), stop=(c == 15)),
                     reads=[s.r, cstb.r], writes=[pb.r])
            S.op("dve", lambda e: e.tensor_scalar(out=rstd.t[:, 0:w], in0=pb.t[:, 0:w], scalar1=1.0 / D, scalar2=RMS_EPS,
                                                  op0=ALU.mult, op1=ALU.add), reads=[pb.r], writes=[rstd.r])
            S.op("act", lambda e: e.activation(out=rstd.t[:, 0:w], in_=rstd.t[:, 0:w], func=AF.Sqrt), reads=[rstd.r], writes=[rstd.r])
            S.op("dve", lambda e: e.reciprocal(out=rstd.t[:, 0:w], in_=rstd.t[:, 0:w]), reads=[rstd.r], writes=[rstd.r])
            for c in range(16):
                S.op("dve", lambda e, c=c: e.scalar_tensor_tensor(out=out_fn(c), in0=xap_fn(c), scalar=pv(gname, l, c),
                                                                  in1=rstd.t[:, 0:w], op0=ALU.mult, op1=ALU.mult),
                     reads=[xres[c], rstd.r, pvec.r], writes=[ores[c]])

        x1res = {}
        hTr16 = None

        A = {}
        for nm in ("sw", "aa", "gg", "cs", "cse", "e1", "e2", "e3", "bg", "osp"):
            A[nm] = sb("A_" + nm, [128, 512], reg="A")
        A["tmp"] = A["cse"]
        A["kk"] = A["sw"]
        A["t1"] = A["cs"]
        A["kkn"] = A["e1"]
        A["kh"] = A["e2"]
        A["beta"] = A["e3"]
        A["sqk"] = sb("A_sqk", [128, 512], BF16, reg="A")
        A["rkk"] = sb("A_rkk", [128, 512], BF16, reg="A")
        RKV = [[sb("A_rkv%d_%d" % (j, i), [128, 512], reg="A") for i in range(2)] for j in range(3)]
        pbufs = [sb("pbuf%d" % i, [128, 520], reg="A") for i in range(2)]
        dtmp = [sb("dtmp%d" % i, [128, 512], reg="A") for i in range(2)]
        ltmp = sb("ltmp", [128, 512], reg="A")
        txw = sb("txw", [64, 512], BF16, reg="A")
        xab = sb("xab", [64, 512], BF16, reg="A")
        sxg0 = sb("sxg0", [128, 512], BF16, reg="A")
        sxg1 = sb("sxg1", [32, 512], BF16, reg="A")
        plast = sb("plast", [128, NPCH, 1 + max(NS, 1)], reg="A")
        tshs = sb("tshs", [128, NPCH, max(NS, 1)], reg="A")
        w2b = sb("w2b", [64, DB], BF16, reg="A")
        a2b = sb("a2b", [64, DB], BF16, reg="A")
        g2b0 = sb("g2b0", [128, DB], BF16, reg="A")
        g2b1 = sb("g2b1", [32, DB], BF16, reg="A")
        eBD = [sb("eBD%d" % i, [128, 1024], BF16, reg="A") for i in range(3)]
        BDS = []
        for i in range(2):
            BDS.append(dict(KR=sb("KR%d" % i, [128, 8, 256], BF16, reg="A"), KkD=sb("KkD%d" % i, [128, 8, 128], BF16, reg="A"),
                            BtD=sb("BtD%d" % i, [128, 8, 128], BF16, reg="A"), KhD=sb("KhD%d" % i, [128, 8, 128], BF16, reg="A"),
                            BhD=sb("BhD%d" % i, [128, 8, 128], BF16, reg="A"), VD=sb("VD%d" % i, [128, 8, 128], BF16, reg="A"),
                            WC=sb("WC%d" % i, [128, 8], reg="A")))
        GC = 4
        Gm = sb("Gm", [128, GC, 512], BF16, reg="A")
        NTb = sb("NTb", [128, GC, 128], BF16, reg="A")
        ABb = [sb("AB%d" % i, [128, GC, 256], BF16, reg="A") for i in range(2)]
        Pb = [sb("Pb%d" % i, [128, GC, 128], BF16, reg="A") for i in range(2)]
        VDT = sb("VDT", [128, GC, 128], BF16, reg="A")
        KhT = sb("KhT", [128, GC, 128], BF16, reg="A")
        BhT = sb("BhT", [128, GC, 128], BF16, reg="A")
        ZT = sb("ZT", [128, 128], BF16, reg="A")
        nUT = sb("nUT", [128, 128], BF16, reg="A")
        S0bf = [sb("S0bf%d" % i, [128, 128], BF16, reg="A") for i in range(2)]
        ST = [sb("ST%d" % hp, [128, 128], reg="A") for hp in range(8)]
        stg = [sb("stg%d" % i, [128, 128], reg="A") for i in range(2)]
        odb = sb("odb", [128, 8, 128], BF16, reg="A")
        s0ctr = [0]
        stgctr = [0]

        def layer_params(l):
            S.op("pool", lambda e: e.dma_start(out=w2b.t[:], in_=w2_in[l]), writes=[w2b.r], dma=True)
            S.op("pool", lambda e: e.dma_start(out=a2b.t[:], in_=a2_in[l]), writes=[a2b.r], dma=True)
            S.op("pool", lambda e: e.dma_start(out=g2b0.t[:], in_=g2_in[l, 0:128, :]), writes=[g2b0.r], dma=True)
            S.op("pool", lambda e: e.dma_start(out=g2b1.t[:], in_=g2_in[l, 128:160, :]), writes=[g2b1.r], dma=True)
            if NS:
                S.op("act", lambda e: e.dma_start(out=tshs.t[:], in_=tsh_in[l]), writes=[tshs.r], dma=True)

        pctr = [0]

        def p_chunk(l, T, blk, blkres, off, rows, cid, out_fn, out_res, post=None):
            w, Lq, nseq = T.w, T.Lq, T.nseq
            pb = gbank()
            for kc in range(16):
                S.op("pe", lambda e, kc=kc: e.matmul(pb.t[0:rows, 0:w], blk[:, kc, off:off + rows], hT.t[:, kc, 0:w],
                                                     start=(kc == 0), stop=(kc == 15)),
                     reads=[blkres, hT.r], writes=[pb.r])
            P = pbufs[pctr[0] % 2]
            dt_ = dtmp[pctr[0] % 2]
            pctr[0] += 1
            pv3 = P.t[0:rows, 0:nseq * (Lq + 1)].rearrange("p (s l) -> p s l", s=nseq)
            S.op("act", lambda e: e.activation(out=pv3[:, :, 1:Lq + 1], in_=pb.t[0:rows, 0:w].rearrange("p (s l) -> p s l", s=nseq),
                                               func=AF.Copy), reads=[pb.r], writes=[P.r])
            if T.kind == "P":
                if T.first:
                    ph = PSB[3]
                    for kc in range(16):
                        S.op("pe", lambda e, kc=kc: e.matmul(ph.t[0:rows, cid:cid + 1], blk[:, kc, off:off + rows], hprev.t[:, kc:kc + 1],
                                                             start=(kc == 0), stop=(kc == 15)),
                             reads=[blkres, hprev.r], writes=[ph.r])
                    S.op("act", lambda e: e.activation(out=pv3[:, 0, 0:1], in_=ph.t[0:rows, cid:cid + 1], func=AF.Copy),
                         reads=[ph.r], writes=[P.r])
                else:
                    S.op("act", lambda e: e.activation(out=pv3[:, 0, 0:1], in_=plast.t[0:rows, cid, 0:1], func=AF.Copy),
                         reads=[plast.r], writes=[P.r])
                S.op("act", lambda e: e.activation(out=plast.t[0:rows, cid, 0:1], in_=pv3[:, 0, Lq:Lq + 1], func=AF.Copy),
                     reads=[P.r], writes=[plast.r])
            else:
                S.op("act", lambda e: e.activation(out=pv3[:, :, 0], in_=tshs.t[0:rows, cid, 0:nseq], func=AF.Copy),
                     reads=[tshs.r], writes=[P.r])
                S.op("act", lambda e: e.activation(out=plast.t[0:rows, cid, 1:1 + nseq], in_=pv3[:, :, Lq], func=AF.Copy),
                     reads=[P.r], writes=[plast.r])
            d3 = dt_.t[0:rows, 0:w].rearrange("p (s l) -> p s l", s=nseq)
            S.op("dve", lambda e: e.tensor_tensor(out=d3, in0=pv3[:, :, 0:Lq], in1=pv3[:, :, 1:Lq + 1], op=ALU.subtract),
                 reads=[P.r], writes=[dt_.r])
            if post is None:
                o3 = out_fn().rearrange("p (s l) -> p s l", s=nseq)
                S.op("dve", lambda e: e.scalar_tensor_tensor(out=o3, in0=d3, scalar=pv("mu", l, cid, rows), in1=pv3[:, :, 1:Lq + 1],
                                                             op0=ALU.mult, op1=ALU.add),
                     reads=[dt_.r, P.r, pvec.r], writes=[out_res])
            else:
                l3 = ltmp.t[0:rows, 0:w].rearrange("p (s l) -> p s l", s=nseq)
                S.op("dve", lambda e: e.scalar_tensor_tensor(out=l3, in0=d3, scalar=pv("mu", l, cid, rows), in1=pv3[:, :, 1:Lq + 1],
                                                             op0=ALU.mult, op1=ALU.add),
                     reads=[dt_.r, P.r, pvec.r], writes=[ltmp.r])
                post(ltmp.t[0:rows, 0:w])

        def bd4(t, w):
            nch = w // 64
            return t[:, 0:w].rearrange("p (c j) -> p c j", j=64).unsqueeze(2).broadcast_to([128, nch, 2, 64])

        def prep(l, T, hp, rb, kb, vb):
            w, nch = T.w, T.nch
            bs_ = BDS[hp % 2]
            KR, KkD, BtD, KhD, BhD, VD, WC = (bs_[k_] for k_ in ("KR", "KkD", "BtD", "KhD", "BhD", "VD", "WC"))
            hc = slice(hp * 128, (hp + 1) * 128)
            a = A
            pb = gbank()
            S.op("pe", lambda e: e.matmul(pb.t[:, 0:w], w2b.t[0:64, hc], txw.t[0:64, 0:w], start=True, stop=True),
                 reads=[w2b.r, txw.r], writes=[pb.r])
            S.op("act", lambda e: e.activation(out=a["sw"].t[:, 0:w], in_=pb.t[:, 0:w], func=AF.Sigmoid, bias=pv("w0", l, hp)),
                 reads=[pb.r, pvec.r], writes=[a["sw"].r])
            pb2 = gbank()
            S.op("pe", lambda e: e.matmul(pb2.t[:, 0:w], a2b.t[0:64, hc], xab.t[0:64, 0:w], start=True, stop=True),
                 reads=[a2b.r, xab.r], writes=[pb2.r])
            S.op("act", lambda e: e.activation(out=a["aa"].t[:, 0:w], in_=pb2.t[:, 0:w], func=AF.Sigmoid, bias=pv("a0", l, hp)),
                 reads=[pb2.r, pvec.r], writes=[a["aa"].r])
            pb3 = gbank()
            S.op("pe", lambda e: e.matmul(pb3.t[:, 0:w], g2b0.t[:, hc], sxg0.t[:, 0:w], start=True, stop=False),
                 reads=[g2b0.r, sxg0.r], writes=[pb3.r])
            S.op("pe", lambda e: e.matmul(pb3.t[:, 0:w], g2b1.t[0:32, hc], sxg1.t[0:32, 0:w], start=False, stop=True),
                 reads=[g2b1.r, sxg1.r], writes=[pb3.r])
            S.op("act", lambda e: e.activation(out=a["gg"].t[:, 0:w], in_=pb3.t[:, 0:w], func=AF.Copy),
                 reads=[pb3.r], writes=[a["gg"].r])
            S.op("dve", lambda e: e.tensor_tensor_scan(out=a["cs"].t[:, 0:w], data0=cf("cmask")[:, 0:w], data1=a["sw"].t[:, 0:w],
                                                       initial=0.0, op0=ALU.mult, op1=ALU.add),
                 reads=[a["sw"].r, cst.r], writes=[a["cs"].r])
            S.op("dve", lambda e: e.tensor_tensor(out=a["cse"].t[:, 0:w], in0=a["cs"].t[:, 0:w], in1=a["sw"].t[:, 0:w], op=ALU.subtract),
                 reads=[a["cs"].r, a["sw"].r], writes=[a["cse"].r])
            S.op("act", lambda e: e.activation(out=a["e1"].t[:, 0:w], in_=a["cs"].t[:, 0:w], func=AF.Exp, scale=-C0),
                 reads=[a["cs"].r], writes=[a["e1"].r])
            S.op("act", lambda e: e.activation(out=a["e2"].t[:, 0:w], in_=a["cs"].t[:, 0:w], func=AF.Exp, scale=C0),
                 reads=[a["cs"].r], writes=[a["e2"].r])
            S.op("act", lambda e: e.activation(out=a["e3"].t[:, 0:w], in_=a["cse"].t[:, 0:w], func=AF.Exp, scale=-C0),
                 reads=[a["cse"].r], writes=[a["e3"].r])
            S.op("dve", lambda e: e.tensor_copy(out=WC.t[:, 0:nch], in_=a["e1"].t[:, 0:w].rearrange("p (c j) -> p c j", j=64)[:, :, 63]),
                 reads=[a["e1"].r], writes=[WC.r])
            mbd = cf("bones").rearrange("p (h j) -> p h j", h=2).unsqueeze(1).broadcast_to([128, nch, 2, 64])

            def v4(buf, n=1024):
                return buf.t[:, 0:nch * 128].rearrange("p (c h j) -> p c h j", h=2, j=64)

            for i, src in enumerate(("e1", "e2", "e3")):
                eng = "pool" if i == 1 else "dve"
                S.op(eng, lambda e, i=i, src=src: e.tensor_tensor(out=v4(eBD[i]), in0=bd4(a[src].t, w), in1=mbd, op=ALU.mult),
                     reads=[a[src].r, cst.r], writes=[eBD[i].r])
            S.op("dve", lambda e: e.tensor_scalar(out=a["kk"].t[:, 0:w], in0=kb.t[:, 0:w], scalar1=pv("k_k", l, hp), scalar2=None, op0=ALU.mult),
                 reads=[kb.r, pvec.r], writes=[a["kk"].r])
            S.op("act", lambda e: e.activation(out=a["sqk"].t[:, 0:w], in_=a["kk"].t[:, 0:w], func=AF.Square),
                 reads=[a["kk"].r], writes=[a["sqk"].r])
            pb4 = gbank()
            S.op("pe", lambda e: e.matmul(pb4.t[:, 0:w], cb("bones"), a["sqk"].t[:, 0:w], start=True, stop=True),
                 reads=[cstb.r, a["sqk"].r], writes=[pb4.r])
            S.op("dve", lambda e: e.tensor_scalar(out=a["t1"].t[:, 0:w], in0=pb4.t[:, 0:w], scalar1=1e-24, scalar2=None, op0=ALU.max),
                 reads=[pb4.r], writes=[a["t1"].r])
            S.op("act", lambda e: e.activation(out=a["t1"].t[:, 0:w], in_=a["t1"].t[:, 0:w], func=AF.Sqrt), reads=[a["t1"].r], writes=[a["t1"].r])
            S.op("dve", lambda e: e.reciprocal(out=a["t1"].t[:, 0:w], in_=a["t1"].t[:, 0:w]), reads=[a["t1"].r], writes=[a["t1"].r])
            S.op("dve", lambda e: e.tensor_tensor(out=a["kkn"].t[:, 0:w], in0=a["kk"].t[:, 0:w], in1=a["t1"].t[:, 0:w], op=ALU.mult),
                 reads=[a["kk"].r, a["t1"].r], writes=[a["kkn"].r])
            S.op("dve", lambda e: e.tensor_scalar(out=a["tmp"].t[:, 0:w], in0=a["aa"].t[:, 0:w], scalar1=pv("k_a", l, hp),
                                                  scalar2=omk.t[:, l * 8 + hp:l * 8 + hp + 1], op0=ALU.mult, op1=ALU.add),
                 reads=[a["aa"].r, pvec.r, omk.r], writes=[a["tmp"].r])
            S.op("dve", lambda e: e.tensor_tensor(out=a["kh"].t[:, 0:w], in0=kb.t[:, 0:w], in1=a["tmp"].t[:, 0:w], op=ALU.mult),
                 reads=[kb.r, a["tmp"].r], writes=[a["kh"].r])
            S.op("dve", lambda e: e.tensor_tensor(out=a["beta"].t[:, 0:w], in0=a["kkn"].t[:, 0:w], in1=a["aa"].t[:, 0:w], op=ALU.mult),
                 reads=[a["kkn"].r, a["aa"].r], writes=[a["beta"].r])
            S.op("dve", lambda e: e.scalar_tensor_tensor(out=a["rkk"].t[:, 0:w], in0=rb.t[:, 0:w], scalar=pv("r_k", l, hp), in1=a["kh"].t[:, 0:w],
                                                         op0=ALU.mult, op1=ALU.mult),
                 reads=[rb.r, a["kh"].r, pvec.r], writes=[a["rkk"].r])
            pb5 = gbank()
            S.op("pe", lambda e: e.matmul(pb5.t[:, 0:w], cb("bones"), a["rkk"].t[:, 0:w], start=True, stop=True),
                 reads=[cstb.r, a["rkk"].r], writes=[pb5.r])
            S.op("dve", lambda e: e.tensor_tensor(out=a["bg"].t[:, 0:w], in0=pb5.t[:, 0:w], in1=vb.t[:, 0:w], op=ALU.mult),
                 reads=[pb5.r, vb.r], writes=[a["bg"].r])
            S.op("dve", lambda e: e.tensor_tensor(out=a["bg"].t[:, 0:w], in0=a["bg"].t[:, 0:w], in1=a["gg"].t[:, 0:w], op=ALU.mult),
                 reads=[a["bg"].r, a["gg"].r], writes=[a["bg"].r])
            if l == 0 and T.idx == 0 and hp == 0:
                for nm in ("sw", "aa", "cs", "e1", "e2", "e3", "kkn", "kh", "beta", "gg"):
                    dump(a[nm], a[nm].t[:, 0:w], 128, w)
                dump(rb, rb.t[:, 0:w], 128, w)
                dump(vb, vb.t[:, 0:w], 128, w)
            kr4 = KR.t[:, 0:nch, :].rearrange("p c (s h j) -> p c s h j", s=2, h=2)
            S.op("dve", lambda e: e.tensor_tensor(out=kr4[:, :, 1], in0=bd4(rb.t, w), in1=v4(eBD[0]), op=ALU.mult),
                 reads=[rb.r, eBD[0].r], writes=[KR.r])
            S.op("pool", lambda e: e.tensor_tensor(out=kr4[:, :, 0], in0=bd4(a["kkn"].t, w), in1=v4(eBD[2]), op=ALU.mult),
                 reads=[a["kkn"].r, eBD[2].r, KR.r], writes=[KR.r])

            def t4(buf):
                return buf.t[:, 0:nch, :].rearrange("p c (h j) -> p c h j", h=2)

            S.op("dve", lambda e: e.tensor_tensor(out=t4(KkD), in0=bd4(a["kh"].t, w), in1=v4(eBD[1]), op=ALU.mult),
                 reads=[a["kh"].r, eBD[1].r], writes=[KkD.r])
            S.op("pool", lambda e: e.tensor_tensor(out=t4(BtD), in0=bd4(a["beta"].t, w), in1=v4(eBD[1]), op=ALU.mult),
                 reads=[a["beta"].r, eBD[1].r], writes=[BtD.r])
            wcb = WC.t[:, 0:nch].unsqueeze(2).broadcast_to([128, nch, 128])
            S.op("dve", lambda e: e.tensor_tensor(out=KhD.t[:, 0:nch, :], in0=KkD.t[:, 0:nch, :], in1=wcb, op=ALU.mult),
                 reads=[KkD.r, WC.r], writes=[KhD.r])
            S.op("pool", lambda e: e.tensor_tensor(out=BhD.t[:, 0:nch, :], in0=BtD.t[:, 0:nch, :], in1=wcb, op=ALU.mult),
                 reads=[BtD.r, WC.r], writes=[BhD.r])
            S.op("pool", lambda e: e.tensor_tensor(out=t4(VD), in0=bd4(vb.t, w), in1=mbd, op=ALU.mult),
                 reads=[vb.r, cst.r], writes=[VD.r])

        def scan(l, T, hp):
            w, nch = T.w, T.nch
            bs_ = BDS[hp % 2]
            KR, KkD, BtD, KhD, BhD, VD, WC = (bs_[k_] for k_ in ("KR", "KkD", "BtD", "KhD", "BhD", "VD", "WC"))
            Q = PSB[4:8]
            st_ = ST[hp]
            for g0 in range(0, nch, GC):
                n = min(GC, nch - g0)
                for ci in range(n):
                    c = g0 + ci
                    q = Q[ci % 2]
                    S.op("pe", lambda e, c=c, q=q: e.matmul(q.t[:, 0:256], BtD.t[:, c, :], KR.t[:, c, :], start=True, stop=True),
                         reads=[BtD.r, KR.r], writes=[q.r])
                    S.op("pe", lambda e, c=c, q=q: e.matmul(q.t[:, 256:512], KkD.t[:, c, :], KR.t[:, c, :], start=True, stop=True),
                         reads=[KkD.r, KR.r], writes=[q.r])
                    S.op("dve", lambda e, ci=ci, q=q: e.tensor_tensor(out=Gm.t[:, ci, :], in0=q.t[:, :], in1=cb("mgram"), op=ALU.mult),
                         reads=[q.r, cstb.r], writes=[Gm.r])
                    S.op("pe", lambda e, c=c, ci=ci: e.matmul(Q[2].t[:, ci * 128:(ci + 1) * 128], KR.t[:, c, 0:128], BtD.t[:, c, :], start=True, stop=True),
                         reads=[BtD.r, KR.r], writes=[Q[2].r])
                mslb = cb("msl").unsqueeze(1).broadcast_to([128, n, 128])
                S.op("dve", lambda e: e.tensor_tensor(out=NTb.t[:, 0:n, :], in0=Q[2].t[:, 0:n * 128].rearrange("p (c j) -> p c j", j=128),
                                                      in1=mslb, op=ALU.mult), reads=[Q[2].r, cstb.r], writes=[NTb.r])
                idb = cb("ident").unsqueeze(1).broadcast_to([128, n, 128])
                S.op("pool", lambda e: e.tensor_tensor(out=Pb[0].t[:, 0:n, :], in0=idb, in1=Gm.t[:, 0:n, 0:128], op=ALU.subtract),
                     reads=[Gm.r, cstb.r], writes=[Pb[0].r])
                for k in range(1, 6):
                    ab_o = ABb[k % 2]
                    ab_i = ABb[(k - 1) % 2]
                    p_o, p_i = Pb[k % 2], Pb[(k - 1) % 2]
                    for ci in range(n):
                        q = Q[ci // 2]
                        co = (ci % 2) * 256
                        if k == 1:
                            Ai = Gm.t[:, ci, 0:128]
                            Bi = NTb.t[:, ci, :]
                            rdi = [Gm.r, NTb.r]
                        else:
                            Ai = ab_i.t[:, ci, 0:128]
                            Bi = ab_i.t[:, ci, 128:256]
                            rdi = [ab_i.r]
                        if k < 5:
                            S.op("pe", lambda e, q=q, co=co, Ai=Ai, Bi=Bi: e.matmul(q.t[:, co:co + 128], Bi, Ai, start=True, stop=True),
                                 reads=rdi, writes=[q.r])
                        S.op("pe", lambda e, q=q, co=co, Ai=Ai, Bi=Bi: e.matmul(q.t[:, co + 128:co + 256], Ai, Bi, start=True, stop=True),
                             reads=rdi, writes=[q.r])
                    for qi in range((n + 1) // 2):
                        m = min(2, n - 2 * qi)
                        eng = "act" if qi == 0 else "dve"
                        copy_op(eng, ab_o.t[:, 2 * qi:2 * qi + m, :], Q[qi].t[:, 0:m * 256].rearrange("p (c j) -> p c j", j=256),
                                [Q[qi].r], [ab_o.r])
                    for ci in range(n):
                        S.op("pe", lambda e, ci=ci: e.matmul(Q[2].t[:, ci * 128:(ci + 1) * 128], ab_o.t[:, ci, 128:256], p_i.t[:, ci, :],
                                                             start=True, stop=True),
                             reads=[ab_o.r, p_i.r], writes=[Q[2].r])
                    if l == 0 and T.idx == 0 and hp == 0 and g0 == 0 and k == 1:
                        dump(NTb, NTb.t[:, 0, :], 128, 128)
                        dump(ab_o, ab_o.t[:, 0, :], 128, 256)
                        dump(Q[2], Q[2].t[:, 0:128], 128, 128)
                        dump(p_i, p_i.t[:, 0, :], 128, 128)
                    S.op("dve", lambda e, p_o=p_o, p_i=p_i: e.tensor_tensor(out=p_o.t[:, 0:n, :], in0=Q[2].t[:, 0:n * 128].rearrange("p (c j) -> p c j", j=128),
                                                                            in1=p_i.t[:, 0:n, :], op=ALU.add),
                         reads=[Q[2].r, p_i.r], writes=[p_o.r])
                    if l == 0 and T.idx == 0 and hp == 0 and g0 == 0 and k == 1:
                        dump(p_o, p_o.t[:, 0, :], 128, 128)
                Tm = Pb[1]
                if l == 0 and T.idx == 0 and hp == 0 and g0 == 0:
                    dump(Gm, Gm.t[:, 0, :], 128, 512)
                    dump(Tm, Tm.t[:, 0, :], 128, 128)
                for (src, dst, q, eng) in ((VD, VDT, Q[0], "act"), (KhD, KhT, Q[1], "dve"), (BhD, BhT, Q[3], "act")):
                    for ci in range(n):
                        c = g0 + ci
                        S.op("pe", lambda e, src=src, q=q, c=c, ci=ci: e.matmul(q.t[:, ci * 128:(ci + 1) * 128], src.t[:, c, :], cb("ident"),
                                                                                start=True, stop=True),
                             reads=[src.r, cstb.r], writes=[q.r])
                    copy_op(eng, dst.t[:, 0:n, :], q.t[:, 0:n * 128].rearrange("p (c j) -> p c j", j=128), [q.r], [dst.r])
                for ci in range(n):
                    c = g0 + ci
                    if T.kind == "S" or (T.first and c == 0):
                        if T.kind == "S":
                            sg = stg[stgctr[0] % 2]
                            stgctr[0] += 1
                            S.op("act", lambda e, sg=sg, c=c: e.dma_start(out=sg.t[:], in_=wkv_in[l, c, hp]), writes=[sg.r], dma=True)
                            S.op("dve", lambda e, sg=sg: e.tensor_tensor(out=st_.t[:], in0=sg.t[:], in1=cf("J"), op=ALU.add),
                                 reads=[sg.r, cst.r], writes=[st_.r])
                        else:
                            S.op("dve", lambda e: e.tensor_copy(out=st_.t[:], in_=cf("J")), reads=[cst.r], writes=[st_.r])
                    s0 = S0bf[s0ctr[0] % 2]
                    s0ctr[0] += 1
                    S.op("act", lambda e, s0=s0: e.activation(out=s0.t[:], in_=st_.t[:], func=AF.Copy), reads=[st_.r], writes=[s0.r])
                    S.op("pe", lambda e, c=c, s0=s0: e.matmul(Q[2].t[:, 0:128], KR.t[:, c, 0:128], s0.t[:], start=True, stop=False),
                         reads=[KR.r, s0.r], writes=[Q[2].r])
                    S.op("pe", lambda e, ci=ci: e.matmul(Q[2].t[:, 0:128], Gm.t[:, ci, 256:384], VDT.t[:, ci, :], start=False, stop=True),
                         reads=[Gm.r, VDT.r], writes=[Q[2].r])
                    S.op("act", lambda e: e.activation(out=ZT.t[:], in_=Q[2].t[:, 0:128], func=AF.Copy), reads=[Q[2].r], writes=[ZT.r])
                    S.op("pe", lambda e, ci=ci: e.matmul(Q[2].t[:, 128:256], Tm.t[:, ci, :], ZT.t[:], start=True, stop=True),
                         reads=[Tm.r, ZT.r], writes=[Q[2].r])
                    S.op("dve", lambda e: e.tensor_scalar(out=nUT.t[:], in0=Q[2].t[:, 128:256], scalar1=-1.0, scalar2=None, op0=ALU.mult),
                         reads=[Q[2].r], writes=[nUT.r])
                    osl = slice(ci * 128, (ci + 1) * 128)
                    S.op("pe", lambda e, c=c, s0=s0, osl=osl: e.matmul(Q[3].t[:, osl], s0.t[:], KR.t[:, c, 128:256], start=True, stop=False),
                         reads=[KR.r, s0.r], writes=[Q[3].r])
                    S.op("pe", lambda e, ci=ci, osl=osl: e.matmul(Q[3].t[:, osl], VDT.t[:, ci, :], Gm.t[:, ci, 384:512], start=False, stop=False),
                         reads=[Gm.r, VDT.r], writes=[Q[3].r])
                    S.op("pe", lambda e, ci=ci, osl=osl: e.matmul(Q[3].t[:, osl], nUT.t[:], Gm.t[:, ci, 128:256], start=False, stop=True),
                         reads=[Gm.r, nUT.r], writes=[Q[3].r])
                    S.op("pe", lambda e, ci=ci: e.matmul(Q[2].t[:, 256:384], KhT.t[:, ci, :], VDT.t[:, ci, :], start=True, stop=False),
                         reads=[KhT.r, VDT.r], writes=[Q[2].r])
                    S.op("pe", lambda e, ci=ci: e.matmul(Q[2].t[:, 256:384], BhT.t[:, ci, :], nUT.t[:], start=False, stop=True),
                         reads=[BhT.r, nUT.r], writes=[Q[2].r])
                    S.op("dve", lambda e, c=c: e.scalar_tensor_tensor(out=st_.t[:], in0=st_.t[:], scalar=WC.t[:, c:c + 1], in1=Q[2].t[:, 256:384],
                                                                      op0=ALU.mult, op1=ALU.add),
                         reads=[st_.r, WC.r, Q[2].r], writes=[st_.r])
                    if l == 0 and T.idx == 0 and hp == 0 and c == 0:
                        dump(st_, st_.t[:], 128, 128)
                    if T.kind == "S":
                        S.op("sp", lambda e, c=c: e.dma_start(out=wkv_out[l, 1 + c, hp], in_=st_.t[:]), reads=[st_.r], dma=True)
                    elif T.last and c == nch - 1:
                        S.op("sp", lambda e: e.dma_start(out=wkv_out[l, 0, hp], in_=st_.t[:]), reads=[st_.r], dma=True)
                for h in range(2):
                    hs = slice(64 * h, 64 * h + 64)
                    src = Q[3].t[hs, 0:n * 128].rearrange("p (c j) -> p c j", j=128)[:, :, 64 * h:64 * h + 64]
                    dst = A["osp"].t[hs, g0 * 64:(g0 + n) * 64].rearrange("p (c j) -> p c j", j=64)
                    copy_op("act" if h == 0 else "dve", dst, src, [Q[3].r], [A["osp"].r])
                copy_op("dve", odb.t[:, g0:g0 + n, :], Q[3].t[:, 0:n * 128].rearrange("p (c j) -> p c j", j=128), [Q[3].r], [odb.r])

        def spill(T, hp, part):
            w, nch = T.w, T.nch
            i = T.idx
            if part == 0:
                S.op("sp", lambda e: e.dma_start(out=bg_sp[i, hp, :, 0:w], in_=A["bg"].t[:, 0:w]), reads=[A["bg"].r], writes=[spres[(i, hp)]], dma=True)
                S.op("sp", lambda e: e.dma_start(out=g_sp[i, hp, :, 0:w], in_=A["gg"].t[:, 0:w]), reads=[A["gg"].r], writes=[spres[(i, hp)]], dma=True)
            else:
                S.op("sp", lambda e: e.dma_start(out=o_sp[i, hp, :, 0:w], in_=A["osp"].t[:, 0:w]), reads=[A["osp"].r], writes=[spres[(i, hp)]], dma=True)
                S.op("sp", lambda e: e.dma_start(out=od_sp[i, hp, :, 0:nch * 128], in_=odb.t[:, 0:nch, :]), reads=[odb.r], writes=[spres[(i, hp)]], dma=True)

        spres = {}

        def phaseA(l, T):
            w = T.w
            xsrc = (xT_in if l == 0 else x1_scr).rearrange("(c p) t -> p c t", p=128)
            pbn = gbank()
            for ps_ in range(2):
                for gq in range(8):
                    xs_ = xstg[gq % 2]
                    rd = [] if l == 0 else [x1res[(T.idx, 2 * gq)], x1res[(T.idx, 2 * gq + 1)]]
                    S.op("act", lambda e, gq=gq, xs_=xs_: e.dma_start(out=xs_.t[:, :, 0:w], in_=xsrc[:, 2 * gq:2 * gq + 2, T.c0:T.c0 + w]),
                         reads=rd, writes=[xs_.r], dma=True)
                    for cc in range(2):
                        c = 2 * gq + cc
                        if ps_ == 0:
                            s_ = sq[c % 2]
                            S.op("act", lambda e, cc=cc, s_=s_, xs_=xs_: e.activation(out=s_.t[:, 0:w], in_=xs_.t[:, cc, 0:w], func=AF.Square),
                                 reads=[xs_.r], writes=[s_.r])
                            S.op("pe", lambda e, c=c, s_=s_: e.matmul(pbn.t[:, 0:w], cb("ones"), s_.t[:, 0:w], start=(c == 0), stop=(c == 15)),
                                 reads=[s_.r, cstb.r], writes=[pbn.r])
                        else:
                            S.op("dve", lambda e, c=c, cc=cc, xs_=xs_: e.scalar_tensor_tensor(out=hT.t[:, c, 0:w], in0=xs_.t[:, cc, 0:w], scalar=pv("norm1", l, c),
                                                                                             in1=rstd.t[:, 0:w], op0=ALU.mult, op1=ALU.mult),
                                 reads=[xs_.r, rstd.r, pvec.r], writes=[hT.r])
                if ps_ == 0:
                    S.op("dve", lambda e: e.tensor_scalar(out=rstd.t[:, 0:w], in0=pbn.t[:, 0:w], scalar1=1.0 / D, scalar2=RMS_EPS,
                                                          op0=ALU.mult, op1=ALU.add), reads=[pbn.r], writes=[rstd.r])
                    S.op("act", lambda e: e.activation(out=rstd.t[:, 0:w], in_=rstd.t[:, 0:w], func=AF.Sqrt), reads=[rstd.r], writes=[rstd.r])
                    S.op("dve", lambda e: e.reciprocal(out=rstd.t[:, 0:w], in_=rstd.t[:, 0:w]), reads=[rstd.r], writes=[rstd.r])
            if T.first:
                if l == 0:
                    S.op("act", lambda e: e.dma_start(out=xprev.t[:, 0:16], in_=xprev_in), writes=[xprev.r], dma=True)
                S.op("act", lambda e: e.activation(out=sq[0].t[:, 0:16], in_=xprev.t[:, 0:16], func=AF.Square), reads=[xprev.r], writes=[sq[0].r])
                pbk = gbank()
                S.op("pe", lambda e: e.matmul(pbk.t[:, 0:16], cb("ones"), sq[0].t[:, 0:16], start=True, stop=True), reads=[sq[0].r, cstb.r], writes=[pbk.r])
                S.op("dve", lambda e: e.reduce_sum(out=xprev.t[:, 16:17], in_=pbk.t[:, 0:16], axis=mybir.AxisListType.X), reads=[pbk.r], writes=[xprev.r])
                S.op("dve", lambda e: e.tensor_scalar(out=xprev.t[:, 16:17], in0=xprev.t[:, 16:17], scalar1=1.0 / D, scalar2=RMS_EPS, op0=ALU.mult, op1=ALU.add),
                     reads=[xprev.r], writes=[xprev.r])
                S.op("act", lambda e: e.activation(out=xprev.t[:, 16:17], in_=xprev.t[:, 16:17], func=AF.Sqrt), reads=[xprev.r], writes=[xprev.r])
                S.op("dve", lambda e: e.reciprocal(out=xprev.t[:, 16:17], in_=xprev.t[:, 16:17]), reads=[xprev.r], writes=[xprev.r])
                n1 = PV[("norm1", l)]
                S.op("dve", lambda e: e.tensor_tensor(out=xprev.t[:, 20:36], in0=xprev.t[:, 0:16], in1=pvec.t[:, n1:n1 + 16], op=ALU.mult),
                     reads=[xprev.r, pvec.r], writes=[xprev.r])
                S.op("dve", lambda e: e.tensor_scalar(out=hprev.t[:, 0:16], in0=xprev.t[:, 20:36], scalar1=xprev.t[:, 16:17], scalar2=None, op0=ALU.mult),
                     reads=[xprev.r], writes=[hprev.r])
            blk, br = wget(l, "A0")
            outs = [(txw, AF.Tanh), (xab, AF.Copy), (sxg0, AF.Sigmoid), (sxg1, AF.Sigmoid)]
            for (cid, off, rows), (ob, fn) in zip(LORA, outs):
                def post(xs, ob=ob, fn=fn, rows=rows):
                    S.op("act", lambda e: e.activation(out=ob.t[0:rows, 0:w], in_=xs, func=fn), reads=[ltmp.r], writes=[ob.r])
                p_chunk(l, T, blk, br, off - 3072, rows, cid, None, None, post=post)
            def front(hp):
                blk, br = wget(l, "Ahp%d" % hp)
                bufs = [RKV[j][hp % 2] for j in range(3)]
                for j in range(3):
                    p_chunk(l, T, blk, br, 128 * j, 128, j * 8 + hp, lambda j=j: bufs[j].t[:, 0:w], bufs[j].r)
                prep(l, T, hp, *bufs)
                spres[(T.idx, hp)] = Res("sp")
                spill(T, hp, 0)

            def back(hp):
                scan(l, T, hp)
                spill(T, hp, 1)

            S.replay(S.capture(lambda: front(0)))
            for hp in range(8):
                sb_ = S.capture(lambda: back(hp))
                sf_ = S.capture(lambda: front(hp + 1)) if hp < 7 else []
                S.replay(sb_, sf_)

        uya = sb("uya", [128, 16, 512], BF16, reg="C")
        uT = Buf(uya.t[:, 0:8, :], "uT")
        ya = Buf(uya.t[:, 8:16, :], "ya")
        uT.r = uya.r
        ya.r = uya.r
        hid = uya
        yb = sb("yb", [128, 8, 512], BF16, reg="C")
        vg = sb("vg", [128, 4, 1024], reg="C")
        vnb = [sb("vnb%d" % i, [128, 1024], BF16, reg="C") for i in range(2)]
        mixT = sb("mixT", [128, 16, 512], BF16, reg="C")
        lng = sb("lng", [128, DA], reg="C")
        lnb = sb("lnb", [128, DA], reg="C")
        bsf = Buf(vg.t[0:1, 0, :], "bsf")
        bsf.r = vg.r
        bhf = Buf(vg.t[0:1, 1, :], "bhf")
        bhf.r = vg.r
        bhi = sb("bhi", [1, 1024], BF16, reg="C")
        blo = sb("blo", [1, 1024], BF16, reg="C")
        wsb = sb("wsb", [128, 8, 128], BF16, reg="C")
        wsb2 = sb("wsb2", [128, 8, 64], BF16, reg="C")
        bnst = sb("bnst", [128, 2, 6], reg="C")
        mv = sb("mv", [128, 2], reg="C")
        Ct = [sb("Ct%d" % i, [128, 512], reg="C") for i in range(6)]
        ob_ = sb("Co", [128, 512], reg="C")
        odl = None
        bgl = sb("Cbg", [128, 512], reg="C")
        ggl = sb("Cgg", [128, 512], reg="C")
        sqb = sb("sqb", [128, 512], BF16, reg="C")
        obf = sb("obf", [128, 512], BF16, reg="C")

        def layer_params_C(l):
            S.op("act", lambda e: e.dma_start(out=lng.t[:], in_=lng_in[l:l + 1, :].partition_broadcast(128)), writes=[lng.r], dma=True)
            S.op("act", lambda e: e.dma_start(out=lnb.t[:], in_=lnb_in[l:l + 1, :].partition_broadcast(128)), writes=[lnb.r], dma=True)
            S.op("act", lambda e: e.dma_start(out=bsf.t[0:1, :], in_=bs_in[l]), writes=[bsf.r], dma=True)
            S.op("dve", lambda e: e.tensor_copy(out=bhi.t[0:1, :], in_=bsf.t[0:1, :]), reads=[bsf.r], writes=[bhi.r])
            S.op("dve", lambda e: e.tensor_copy(out=bhf.t[0:1, :], in_=bhi.t[0:1, :]), reads=[bhi.r], writes=[bhf.r])
            S.op("dve", lambda e: e.tensor_tensor(out=blo.t[0:1, :], in0=bsf.t[0:1, :], in1=bhf.t[0:1, :], op=ALU.subtract),
                 reads=[bsf.r, bhf.r], writes=[blo.r])
            S.op("pool", lambda e: e.dma_start(out=wsb.t[:], in_=wsT_in[l]), writes=[wsb.r], dma=True)
            trb = cb("triu").unsqueeze(1).broadcast_to([128, 8, 128])
            S.op("dve", lambda e: e.tensor_tensor(out=wsb.t[:], in0=wsb.t[:], in1=trb, op=ALU.mult), reads=[wsb.r, cstb.r], writes=[wsb.r])
            S.op("pool", lambda e: e.dma_start(out=wsb2.t[64:128, :, :], in_=wsT_in[l, 0:64, :, 0:64]), writes=[wsb2.r], dma=True)
            trb2 = cb("triu")[64:128, 64:128].unsqueeze(1).broadcast_to([64, 8, 64])
            S.op("dve", lambda e: e.tensor_tensor(out=wsb2.t[64:128, :, :], in0=wsb2.t[64:128, :, :], in1=trb2, op=ALU.mult),
                 reads=[wsb2.r, cstb.r], writes=[wsb2.r])

        def gemm_fm(blk, blkres, moff, rhs_fn, nk, rhsres, w):
            pb = gbank()
            for kc in range(nk):
                S.op("pe", lambda e, kc=kc: e.matmul(pb.t[:, 0:w], blk[:, kc, moff:moff + 128], rhs_fn(kc), start=(kc == 0), stop=(kc == nk - 1)),
                     reads=[blkres] + rhsres, writes=[pb.r])
            return pb

        def phaseC(l, T, last_layer):
            w, nseq = T.w, T.nseq
            load_x(l, T)
            rmsnorm(xTr, lambda c: xT.t[:, c, 0:w], w, "norm1", l, lambda c: hT.t[:, c, 0:w], [hT.r] * 16)
            ntb = (w + 127) // 128
            def post_hp(hp):
                i = T.idx
                o_, od_, bg_, gg_ = ob_, odl, bgl, ggl
                rs = [spres[(i, hp)]]
                S.op("pool", lambda e, o_=o_: e.dma_start(out=o_.t[:, 0:w], in_=o_sp[i, hp, :, 0:w]), reads=rs, writes=[o_.r], dma=True)
                S.op("pool", lambda e, bg_=bg_: e.dma_start(out=bg_.t[:, 0:w], in_=bg_sp[i, hp, :, 0:w]), reads=rs, writes=[bg_.r], dma=True)
                S.op("pool", lambda e, gg_=gg_: e.dma_start(out=gg_.t[:, 0:w], in_=g_sp[i, hp, :, 0:w]), reads=rs, writes=[gg_.r], dma=True)
                S.op("act", lambda e, o_=o_: e.activation(out=obf.t[:, 0:w], in_=o_.t[:, 0:w], func=AF.Copy), reads=[o_.r], writes=[obf.r])
                pb = gbank()
                S.op("pe", lambda e, pb=pb: e.matmul(pb.t[:, 0:w], cb("bones"), obf.t[:, 0:w], start=True, stop=True), reads=[cstb.r, obf.r], writes=[pb.r])
                oc = Ct[0]
                S.op("dve", lambda e, pb=pb, o_=o_: e.scalar_tensor_tensor(out=oc.t[:, 0:w], in0=pb.t[:, 0:w], scalar=-1.0 / 64, in1=o_.t[:, 0:w],
                                                                          op0=ALU.mult, op1=ALU.add), reads=[pb.r, o_.r], writes=[oc.r])
                S.op("act", lambda e: e.activation(out=sqb.t[:, 0:w], in_=oc.t[:, 0:w], func=AF.Square), reads=[oc.r], writes=[sqb.r])
                pb2 = gbank()
                S.op("pe", lambda e, pb2=pb2: e.matmul(pb2.t[:, 0:w], cb("bones"), sqb.t[:, 0:w], start=True, stop=True), reads=[cstb.r, sqb.r], writes=[pb2.r])
                rs_ = Ct[1]
                S.op("dve", lambda e, pb2=pb2: e.tensor_scalar(out=rs_.t[:, 0:w], in0=pb2.t[:, 0:w], scalar1=1.0 / 64, scalar2=GN_EPS, op0=ALU.mult, op1=ALU.add),
                     reads=[pb2.r], writes=[rs_.r])
                S.op("act", lambda e: e.activation(out=rs_.t[:, 0:w], in_=rs_.t[:, 0:w], func=AF.Sqrt), reads=[rs_.r], writes=[rs_.r])
                S.op("dve", lambda e: e.reciprocal(out=rs_.t[:, 0:w], in_=rs_.t[:, 0:w]), reads=[rs_.r], writes=[rs_.r])
                S.op("dve", lambda e: e.tensor_tensor(out=oc.t[:, 0:w], in0=oc.t[:, 0:w], in1=rs_.t[:, 0:w], op=ALU.mult), reads=[oc.r, rs_.r], writes=[oc.r])
                S.op("dve", lambda e, hp=hp: e.tensor_scalar(out=oc.t[:, 0:w], in0=oc.t[:, 0:w], scalar1=pv("gn_g", l, hp), scalar2=pv("gn_b", l, hp),
                                                             op0=ALU.mult, op1=ALU.add), reads=[oc.r, pvec.r], writes=[oc.r])
                S.op("dve", lambda e, gg_=gg_: e.tensor_tensor(out=oc.t[:, 0:w], in0=oc.t[:, 0:w], in1=gg_.t[:, 0:w], op=ALU.mult), reads=[oc.r, gg_.r], writes=[oc.r])
                S.op("dve", lambda e, bg_=bg_, hp=hp: e.tensor_tensor(out=yb.t[:, hp, 0:w], in0=oc.t[:, 0:w], in1=bg_.t[:, 0:w], op=ALU.add),
                     reads=[oc.r, bg_.r], writes=[yb.r])


            pending = list(range(8))
            for j in range(2):
                blk, br = wget(l, "Cv%d" % j)
                for tb in range(ntb):
                    pb = gbank()
                    for kc in range(16):
                        S.op("pe", lambda e, kc=kc, tb=tb, pb=pb, blk=blk: e.matmul(pb.t[:, 0:512], hT.t[:, kc, tb * 128:(tb + 1) * 128], blk[:, kc, 0:512],
                                                                                   start=(kc == 0), stop=(kc == 15)),
                             reads=[br, hT.r], writes=[pb.r])
                    S.op("act", lambda e, tb=tb, pb=pb, j=j: e.activation(out=vg.t[:, tb, j * 512:(j + 1) * 512], in_=pb.t[:, 0:512], func=AF.Gelu_apprx_tanh),
                         reads=[pb.r], writes=[vg.r])
                    if pending:
                        post_hp(pending.pop(0))
            for j in range(2):
                blk, br = wget(l, "Cu%d" % j)
                for m in range(4):
                    pb = gemm_fm(blk, br, m * 128, lambda kc: hT.t[:, kc, 0:w], 16, [hT.r], w)
                    S.op("act", lambda e, pb=pb, j=j, m=m: e.activation(out=uT.t[:, j * 4 + m, 0:w], in_=pb.t[:, 0:w], func=AF.Gelu_apprx_tanh),
                         reads=[pb.r], writes=[uT.r])
            while pending:
                post_hp(pending.pop(0))
            def ln_block(tb):
                vn_ = vnb[tb % 2]
                for j in range(2):
                    S.op("dve", lambda e, tb=tb, j=j: e.bn_stats(out=bnst.t[:, j, :], in_=vg.t[:, tb, j * 512:(j + 1) * 512]), reads=[vg.r], writes=[bnst.r])
                S.op("dve", lambda e: e.bn_aggr(out=mv.t[:], in_=bnst.t[:].rearrange("p a b -> p (a b)")), reads=[bnst.r], writes=[mv.r])
                S.op("dve", lambda e: e.tensor_scalar(out=mv.t[:, 1:2], in0=mv.t[:, 1:2], scalar1=LN_EPS, scalar2=None, op0=ALU.add), reads=[mv.r], writes=[mv.r])
                S.op("act", lambda e: e.activation(out=mv.t[:, 1:2], in_=mv.t[:, 1:2], func=AF.Sqrt), reads=[mv.r], writes=[mv.r])
                S.op("dve", lambda e: e.reciprocal(out=mv.t[:, 1:2], in_=mv.t[:, 1:2]), reads=[mv.r], writes=[mv.r])
                S.op("dve", lambda e, tb=tb: e.tensor_scalar(out=vg.t[:, tb, :], in0=vg.t[:, tb, :], scalar1=mv.t[:, 0:1], scalar2=mv.t[:, 1:2],
                                                             op0=ALU.subtract, op1=ALU.mult), reads=[vg.r, mv.r], writes=[vg.r])
                S.op("pool", lambda e, tb=tb: e.tensor_tensor(out=vg.t[:, tb, :], in0=vg.t[:, tb, :], in1=lng.t[:], op=ALU.mult), reads=[vg.r, lng.r], writes=[vg.r])
                S.op("dve", lambda e, tb=tb: e.tensor_tensor(out=vg.t[:, tb, :], in0=vg.t[:, tb, :], in1=lnb.t[:], op=ALU.add), reads=[vg.r, lnb.r], writes=[vg.r])
                S.op("act", lambda e, tb=tb, vn_=vn_: e.activation(out=vn_.t[:], in_=vg.t[:, tb, :], func=AF.Copy), reads=[vg.r], writes=[vn_.r])
                if T.kind == "S":
                    S.op("act", lambda e, tb=tb: e.dma_start(out=vn_out[l, tb], in_=vg.t[:, tb, :]), reads=[vg.r], dma=True)

            def spatial_block(tb):
                vn_ = vnb[tb % 2]
                if T.kind == "P":
                    blocks = [(0, 128, tb * 128)]
                else:
                    blocks = [(64 * b, 64, tb * 128 + 64 * b) for b in range(2)]
                for half in range(2):
                    pb = gbank()
                    for gi in range(4):
                        g = half * 4 + gi
                        for (prow, bl, qoff) in blocks:
                            wsrc = wsb if prow == 0 else wsb2
                            oc_ = gi * 128 + (qoff - tb * 128)
                            S.op("pe", lambda e, g=g, prow=prow, bl=bl, oc_=oc_, wsrc=wsrc, pb=pb, vn_=vn_: e.matmul(
                                pb.t[:, oc_:oc_ + bl], vn_.t[prow:prow + bl, g * 128:(g + 1) * 128], wsrc.t[prow:prow + bl, g, 0:bl], start=True, stop=False),
                                reads=[vn_.r, wsrc.r], writes=[pb.r])
                            S.op("pe", lambda e, g=g, bl=bl, oc_=oc_, pb=pb: e.matmul(pb.t[:, oc_:oc_ + bl], cb("ones")[0:1, :], bhi.t[0:1, g * 128:g * 128 + bl],
                                                                                     start=False, stop=False), reads=[cstb.r, bhi.r], writes=[pb.r])
                            S.op("pe", lambda e, g=g, bl=bl, oc_=oc_, pb=pb: e.matmul(pb.t[:, oc_:oc_ + bl], cb("ones")[0:1, :], blo.t[0:1, g * 128:g * 128 + bl],
                                                                                     start=False, stop=True), reads=[cstb.r, blo.r], writes=[pb.r])
                    S.op("dve", lambda e, pb=pb, half=half, tb=tb: e.tensor_tensor(
                        out=ya.t[:, half * 4:half * 4 + 4, tb * 128:(tb + 1) * 128], in0=pb.t[:, 0:512].rearrange("p (g q) -> p g q", g=4),
                        in1=uT.t[:, half * 4:half * 4 + 4, tb * 128:(tb + 1) * 128], op=ALU.mult),
                        reads=[pb.r, uT.r], writes=[ya.r])

            ln_block(0)
            for tb in range(ntb):
                if tb + 1 < ntb:
                    ln_block(tb + 1)
                spatial_block(tb)
            for mg in range(8):
                G, gr = wget(l, "G%d" % mg)
                P_, pr = wget(l, "P%d" % mg)
                for mm in range(2):
                    m = 2 * mg + mm
                    pga = gemm_fm(G, gr, mm * 128, lambda kc: hT.t[:, kc, 0:w], 16, [hT.r], w)
                    pgb = gemm_fm(G, gr, 256 + mm * 128, lambda kc: hT.t[:, kc, 0:w], 16, [hT.r], w)
                    ppa = gemm_fm(P_, pr, mm * 128, lambda kc: ya.t[:, kc, 0:w], 8, [ya.r], w)
                    ppb = gemm_fm(P_, pr, 256 + mm * 128, lambda kc: yb.t[:, kc, 0:w], 8, [yb.r], w)
                    sa, sb_ = Ct[2 + (m % 2) * 2], Ct[3 + (m % 2) * 2]
                    S.op("act", lambda e, pga=pga, sa=sa: e.activation(out=sa.t[:, 0:w], in_=pga.t[:, 0:w], func=AF.Sigmoid), reads=[pga.r], writes=[sa.r])
                    S.op("act", lambda e, pgb=pgb, sb_=sb_: e.activation(out=sb_.t[:, 0:w], in_=pgb.t[:, 0:w], func=AF.Sigmoid), reads=[pgb.r], writes=[sb_.r])
                    S.op("dve", lambda e, ppa=ppa, sa=sa: e.tensor_tensor(out=sa.t[:, 0:w], in0=ppa.t[:, 0:w], in1=sa.t[:, 0:w], op=ALU.mult),
                         reads=[ppa.r, sa.r], writes=[sa.r])
                    S.op("dve", lambda e, ppb=ppb, sb_=sb_: e.tensor_tensor(out=sb_.t[:, 0:w], in0=ppb.t[:, 0:w], in1=sb_.t[:, 0:w], op=ALU.mult),
                         reads=[ppb.r, sb_.r], writes=[sb_.r])
                    S.op("pool", lambda e, sa=sa, sb_=sb_, m=m: e.tensor_tensor(out=mixT.t[:, m, 0:w], in0=sa.t[:, 0:w], in1=sb_.t[:, 0:w], op=ALU.add),
                         reads=[sa.r, sb_.r], writes=[mixT.r])
            for j in range(4):
                blk, br = wget(l, "O%d" % j)
                for m in range(4):
                    pb = gemm_fm(blk, br, m * 128, lambda kc: mixT.t[:, kc, 0:w], 16, [mixT.r], w)
                    c = 4 * j + m
                    S.op("dve", lambda e, pb=pb, c=c: e.tensor_tensor(out=xT.t[:, c, 0:w], in0=pb.t[:, 0:w], in1=xT.t[:, c, 0:w], op=ALU.add),
                         reads=[pb.r, xTr[c]], writes=[xTr[c]])
            rmsnorm(xTr, lambda c: xT.t[:, c, 0:w], w, "norm2", l, lambda c: hT.t[:, c, 0:w], [hT.r] * 16)
            for q in range(4):
                for j in range(4):
                    blk, br = wget(l, "U%d_%d" % (q, j))
                    for m in range(4):
                        pb = gemm_fm(blk, br, m * 128, lambda kc: hT.t[:, kc, 0:w], 16, [hT.r], w)
                        t = Ct[(4 * j + m) % 2]
                        S.op("act", lambda e, pb=pb, t=t: e.activation(out=t.t[:, 0:w], in_=pb.t[:, 0:w], func=AF.Relu), reads=[pb.r], writes=[t.r])
                        S.op("pool", lambda e, t=t, j=j, m=m: e.tensor_tensor(out=hid.t[:, 4 * j + m, 0:w], in0=t.t[:, 0:w], in1=t.t[:, 0:w], op=ALU.mult),
                             reads=[t.r], writes=[hid.r])
                for j in range(4):
                    blk, br = wget(l, "D%d_%d" % (q, j))
                    for m in range(4):
                        pb = gemm_fm(blk, br, m * 128, lambda kc: hid.t[:, kc, 0:w], 16, [hid.r], w)
                        c = 4 * j + m
                        S.op("dve", lambda e, pb=pb, c=c: e.tensor_tensor(out=xT.t[:, c, 0:w], in0=pb.t[:, 0:w], in1=xT.t[:, c, 0:w], op=ALU.add),
                             reads=[pb.r, xTr[c]], writes=[xTr[c]])
            if not last_layer:
                dst = x1_scr.rearrange("(c p) t -> p c t", p=128)
                for c in range(16):
                    x1res[(T.idx, c)] = Res("x1")
                    S.op("act", lambda e, c=c: e.dma_start(out=dst[:, c, T.c0:T.c0 + w], in_=xT.t[:, c, 0:w]), reads=[xTr[c]], writes=[x1res[(T.idx, c)]], dma=True)
            else:
                rmsnorm(xTr, lambda c: xT.t[:, c, 0:w], w, "norm_f", 0, lambda c: xT.t[:, c, 0:w], xTr)
                dst = yT_out.rearrange("(c p) t -> p c t", p=128)
                for c in range(16):
                    S.op("act", lambda e, c=c: e.dma_start(out=dst[:, c, T.c0:T.c0 + w], in_=xT.t[:, c, 0:w]), reads=[xTr[c]], dma=True)

        for l in range(n_layers):
            phase[0] = 0
            layer_params(l)
            for T in tiles:
                phaseA(l, T)
            S.op("act", lambda e, l=l: e.dma_start(out=tsh_out[l], in_=plast.t[:]), reads=[plast.r], dma=True)
            S.barrier()
            phase[0] = 1
            layer_params_C(l)
            for T in tiles:
                phaseC(l, T, l == n_layers - 1)
            S.barrier()
        S.emit()
    return nc


def _cols(v):
    v = np.asarray(v, np.float32).reshape(-1, 128)
    return np.ascontiguousarray(v.T)


def _pvec(inp):
    pvv = np.zeros((128, NPV), np.float32)
    for l in range(NL):
        pvv[:, PV[("norm1", l)]:PV[("norm1", l)] + 16] = _cols(inp["norm1"][l])
        pvv[:, PV[("norm2", l)]:PV[("norm2", l)] + 16] = _cols(inp["norm2"][l])
        mu = np.asarray(inp["mu_shift"][l], np.float32)
        mo = PV[("mu", l)]
        pvv[:, mo:mo + 24] = _cols(mu[0:3072])
        for (cid, off, rows) in LORA:
            pvv[0:rows, mo + cid] = mu[off:off + rows]
        for nm in ("w0", "a0", "k_k", "k_a", "gn_g", "gn_b"):
            pvv[:, PV[(nm, l)]:PV[(nm, l)] + 8] = _cols(inp[nm][l])
        pvv[:, PV[("r_k", l)]:PV[("r_k", l)] + 8] = _cols(np.asarray(inp["r_k"][l]).reshape(-1))
    pvv[:, PV[("norm_f", 0)]:PV[("norm_f", 0)] + 16] = _cols(inp["norm_f"])
    return pvv


def _tsh_layout(ts):
    L, n, _ = ts.shape
    out = np.zeros((L, 128, NPCH, n), np.float32)
    out[:, :, 0:24, :] = ts[:, :, 0:3072].reshape(L, n, 24, 128).transpose(0, 3, 2, 1)
    for (cid, off, rows) in LORA:
        out[:, 0:rows, cid, :] = ts[:, :, off:off + rows].transpose(0, 2, 1)
    return out


def _tsh_unlayout(t):
    L, _, _, n = t.shape
    out = np.zeros((L, n, DSH), np.float32)
    out[:, :, 0:3072] = t[:, :, 0:24, :].transpose(0, 3, 2, 1).reshape(L, n, 3072)
    for (cid, off, rows) in LORA:
        out[:, :, off:off + rows] = t[:, 0:rows, cid, :].transpose(0, 2, 1)
    return out


def _wkv_layout(s):
    L, n = s.shape[:2]
    out = np.zeros((L, n, 8, 128, 128), np.float32)
    sT = s.transpose(0, 1, 2, 4, 3).reshape(L, n, 8, 2, 64, 64)
    out[:, :, :, 0:64, 0:64] = sT[:, :, :, 0]
    out[:, :, :, 64:128, 64:128] = sT[:, :, :, 1]
    return out


def _wkv_unlayout(o):
    L, n = o.shape[:2]
    s = np.zeros((L, n, 8, 2, 64, 64), np.float32)
    s[:, :, :, 0] = o[:, :, :, 0:64, 0:64]
    s[:, :, :, 1] = o[:, :, :, 64:128, 64:128]
    return np.ascontiguousarray(s.reshape(L, n, 16, 64, 64).transpose(0, 1, 2, 4, 3))


_PROG = {}


def _program(key, **kw):
    if key not in _PROG:
        _PROG[key] = build_program(**kw)
    return _PROG[key]


def make_in_maps(inp, seg_x, seg_prev, samp_ids):
    f = lambda a: np.ascontiguousarray(np.asarray(a, np.float32))
    shared = {
        "pvec": _pvec(inp), "cst": CST.copy(),
        "ln_v_g": f(inp["ln_v_g"]), "ln_v_b": f(inp["ln_v_b"]),
        "b_s": f(inp["b_s"]).reshape(NL, 1, 1024),
        "w_sT": np.ascontiguousarray(f(inp["w_s"]).transpose(0, 3, 1, 2)),
        "w2": f(inp["w2"]), "a2": f(inp["a2"]), "g2": f(inp["g2"]),
    }
    for k in WSHAPES:
        shared[k] = f(inp[k])
    xs = f(inp["x_sample"])
    ts = f(inp["state_tshift"])
    wk = f(inp["state_wkv"])
    maps = []
    for c in range(len(seg_x)):
        ids = samp_ids[c]
        xa = np.concatenate([seg_x[c]] + [xs[i] for i in ids], axis=0)
        m = dict(shared)
        m["xT"] = np.ascontiguousarray(xa.T)
        m["xprev"] = _cols(seg_prev[c])
        m["tsh_s"] = _tsh_layout(ts[:, ids, :])
        m["wkv_s"] = _wkv_layout(wk[:, ids])
        maps.append(m)
    return maps


def kernel(**inp):
    xp = np.asarray(inp["x_prompt"], np.float32)
    B, SEQ, _ = xp.shape
    nb = np.asarray(inp["x_sample"]).shape[0]
    ncores = 8
    NP = SEQ
    NPT = NP // 512
    NS = nb // ncores
    seg_x, seg_prev, samp = [], [], []
    zeros = np.zeros((NP, D), np.float32)
    for c in range(ncores):
        seg_x.append(xp[c] if c < B else zeros)
        seg_prev.append(np.zeros(D, np.float32))
        samp.append(list(range(c * NS, (c + 1) * NS)))
    maps = make_in_maps(inp, seg_x, seg_prev, samp)
    nc = _program(("full", NPT, NS), NPT=NPT, NS=NS)
    res = run_bass_kernel_spmd(nc, maps, core_ids=list(range(ncores))).results
    y_p = np.zeros((B, SEQ, D), np.float32)
    y_s = np.zeros((nb, 64, D), np.float32)
    tsh_p = np.zeros((NL, B, DSH), np.float32)
    wkv_p = np.zeros((NL, B, 16, 64, 64), np.float32)
    tsh_s = np.zeros((NL, nb, DSH), np.float32)
    wkv_s = np.zeros((NL, nb, 16, 64, 64), np.float32)
    vn_s = np.zeros((NL, nb, 64, DA), np.float32)
    for c in range(ncores):
        r = res[c]
        yT = r["yT"]
        ids = samp[c]
        y_s[ids] = yT[:, NP:].T.reshape(NS, 64, D)
        tso = _tsh_unlayout(r["tsh_o"])
        wko = _wkv_unlayout(r["wkv_o"])
        tsh_s[:, ids] = tso[:, 1:]
        wkv_s[:, ids] = wko[:, 1:]
        vn_s[:, ids] = r["vn_o"].reshape(NL, NS, 64, DA)
        if c < B:
            y_p[c] = yT[:, 0:NP].T
            tsh_p[:, c] = tso[:, 0]
            wkv_p[:, c] = wko[:, 0]
    return (y_p, y_s, tsh_p, wkv_p, tsh_s, wkv_s, vn_s)
```

```python
import contextlib
import types
import numpy as np
import concourse.bass as bass
import concourse.mybir as mybir
from concourse.bass_utils import run_bass_kernel_spmd

F32 = mybir.dt.float32
BF16 = mybir.dt.bfloat16
ALU = mybir.AluOpType
AF = mybir.ActivationFunctionType

D = 2048
DA = 1024
DB = 1024
DIN = 5408
DFF = 8192
NL = 2
PC0 = 2048
DSH = 3360
NPCH = 28
C0 = 0.6065306597126334
RMS_EPS = 1e-5
LN_EPS = 1e-5
GN_EPS = 64e-5
ENGS = ("pe", "act", "dve", "pool", "sp")


class Res:
    __slots__ = ("name", "w", "r", "rd")

    def __init__(self, name=""):
        self.name = name
        self.w = None
        self.r = {}
        self.rd = []


class Op:
    __slots__ = ("eng", "fn", "deps", "dma", "sig", "sigval", "dsem", "dval", "idx")


def _freeze(fn):
    if fn is None or fn.__closure__ is None:
        return fn
    cells = []
    for c in fn.__closure__:
        try:
            cells.append(types.CellType(c.cell_contents))
        except ValueError:
            cells.append(c)
    return types.FunctionType(fn.__code__, fn.__globals__, fn.__name__, fn.__defaults__, tuple(cells))


class Sched:
    def __init__(self, nc, n_dma_sems=20):
        self.nc = nc
        self.ops = []
        self.n_dma_sems = n_dma_sems
        self.dma_count = {e: 0 for e in ENGS}
        self.last = {}
        self.last_dma = {}
        self.cap = None

    def capture(self, gen):
        old = self.cap
        self.cap = []
        gen()
        out = self.cap
        self.cap = old
        return out

    def replay(self, *streams):
        streams = [st_ for st_ in streams if st_]
        pos = [0] * len(streams)
        tot = sum(len(st_) for st_ in streams)
        for _ in range(tot):
            best, bi = None, 0
            for i, st_ in enumerate(streams):
                if pos[i] < len(st_):
                    frac = pos[i] / len(st_)
                    if best is None or frac < best:
                        best, bi = frac, i
            a = streams[bi][pos[bi]]
            pos[bi] += 1
            self.op(*a[:2], reads=a[2], writes=a[3], dma=a[4], extra_deps=a[5], frozen=True)

    def op(self, eng, fn, reads=(), writes=(), dma=False, extra_deps=(), frozen=False):
        if not frozen:
            fn = _freeze(fn)
        if self.cap is not None:
            self.cap.append((eng, fn, list(reads), list(writes), dma, tuple(extra_deps)))
            return None
        o = Op()
        o.eng, o.fn, o.dma = eng, fn, dma
        o.idx = len(self.ops)
        o.sig = False
        o.sigval = 0
        deps = set(extra_deps)
        for r in reads:
            if r.w is not None:
                deps.add(r.w)
        for r in writes:
            if r.w is not None:
                deps.add(r.w)
            deps.update(r.r.values())
            deps.update(r.rd)
        ops = self.ops
        if eng == "pe" and not dma:
            deps = {d for d in deps if not (ops[d].eng == "pe" and not ops[d].dma)}
        o.deps = deps
        for r in reads:
            if dma:
                r.rd.append(o.idx)
            else:
                r.r[eng] = o.idx
        for r in writes:
            r.w = o.idx
            r.r = {}
            r.rd = []
        if dma:
            n = self.dma_count[eng]
            self.dma_count[eng] = n + 1
            o.dsem = (eng, n % self.n_dma_sems)
            o.dval = 16 * (n // self.n_dma_sems + 1)
            if o.dsem in self.last_dma:
                o.deps.add(self.last_dma[o.dsem])
            self.last_dma[o.dsem] = o.idx
        else:
            self.last[eng] = o.idx
        self.ops.append(o)
        return o

    def barrier(self, engines=("pe", "act", "dve", "pool")):
        deps = set(self.last.values()) | set(self.last_dma.values())
        for e in engines:
            self.op(e, None, extra_deps=deps)

    def emit(self, final_wait_eng="sp"):
        nc = self.nc
        ops = self.ops
        for o in ops:
            for d in o.deps:
                if not ops[d].dma:
                    ops[d].sig = True
        cnt = {e: 0 for e in ENGS}
        for o in ops:
            if o.sig and not o.dma:
                if o.fn is None:
                    o.sig = False
                    continue
                cnt[o.eng] += 1
                o.sigval = cnt[o.eng]
        last_dma = {}
        for o in ops:
            if o.dma:
                last_dma[o.dsem] = max(last_dma.get(o.dsem, 0), o.dval)
        with contextlib.ExitStack() as st:
            esem = {e: st.enter_context(nc.semaphore("s_" + e)) for e in ENGS if e != "sp"}
            dsem = {}
            for e in ENGS:
                for i in range(min(self.n_dma_sems, self.dma_count[e])):
                    dsem[(e, i)] = st.enter_context(nc.semaphore("d_%s%d" % (e, i)))
            block = st.enter_context(nc.Block())
            per_eng = {e: [o for o in ops if o.eng == e] for e in ENGS}

            def run(e, engobj):
                frontier = {}
                for o in per_eng[e]:
                    need = {}
                    for d in o.deps:
                        p = ops[d]
                        if p.dma:
                            key, val = ("d",) + p.dsem, p.dval
                        else:
                            if p.fn is None:
                                continue
                            key, val = ("e", p.eng), p.sigval
                        if val > need.get(key, 0):
                            need[key] = val
                    for key, val in need.items():
                        if frontier.get(key, 0) >= val:
                            continue
                        frontier[key] = val
                        s = dsem[key[1:]] if key[0] == "d" else esem[key[1]]
                        engobj.wait_ge(s, val)
                    if o.fn is None:
                        continue
                    ins = o.fn(engobj)
                    if o.dma:
                        ins.then_inc(dsem[o.dsem], 16)
                    elif o.sig:
                        ins.then_inc(esem[o.eng], 1)
                if e == final_wait_eng:
                    for key, val in last_dma.items():
                        engobj.wait_ge(dsem[key], val)

            @block.tensor
            def _(eng):
                run("pe", eng)

            @block.scalar
            def _(eng):
                run("act", eng)

            @block.vector
            def _(eng):
                run("dve", eng)

            @block.gpsimd
            def _(eng):
                run("pool", eng)

            @block.sync
            def _(eng):
                run("sp", eng)


class Buf:
    def __init__(self, t, name=""):
        self.t = t
        self.r = Res(name)


def _consts():
    p = np.arange(128)[:, None]
    c = np.arange(128)[None, :]
    same = (p // 64) == (c // 64)
    ident = (p == c).astype(np.float32)
    J = (c == (p + 64) % 128).astype(np.float32)
    su = (same & ((p % 64) < (c % 64))).astype(np.float32)
    ui = (same & ((p % 64) <= (c % 64))).astype(np.float32)
    sl = (same & ((p % 64) > (c % 64))).astype(np.float32)
    mgram = np.concatenate([su, ui, su, ui], axis=1)
    bones = same.astype(np.float32)
    triu = (p <= c).astype(np.float32)
    cm = np.ones((128, 512), np.float32)
    cm[:, ::64] = 0.0
    ones = np.ones((128, 128), np.float32)
    parts = [("ident", ident), ("J", J), ("mgram", mgram), ("msl", sl), ("bones", bones),
             ("triu", triu), ("cmask", cm), ("ones", ones)]
    off = {}
    o = 0
    for k, v in parts:
        off[k] = (o, v.shape[1])
        o += v.shape[1]
    return np.concatenate([v for _, v in parts], axis=1), off


CST, CST_OFF = _consts()

PV = {}
_o = 0
for _l in range(NL):
    for _nm, _n in (("norm1", 16), ("norm2", 16), ("mu", NPCH), ("w0", 8), ("a0", 8), ("k_k", 8),
                    ("k_a", 8), ("r_k", 8), ("gn_g", 8), ("gn_b", 8)):
        PV[(_nm, _l)] = _o
        _o += _n
PV[("norm_f", 0)] = _o
_o += 16
NPV = _o

LORA = [(24, 3072, 64), (25, 3136, 64), (26, 3200, 128), (27, 3328, 32)]


def _blocks():
    bl = []
    bl.append(("A0", 16, 288, [("w_in", 0, PC0 + 3072, 288, 0)]))
    for hp in range(8):
        bl.append(("Ahp%d" % hp, 16, 384, [("w_in", 0, PC0 + j * 1024 + hp * 128, 128, j * 128) for j in range(3)]))
    for j in range(2):
        bl.append(("Cv%d" % j, 16, 512, [("w_in", 0, 1024 + 512 * j, 512, 0)]))
    for j in range(2):
        bl.append(("Cu%d" % j, 16, 512, [("w_in", 0, 512 * j, 512, 0)]))
    for mg in range(8):
        bl.append(("G%d" % mg, 16, 512, [("w_gate", 0, 256 * mg, 256, 0), ("w_gate", 0, 2048 + 256 * mg, 256, 256)]))
        bl.append(("P%d" % mg, 8, 512, [("w_pa", 0, 256 * mg, 256, 0), ("w_pb", 0, 256 * mg, 256, 256)]))
    for j in range(4):
        bl.append(("O%d" % j, 16, 512, [("w_o", 0, 512 * j, 512, 0)]))
    for q in range(4):
        for j in range(4):
            bl.append(("U%d_%d" % (q, j), 16, 512, [("w_up", 0, 2048 * q + 512 * j, 512, 0)]))
        for j in range(4):
            bl.append(("D%d_%d" % (q, j), 16, 512, [("w_down", 2048 * q, 512 * j, 512, 0)]))
    return bl


BLOCKS = _blocks()
WSHAPES = {"w_in": (D, DIN), "w_gate": (D, 2 * D), "w_pa": (DA, D), "w_pb": (DB, D), "w_o": (D, D),
           "w_up": (D, DFF), "w_down": (DFF, D)}


class Tile:
    def __init__(self, c0, w, nseq, kind, first, last, idx):
        self.c0, self.w, self.nseq, self.kind = c0, w, nseq, kind
        self.Lq = w // nseq
        self.first, self.last, self.idx = first, last, idx
        self.nch = w // 64


def build_program(NPT=4, NS=4, GS=4, dbg=False, n_layers=NL):
    NP = NPT * 512
    NSW = NS * 64
    NTOK = NP + NSW
    tiles = [Tile(512 * i, 512, 1, "P", i == 0, i == NPT - 1, i) for i in range(NPT)]
    if NS:
        tiles.append(Tile(NP, NSW, NS, "S", False, False, NPT))
    NT = len(tiles)
    nc = bass.Bass("TRN2", target_bir_lowering=False)

    def din(name, shape, dt=F32):
        return nc.dram_tensor(name, list(shape), dt, kind="ExternalInput").ap()

    def dout(name, shape, dt=F32):
        return nc.dram_tensor(name, list(shape), dt, kind="ExternalOutput").ap()

    def dscr(name, shape, dt=F32):
        return nc.dram_tensor(name, list(shape), dt, kind="Internal").ap()

    xT_in = din("xT", [D, NTOK])
    xprev_in = din("xprev", [128, 16])
    pvec_in = din("pvec", [128, NPV])
    cst_in = din("cst", [128, CST.shape[1]])
    lng_in = din("ln_v_g", [NL, DA])
    lnb_in = din("ln_v_b", [NL, DA])
    bs_in = din("b_s", [NL, 1, 1024])
    wsT_in = din("w_sT", [NL, 128, 8, 128])
    w2_in = din("w2", [NL, 64, DB])
    a2_in = din("a2", [NL, 64, DB])
    g2_in = din("g2", [NL, 160, DB])
    tsh_in = din("tsh_s", [NL, 128, NPCH, max(NS, 1)])
    wkv_in = din("wkv_s", [NL, max(NS, 1), 8, 128, 128])
    W_in = {k: din(k, [NL] + list(v)) for k, v in WSHAPES.items()}

    yT_out = dout("yT", [D, NTOK])
    tsh_out = dout("tsh_o", [NL, 128, NPCH, 1 + max(NS, 1)])
    wkv_out = dout("wkv_o", [NL, 1 + max(NS, 1), 8, 128, 128])
    vn_out = dout("vn_o", [NL, max(NSW, 128) // 128, 128, DA])
    dbg_out = dout("dbg", [32, 128, 512]) if dbg else None

    x1_scr = dscr("x1_scr", [D, NTOK])
    o_sp = dscr("o_sp", [NT, 8, 128, 512])
    od_sp = dscr("od_sp", [NT, 8, 128, 1024], BF16)
    bg_sp = dscr("bg_sp", [NT, 8, 128, 512])
    g_sp = dscr("g_sp", [NT, 8, 128, 512])
    wscr = {}
    for l in range(n_layers):
        for (nm, kc, ncol, pieces) in BLOCKS:
            wscr[(l, nm)] = dscr("ws_%d_%s" % (l, nm), [128, kc, ncol], BF16)

    st = contextlib.ExitStack()
    with st:
        S = Sched(nc)
        BLK = {b[0]: b for b in BLOCKS}

        ARENA_BYTES = 130 * 1024
        arena = st.enter_context(nc.sbuf_tensor("arena", [128, ARENA_BYTES // 4], F32))
        aoff = {"A": 0, "C": 0}

        def sb(name, shape, dt=F32, reg="S"):
            shape = list(shape)
            if reg == "S":
                return Buf(st.enter_context(nc.sbuf_tensor("sb_" + name, shape, dt)), name)
            esz = 4 if dt == F32 else 2
            n = 1
            for d_ in shape[1:]:
                n *= d_
            nbytes = (n * esz + 31) // 32 * 32
            o = aoff[reg]
            aoff[reg] = o + nbytes
            assert aoff[reg] <= ARENA_BYTES, (name, reg, aoff[reg])
            ap = arena[0:shape[0], o // 4:(o + nbytes) // 4]
            if dt != F32:
                ap = ap.bitcast(dt)
            ap = ap[:, 0:n]
            if len(shape) == 3:
                ap = ap.rearrange("p (a b) -> p a b", a=shape[1])
            b = Buf(ap, name)
            b.rows = shape[0]
            return b

        def ps(name):
            return Buf(st.enter_context(nc.psum_tensor(name, [128, 512], F32)), name)

        PSB = [ps("ps%d" % i) for i in range(8)]

        pvec = sb("pvec", [128, NPV])
        omk = sb("omk", [128, NL * 8])
        cstb = sb("cstb", [128, CST.shape[1]], BF16)
        cst = cstb

        def cb(name):
            o, n = CST_OFF[name]
            return cstb.t[:, o:o + n]

        cf = cb
        S.op("act", lambda e: e.dma_start(out=pvec.t[:], in_=pvec_in), writes=[pvec.r], dma=True)
        S.op("pool", lambda e: e.dma_start(out=cstb.t[:], in_=cst_in), writes=[cstb.r], dma=True)
        for l in range(NL):
            ka = PV[("k_a", l)]
            S.op("dve", lambda e, l=l, ka=ka: e.tensor_scalar(out=omk.t[:, l * 8:(l + 1) * 8], in0=pvec.t[:, ka:ka + 8],
                                                              scalar1=-1.0, scalar2=1.0, op0=ALU.mult, op1=ALU.add),
                 reads=[pvec.r], writes=[omk.r])

        def pv(name, l, c=0, rows=128):
            o = PV[(name, l)] + c
            return pvec.t[0:rows, o:o + 1]

        wres = {}
        conv_order = [(l, b[0]) for l in range(n_layers) for b in BLOCKS]
        conv_pos = [0]

        def conv_next():
            if conv_pos[0] >= len(conv_order):
                return
            l, nm = conv_order[conv_pos[0]]
            conv_pos[0] += 1
            _, kc, ncol, pieces = BLK[nm]
            rl = []
            for (wn, r0, c0, n, d0) in pieces:
                src = W_in[wn][l, r0:r0 + kc * 128, c0:c0 + n].rearrange("(k p) c -> p k c", p=128)
                dst = wscr[(l, nm)][:, :, d0:d0 + n]
                r = Res("cv")
                S.op("pool", lambda e, s=src, d=dst: e.dma_start(out=d, in_=s), writes=[r], dma=True)
                rl.append(r)
            wres[(l, nm)] = rl

        NSLOT = 3
        slots = [sb("wslot%d" % i, [128, 16 * 512], BF16) for i in range(NSLOT)]
        wctr = [0]

        def wget(l, nm):
            _, kc, ncol, _ = BLK[nm]
            while (l, nm) not in wres:
                conv_next()
            conv_next()
            sl = slots[wctr[0] % NSLOT]
            wctr[0] += 1
            view = sl.t[:, 0:kc * ncol].rearrange("p (k c) -> p k c", k=kc)
            src = wscr[(l, nm)]
            S.op("sp", lambda e, v=view, s=src: e.dma_start(out=v, in_=s), reads=wres[(l, nm)], writes=[sl.r], dma=True)
            return view, sl.r

        xT = sb("xT", [128, 16, 512], reg="C")
        xTr = [Res("xT%d" % c) for c in range(16)]
        xstg = [sb("xstg%d" % i, [128, 2, 512], reg="A") for i in range(2)]
        hT = sb("hT", [128, 16, 512], BF16)
        hprev = sb("hprev", [128, 16], BF16)
        xprev = sb("xprev", [128, 40])
        sq = [sb("sq%d" % i, [128, 512], BF16) for i in range(4)]
        rstd = sb("rstd", [128, 512])
        gctr = [0]
        gbanks = [list(range(0, 3)), list(range(0, 8))]
        phase = [0]

        def gbank():
            bl = gbanks[phase[0]]
            b = PSB[bl[gctr[0] % len(bl)]]
            gctr[0] += 1
            return b

        ectr = [0]

        def evac_eng():
            ectr[0] += 1
            return "act" if ectr[0] % 2 else "dve"

        def copy_op(eng, out, in_, reads, writes):
            if eng == "act":
                S.op("act", lambda e: e.activation(out=out, in_=in_, func=AF.Copy), reads=reads, writes=writes)
            else:
                S.op(eng, lambda e: e.tensor_copy(out=out, in_=in_), reads=reads, writes=writes)

        dbg_i = [0]
        dbgbufs = []

        def dump(buf, ap, rows=128, cols=512):
            if not dbg:
                return
            i = dbg_i[0]
            dbg_i[0] += 1
            if not dbgbufs:
                dbgbufs.extend([sb("dbgt%d" % j, [128, 512]) for j in range(2)])
            tmp = dbgbufs[i % 2]
            S.op("dve", lambda e: e.memset(tmp.t[:], 0.0), writes=[tmp.r])
            S.op("dve", lambda e: e.tensor_copy(out=tmp.t[0:rows, 0:cols], in_=ap), reads=[buf.r], writes=[tmp.r])
            S.op("act", lambda e: e.dma_start(out=dbg_out[i], in_=tmp.t[:]), reads=[tmp.r], dma=True)

        def load_x(l, T):
            src = (xT_in if l == 0 else x1_scr).rearrange("(c p) t -> p c t", p=128)
            for c in range(16):
                rd = [] if l == 0 else [x1res[(T.idx, c)]]
                S.op("act", lambda e, c=c: e.dma_start(out=xT.t[:, c, 0:T.w], in_=src[:, c, T.c0:T.c0 + T.w]), reads=rd, writes=[xTr[c]], dma=True)

        def rmsnorm(xres, xap_fn, w, gname, l, out_fn, ores):
            pb = gbank()
            for c in range(16):
                s = sq[c % 4]
                S.op("act", lambda e, c=c, s=s: e.activation(out=s.t[:, 0:w], in_=xap_fn(c), func=AF.Square),
                     reads=[xres[c]], writes=[s.r])
                S.op("pe", lambda e, c=c, s=s: e.matmul(pb.t[:, 0:w], cb("ones"), s.t[:, 0:w], start=(c == 0), stop=(c == 15)),
                     reads=[s.r, cstb.r], writes=[pb.r])
            S.op("dve", lambda e: e.tensor_scalar(out=rstd.t[:, 0:w], in0=pb.t[:, 0:w], scalar1=1.0 / D, scalar2=RMS_EPS,
                                                  op0=ALU.mult, op1=ALU.add), reads=[pb.r], writes=[rstd.r])
            S.op("act", lambda e: e.activation(out=rstd.t[:, 0:w], in_=rstd.t[:, 0:w], func=AF.Sqrt), reads=[rstd.r], writes=[rstd.r])
            S.op("dve", lambda e: e.reciprocal(out=rstd.t[:, 0:w], in_=rstd.t[:, 0:w]), reads=[rstd.r], writes=[rstd.r])
            for c in range(16):
                S.op("dve", lambda e, c=c: e.scalar_tensor_tensor(out=out_fn(c), in0=xap_fn(c), scalar=pv(gname, l, c),
                                                                  in1=rstd.t[:, 0:w], op0=ALU.mult, op1=ALU.mult),
                     reads=[xres[c], rstd.r, pvec.r], writes=[ores[c]])

        x1res = {}
        hTr16 = None

        A = {}
        for nm in ("sw", "aa", "gg", "cs", "cse", "e1", "e2", "e3", "bg", "osp"):
            A[nm] = sb("A_" + nm, [128, 512], reg="A")
        A["tmp"] = A["cse"]
        A["kk"] = A["sw"]
        A["t1"] = A["cs"]
        A["kkn"] = A["e1"]
        A["kh"] = A["e2"]
        A["beta"] = A["e3"]
        A["sqk"] = sb("A_sqk", [128, 512], BF16, reg="A")
        A["rkk"] = sb("A_rkk", [128, 512], BF16, reg="A")
        RKV = [[sb("A_rkv%d_%d" % (j, i), [128, 512], reg="A") for i in range(2)] for j in range(3)]
        pbufs = [sb("pbuf%d" % i, [128, 520], reg="A") for i in range(2)]
        dtmp = [sb("dtmp%d" % i, [128, 512], reg="A") for i in range(2)]
        ltmp = sb("ltmp", [128, 512], reg="A")
        txw = sb("txw", [64, 512], BF16, reg="A")
        xab = sb("xab", [64, 512], BF16, reg="A")
        sxg0 = sb("sxg0", [128, 512], BF16, reg="A")
        sxg1 = sb("sxg1", [32, 512], BF16, reg="A")
        plast = sb("plast", [128, NPCH, 1 + max(NS, 1)], reg="A")
        tshs = sb("tshs", [128, NPCH, max(NS, 1)], reg="A")
        w2b = sb("w2b", [64, DB], BF16, reg="A")
        a2b = sb("a2b", [64, DB], BF16, reg="A")
        g2b0 = sb("g2b0", [128, DB], BF16, reg="A")
        g2b1 = sb("g2b1", [32, DB], BF16, reg="A")
        eBD = [sb("eBD%d" % i, [128, 1024], BF16, reg="A") for i in range(3)]
        BDS = []
        for i in range(2):
            BDS.append(dict(KR=sb("KR%d" % i, [128, 8, 256], BF16, reg="A"), KkD=sb("KkD%d" % i, [128, 8, 128], BF16, reg="A"),
                            BtD=sb("BtD%d" % i, [128, 8, 128], BF16, reg="A"), KhD=sb("KhD%d" % i, [128, 8, 128], BF16, reg="A"),
                            BhD=sb("BhD%d" % i, [128, 8, 128], BF16, reg="A"), VD=sb("VD%d" % i, [128, 8, 128], BF16, reg="A"),
                            WC=sb("WC%d" % i, [128, 8], reg="A")))
        GC = 4
        Gm = sb("Gm", [128, GC, 512], BF16, reg="A")
        NTb = sb("NTb", [128, GC, 128], BF16, reg="A")
        ABb = [sb("AB%d" % i, [128, GC, 256], BF16, reg="A") for i in range(2)]
        Pb = [sb("Pb%d" % i, [128, GC, 128], BF16, reg="A") for i in range(2)]
        VDT = sb("VDT", [128, GC, 128], BF16, reg="A")
        KhT = sb("KhT", [128, GC, 128], BF16, reg="A")
        BhT = sb("BhT", [128, GC, 128], BF16, reg="A")
        ZT = sb("ZT", [128, 128], BF16, reg="A")
        nUT = sb("nUT", [128, 128], BF16, reg="A")
        S0bf = [sb("S0bf%d" % i, [128, 128], BF16, reg="A") for i in range(2)]
        ST = [sb("ST%d" % hp, [128, 128], reg="A") for hp in range(8)]
        stg = [sb("stg%d" % i, [128, 128], reg="A") for i in range(2)]
        odb = sb("odb", [128, 8, 128], BF16, reg="A")
        s0ctr = [0]
        stgctr = [0]

        def layer_params(l):
            S.op("pool", lambda e: e.dma_start(out=w2b.t[:], in_=w2_in[l]), writes=[w2b.r], dma=True)
            S.op("pool", lambda e: e.dma_start(out=a2b.t[:], in_=a2_in[l]), writes=[a2b.r], dma=True)
            S.op("pool", lambda e: e.dma_start(out=g2b0.t[:], in_=g2_in[l, 0:128, :]), writes=[g2b0.r], dma=True)
            S.op("pool", lambda e: e.dma_start(out=g2b1.t[:], in_=g2_in[l, 128:160, :]), writes=[g2b1.r], dma=True)
            if NS:
                S.op("act", lambda e: e.dma_start(out=tshs.t[:], in_=tsh_in[l]), writes=[tshs.r], dma=True)

        pctr = [0]

        def p_chunk(l, T, blk, blkres, off, rows, cid, out_fn, out_res, post=None):
            w, Lq, nseq = T.w, T.Lq, T.nseq
            pb = gbank()
            for kc in range(16):
                S.op("pe", lambda e, kc=kc: e.matmul(pb.t[0:rows, 0:w], blk[:, kc, off:off + rows], hT.t[:, kc, 0:w],
                                                     start=(kc == 0), stop=(kc == 15)),
                     reads=[blkres, hT.r], writes=[pb.r])
            P = pbufs[pctr[0] % 2]
            dt_ = dtmp[pctr[0] % 2]
            pctr[0] += 1
            pv3 = P.t[0:rows, 0:nseq * (Lq + 1)].rearrange("p (s l) -> p s l", s=nseq)
            S.op("act", lambda e: e.activation(out=pv3[:, :, 1:Lq + 1], in_=pb.t[0:rows, 0:w].rearrange("p (s l) -> p s l", s=nseq),
                                               func=AF.Copy), reads=[pb.r], writes=[P.r])
            if T.kind == "P":
                if T.first:
                    ph = PSB[3]
                    for kc in range(16):
                        S.op("pe", lambda e, kc=kc: e.matmul(ph.t[0:rows, cid:cid + 1], blk[:, kc, off:off + rows], hprev.t[:, kc:kc + 1],
                                                             start=(kc == 0), stop=(kc == 15)),
                             reads=[blkres, hprev.r], writes=[ph.r])
                    S.op("act", lambda e: e.activation(out=pv3[:, 0, 0:1], in_=ph.t[0:rows, cid:cid + 1], func=AF.Copy),
                         reads=[ph.r], writes=[P.r])
                else:
                    S.op("act", lambda e: e.activation(out=pv3[:, 0, 0:1], in_=plast.t[0:rows, cid, 0:1], func=AF.Copy),
                         reads=[plast.r], writes=[P.r])
                S.op("act", lambda e: e.activation(out=plast.t[0:rows, cid, 0:1], in_=pv3[:, 0, Lq:Lq + 1], func=AF.Copy),
                     reads=[P.r], writes=[plast.r])
            else:
                S.op("act", lambda e: e.activation(out=pv3[:, :, 0], in_=tshs.t[0:rows, cid, 0:nseq], func=AF.Copy),
                     reads=[tshs.r], writes=[P.r])
                S.op("act", lambda e: e.activation(out=plast.t[0:rows, cid, 1:1 + nseq], in_=pv3[:, :, Lq], func=AF.Copy),
                     reads=[P.r], writes=[plast.r])
            d3 = dt_.t[0:rows, 0:w].rearrange("p (s l) -> p s l", s=nseq)
            S.op("dve", lambda e: e.tensor_tensor(out=d3, in0=pv3[:, :, 0:Lq], in1=pv3[:, :, 1:Lq + 1], op=ALU.subtract),
                 reads=[P.r], writes=[dt_.r])
            if post is None:
                o3 = out_fn().rearrange("p (s l) -> p s l", s=nseq)
                S.op("dve", lambda e: e.scalar_tensor_tensor(out=o3, in0=d3, scalar=pv("mu", l, cid, rows), in1=pv3[:, :, 1:Lq + 1],
                                                             op0=ALU.mult, op1=ALU.add),
                     reads=[dt_.r, P.r, pvec.r], writes=[out_res])
            else:
                l3 = ltmp.t[0:rows, 0:w].rearrange("p (s l) -> p s l", s=nseq)
                S.op("dve", lambda e: e.scalar_tensor_tensor(out=l3, in0=d3, scalar=pv("mu", l, cid, rows), in1=pv3[:, :, 1:Lq + 1],
                                                             op0=ALU.mult, op1=ALU.add),
                     reads=[dt_.r, P.r, pvec.r], writes=[ltmp.r])
                post(ltmp.t[0:rows, 0:w])

        def bd4(t, w):
            nch = w // 64
            return t[:, 0:w].rearrange("p (c j) -> p c j", j=64).unsqueeze(2).broadcast_to([128, nch, 2, 64])

        def prep(l, T, hp, rb, kb, vb):
            w, nch = T.w, T.nch
            bs_ = BDS[hp % 2]
            KR, KkD, BtD, KhD, BhD, VD, WC = (bs_[k_] for k_ in ("KR", "KkD", "BtD", "KhD", "BhD", "VD", "WC"))
            hc = slice(hp * 128, (hp + 1) * 128)
            a = A
            pb = gbank()
            S.op("pe", lambda e: e.matmul(pb.t[:, 0:w], w2b.t[0:64, hc], txw.t[0:64, 0:w], start=True, stop=True),
                 reads=[w2b.r, txw.r], writes=[pb.r])
            S.op("act", lambda e: e.activation(out=a["sw"].t[:, 0:w], in_=pb.t[:, 0:w], func=AF.Sigmoid, bias=pv("w0", l, hp)),
                 reads=[pb.r, pvec.r], writes=[a["sw"].r])
            pb2 = gbank()
            S.op("pe", lambda e: e.matmul(pb2.t[:, 0:w], a2b.t[0:64, hc], xab.t[0:64, 0:w], start=True, stop=True),
                 reads=[a2b.r, xab.r], writes=[pb2.r])
            S.op("act", lambda e: e.activation(out=a["aa"].t[:, 0:w], in_=pb2.t[:, 0:w], func=AF.Sigmoid, bias=pv("a0", l, hp)),
                 reads=[pb2.r, pvec.r], writes=[a["aa"].r])
            pb3 = gbank()
            S.op("pe", lambda e: e.matmul(pb3.t[:, 0:w], g2b0.t[:, hc], sxg0.t[:, 0:w], start=True, stop=False),
                 reads=[g2b0.r, sxg0.r], writes=[pb3.r])
            S.op("pe", lambda e: e.matmul(pb3.t[:, 0:w], g2b1.t[0:32, hc], sxg1.t[0:32, 0:w], start=False, stop=True),
                 reads=[g2b1.r, sxg1.r], writes=[pb3.r])
            S.op("act", lambda e: e.activation(out=a["gg"].t[:, 0:w], in_=pb3.t[:, 0:w], func=AF.Copy),
                 reads=[pb3.r], writes=[a["gg"].r])
            S.op("dve", lambda e: e.tensor_tensor_scan(out=a["cs"].t[:, 0:w], data0=cf("cmask")[:, 0:w], data1=a["sw"].t[:, 0:w],
                                                       initial=0.0, op0=ALU.mult, op1=ALU.add),
                 reads=[a["sw"].r, cst.r], writes=[a["cs"].r])
            S.op("dve", lambda e: e.tensor_tensor(out=a["cse"].t[:, 0:w], in0=a["cs"].t[:, 0:w], in1=a["sw"].t[:, 0:w], op=ALU.subtract),
                 reads=[a["cs"].r, a["sw"].r], writes=[a["cse"].r])
            S.op("act", lambda e: e.activation(out=a["e1"].t[:, 0:w], in_=a["cs"].t[:, 0:w], func=AF.Exp, scale=-C0),
                 reads=[a["cs"].r], writes=[a["e1"].r])
            S.op("act", lambda e: e.activation(out=a["e2"].t[:, 0:w], in_=a["cs"].t[:, 0:w], func=AF.Exp, scale=C0),
                 reads=[a["cs"].r], writes=[a["e2"].r])
            S.op("act", lambda e: e.activation(out=a["e3"].t[:, 0:w], in_=a["cse"].t[:, 0:w], func=AF.Exp, scale=-C0),
                 reads=[a["cse"].r], writes=[a["e3"].r])
            S.op("dve", lambda e: e.tensor_copy(out=WC.t[:, 0:nch], in_=a["e1"].t[:, 0:w].rearrange("p (c j) -> p c j", j=64)[:, :, 63]),
                 reads=[a["e1"].r], writes=[WC.r])
            mbd = cf("bones").rearrange("p (h j) -> p h j", h=2).unsqueeze(1).broadcast_to([128, nch, 2, 64])

            def v4(buf, n=1024):
                return buf.t[:, 0:nch * 128].rearrange("p (c h j) -> p c h j", h=2, j=64)

            for i, src in enumerate(("e1", "e2", "e3")):
                eng = "pool" if i == 1 else "dve"
                S.op(eng, lambda e, i=i, src=src: e.tensor_tensor(out=v4(eBD[i]), in0=bd4(a[src].t, w), in1=mbd, op=ALU.mult),
                     reads=[a[src].r, cst.r], writes=[eBD[i].r])
            S.op("dve", lambda e: e.tensor_scalar(out=a["kk"].t[:, 0:w], in0=kb.t[:, 0:w], scalar1=pv("k_k", l, hp), scalar2=None, op0=ALU.mult),
                 reads=[kb.r, pvec.r], writes=[a["kk"].r])
            S.op("act", lambda e: e.activation(out=a["sqk"].t[:, 0:w], in_=a["kk"].t[:, 0:w], func=AF.Square),
                 reads=[a["kk"].r], writes=[a["sqk"].r])
            pb4 = gbank()
            S.op("pe", lambda e: e.matmul(pb4.t[:, 0:w], cb("bones"), a["sqk"].t[:, 0:w], start=True, stop=True),
                 reads=[cstb.r, a["sqk"].r], writes=[pb4.r])
            S.op("dve", lambda e: e.tensor_scalar(out=a["t1"].t[:, 0:w], in0=pb4.t[:, 0:w], scalar1=1e-24, scalar2=None, op0=ALU.max),
                 reads=[pb4.r], writes=[a["t1"].r])
            S.op("act", lambda e: e.activation(out=a["t1"].t[:, 0:w], in_=a["t1"].t[:, 0:w], func=AF.Sqrt), reads=[a["t1"].r], writes=[a["t1"].r])
            S.op("dve", lambda e: e.reciprocal(out=a["t1"].t[:, 0:w], in_=a["t1"].t[:, 0:w]), reads=[a["t1"].r], writes=[a["t1"].r])
            S.op("dve", lambda e: e.tensor_tensor(out=a["kkn"].t[:, 0:w], in0=a["kk"].t[:, 0:w], in1=a["t1"].t[:, 0:w], op=ALU.mult),
                 reads=[a["kk"].r, a["t1"].r], writes=[a["kkn"].r])
            S.op("dve", lambda e: e.tensor_scalar(out=a["tmp"].t[:, 0:w], in0=a["aa"].t[:, 0:w], scalar1=pv("k_a", l, hp),
                                                  scalar2=omk.t[:, l * 8 + hp:l * 8 + hp + 1], op0=ALU.mult, op1=ALU.add),
                 reads=[a["aa"].r, pvec.r, omk.r], writes=[a["tmp"].r])
            S.op("dve", lambda e: e.tensor_tensor(out=a["kh"].t[:, 0:w], in0=kb.t[:, 0:w], in1=a["tmp"].t[:, 0:w], op=ALU.mult),
                 reads=[kb.r, a["tmp"].r], writes=[a["kh"].r])
            S.op("dve", lambda e: e.tensor_tensor(out=a["beta"].t[:, 0:w], in0=a["kkn"].t[:, 0:w], in1=a["aa"].t[:, 0:w], op=ALU.mult),
                 reads=[a["kkn"].r, a["aa"].r], writes=[a["beta"].r])
            S.op("dve", lambda e: e.scalar_tensor_tensor(out=a["rkk"].t[:, 0:w], in0=rb.t[:, 0:w], scalar=pv("r_k", l, hp), in1=a["kh"].t[:, 0:w],
                                                         op0=ALU.mult, op1=ALU.mult),
                 reads=[rb.r, a["kh"].r, pvec.r], writes=[a["rkk"].r])
            pb5 = gbank()
            S.op("pe", lambda e: e.matmul(pb5.t[:, 0:w], cb("bones"), a["rkk"].t[:, 0:w], start=True, stop=True),
                 reads=[cstb.r, a["rkk"].r], writes=[pb5.r])
            S.op("dve", lambda e: e.tensor_tensor(out=a["bg"].t[:, 0:w], in0=pb5.t[:, 0:w], in1=vb.t[:, 0:w], op=ALU.mult),
                 reads=[pb5.r, vb.r], writes=[a["bg"].r])
            S.op("dve", lambda e: e.tensor_tensor(out=a["bg"].t[:, 0:w], in0=a["bg"].t[:, 0:w], in1=a["gg"].t[:, 0:w], op=ALU.mult),
                 reads=[a["bg"].r, a["gg"].r], writes=[a["bg"].r])
            if l == 0 and T.idx == 0 and hp == 0:
                for nm in ("sw", "aa", "cs", "e1", "e2", "e3", "kkn", "kh", "beta", "gg"):
                    dump(a[nm], a[nm].t[:, 0:w], 128, w)
                dump(rb, rb.t[:, 0:w], 128, w)
                dump(vb, vb.t[:, 0:w], 128, w)
            kr4 = KR.t[:, 0:nch, :].rearrange("p c (s h j) -> p c s h j", s=2, h=2)
            S.op("dve", lambda e: e.tensor_tensor(out=kr4[:, :, 1], in0=bd4(rb.t, w), in1=v4(eBD[0]), op=ALU.mult),
                 reads=[rb.r, eBD[0].r], writes=[KR.r])
            S.op("pool", lambda e: e.tensor_tensor(out=kr4[:, :, 0], in0=bd4(a["kkn"].t, w), in1=v4(eBD[2]), op=ALU.mult),
                 reads=[a["kkn"].r, eBD[2].r, KR.r], writes=[KR.r])

            def t4(buf):
                return buf.t[:, 0:nch, :].rearrange("p c (h j) -> p c h j", h=2)

            S.op("dve", lambda e: e.tensor_tensor(out=t4(KkD), in0=bd4(a["kh"].t, w), in1=v4(eBD[1]), op=ALU.mult),
                 reads=[a["kh"].r, eBD[1].r], writes=[KkD.r])
            S.op("pool", lambda e: e.tensor_tensor(out=t4(BtD), in0=bd4(a["beta"].t, w), in1=v4(eBD[1]), op=ALU.mult),
                 reads=[a["beta"].r, eBD[1].r], writes=[BtD.r])
            wcb = WC.t[:, 0:nch].unsqueeze(2).broadcast_to([128, nch, 128])
            S.op("dve", lambda e: e.tensor_tensor(out=KhD.t[:, 0:nch, :], in0=KkD.t[:, 0:nch, :], in1=wcb, op=ALU.mult),
                 reads=[KkD.r, WC.r], writes=[KhD.r])
            S.op("pool", lambda e: e.tensor_tensor(out=BhD.t[:, 0:nch, :], in0=BtD.t[:, 0:nch, :], in1=wcb, op=ALU.mult),
                 reads=[BtD.r, WC.r], writes=[BhD.r])
            S.op("pool", lambda e: e.tensor_tensor(out=t4(VD), in0=bd4(vb.t, w), in1=mbd, op=ALU.mult),
                 reads=[vb.r, cst.r], writes=[VD.r])

        def scan(l, T, hp):
            w, nch = T.w, T.nch
            bs_ = BDS[hp % 2]
            KR, KkD, BtD, KhD, BhD, VD, WC = (bs_[k_] for k_ in ("KR", "KkD", "BtD", "KhD", "BhD", "VD", "WC"))
            Q = PSB[4:8]
            st_ = ST[hp]
            for g0 in range(0, nch, GC):
                n = min(GC, nch - g0)
                for ci in range(n):
                    c = g0 + ci
                    q = Q[ci % 2]
                    S.op("pe", lambda e, c=c, q=q: e.matmul(q.t[:, 0:256], BtD.t[:, c, :], KR.t[:, c, :], start=True, stop=True),
                         reads=[BtD.r, KR.r], writes=[q.r])
                    S.op("pe", lambda e, c=c, q=q: e.matmul(q.t[:, 256:512], KkD.t[:, c, :], KR.t[:, c, :], start=True, stop=True),
                         reads=[KkD.r, KR.r], writes=[q.r])
                    S.op("dve", lambda e, ci=ci, q=q: e.tensor_tensor(out=Gm.t[:, ci, :], in0=q.t[:, :], in1=cb("mgram"), op=ALU.mult),
                         reads=[q.r, cstb.r], writes=[Gm.r])
                    S.op("pe", lambda e, c=c, ci=ci: e.matmul(Q[2].t[:, ci * 128:(ci + 1) * 128], KR.t[:, c, 0:128], BtD.t[:, c, :], start=True, stop=True),
                         reads=[BtD.r, KR.r], writes=[Q[2].r])
                mslb = cb("msl").unsqueeze(1).broadcast_to([128, n, 128])
                S.op("dve", lambda e: e.tensor_tensor(out=NTb.t[:, 0:n, :], in0=Q[2].t[:, 0:n * 128].rearrange("p (c j) -> p c j", j=128),
                                                      in1=mslb, op=ALU.mult), reads=[Q[2].r, cstb.r], writes=[NTb.r])
                idb = cb("ident").unsqueeze(1).broadcast_to([128, n, 128])
                S.op("pool", lambda e: e.tensor_tensor(out=Pb[0].t[:, 0:n, :], in0=idb, in1=Gm.t[:, 0:n, 0:128], op=ALU.subtract),
                     reads=[Gm.r, cstb.r], writes=[Pb[0].r])
                for k in range(1, 6):
                    ab_o = ABb[k % 2]
                    ab_i = ABb[(k - 1) % 2]
                    p_o, p_i = Pb[k % 2], Pb[(k - 1) % 2]
                    for ci in range(n):
                        q = Q[ci // 2]
                        co = (ci % 2) * 256
                        if k == 1:
                            Ai = Gm.t[:, ci, 0:128]
                            Bi = NTb.t[:, ci, :]
                            rdi = [Gm.r, NTb.r]
                        else:
                            Ai = ab_i.t[:, ci, 0:128]
                            Bi = ab_i.t[:, ci, 128:256]
                            rdi = [ab_i.r]
                        if k < 5:
                            S.op("pe", lambda e, q=q, co=co, Ai=Ai, Bi=Bi: e.matmul(q.t[:, co:co + 128], Bi, Ai, start=True, stop=True),
                                 reads=rdi, writes=[q.r])
                        S.op("pe", lambda e, q=q, co=co, Ai=Ai, Bi=Bi: e.matmul(q.t[:, co + 128:co + 256], Ai, Bi, start=True, stop=True),
                             reads=rdi, writes=[q.r])
                    for qi in range((n + 1) // 2):
                        m = min(2, n - 2 * qi)
                        eng = "act" if qi == 0 else "dve"
                        copy_op(eng, ab_o.t[:, 2 * qi:2 * qi + m, :], Q[qi].t[:, 0:m * 256].rearrange("p (c j) -> p c j", j=256),
                                [Q[qi].r], [ab_o.r])
                    for ci in range(n):
                        S.op("pe", lambda e, ci=ci: e.matmul(Q[2].t[:, ci * 128:(ci + 1) * 128], ab_o.t[:, ci, 128:256], p_i.t[:, ci, :],
                                                             start=True, stop=True),
                             reads=[ab_o.r, p_i.r], writes=[Q[2].r])
                    if l == 0 and T.idx == 0 and hp == 0 and g0 == 0 and k == 1:
                        dump(NTb, NTb.t[:, 0, :], 128, 128)
                        dump(ab_o, ab_o.t[:, 0, :], 128, 256)
                        dump(Q[2], Q[2].t[:, 0:128], 128, 128)
                        dump(p_i, p_i.t[:, 0, :], 128, 128)
                    S.op("dve", lambda e, p_o=p_o, p_i=p_i: e.tensor_tensor(out=p_o.t[:, 0:n, :], in0=Q[2].t[:, 0:n * 128].rearrange("p (c j) -> p c j", j=128),
                                                                            in1=p_i.t[:, 0:n, :], op=ALU.add),
                         reads=[Q[2].r, p_i.r], writes=[p_o.r])
                    if l == 0 and T.idx == 0 and hp == 0 and g0 == 0 and k == 1:
                        dump(p_o, p_o.t[:, 0, :], 128, 128)
                Tm = Pb[1]
                if l == 0 and T.idx == 0 and hp == 0 and g0 == 0:
                    dump(Gm, Gm.t[:, 0, :], 128, 512)
                    dump(Tm, Tm.t[:, 0, :], 128, 128)
                for (src, dst, q, eng) in ((VD, VDT, Q[0], "act"), (KhD, KhT, Q[1], "dve"), (BhD, BhT, Q[3], "act")):
                    for ci in range(n):
                        c = g0 + ci
                        S.op("pe", lambda e, src=src, q=q, c=c, ci=ci: e.matmul(q.t[:, ci * 128:(ci + 1) * 128], src.t[:, c, :], cb("ident"),
                                                                                start=True, stop=True),
                             reads=[src.r, cstb.r], writes=[q.r])
                    copy_op(eng, dst.t[:, 0:n, :], q.t[:, 0:n * 128].rearrange("p (c j) -> p c j", j=128), [q.r], [dst.r])
                for ci in range(n):
                    c = g0 + ci
                    if T.kind == "S" or (T.first and c == 0):
                        if T.kind == "S":
                            sg = stg[stgctr[0] % 2]
                            stgctr[0] += 1
                            S.op("act", lambda e, sg=sg, c=c: e.dma_start(out=sg.t[:], in_=wkv_in[l, c, hp]), writes=[sg.r], dma=True)
                            S.op("dve", lambda e, sg=sg: e.tensor_tensor(out=st_.t[:], in0=sg.t[:], in1=cf("J"), op=ALU.add),
                                 reads=[sg.r, cst.r], writes=[st_.r])
                        else:
                            S.op("dve", lambda e: e.tensor_copy(out=st_.t[:], in_=cf("J")), reads=[cst.r], writes=[st_.r])
                    s0 = S0bf[s0ctr[0] % 2]
                    s0ctr[0] += 1
                    S.op("act", lambda e, s0=s0: e.activation(out=s0.t[:], in_=st_.t[:], func=AF.Copy), reads=[st_.r], writes=[s0.r])
                    S.op("pe", lambda e, c=c, s0=s0: e.matmul(Q[2].t[:, 0:128], KR.t[:, c, 0:128], s0.t[:], start=True, stop=False),
                         reads=[KR.r, s0.r], writes=[Q[2].r])
                    S.op("pe", lambda e, ci=ci: e.matmul(Q[2].t[:, 0:128], Gm.t[:, ci, 256:384], VDT.t[:, ci, :], start=False, stop=True),
                         reads=[Gm.r, VDT.r], writes=[Q[2].r])
                    S.op("act", lambda e: e.activation(out=ZT.t[:], in_=Q[2].t[:, 0:128], func=AF.Copy), reads=[Q[2].r], writes=[ZT.r])
                    S.op("pe", lambda e, ci=ci: e.matmul(Q[2].t[:, 128:256], Tm.t[:, ci, :], ZT.t[:], start=True, stop=True),
                         reads=[Tm.r, ZT.r], writes=[Q[2].r])
                    S.op("dve", lambda e: e.tensor_scalar(out=nUT.t[:], in0=Q[2].t[:, 128:256], scalar1=-1.0, scalar2=None, op0=ALU.mult),
                         reads=[Q[2].r], writes=[nUT.r])
                    osl = slice(ci * 128, (ci + 1) * 128)
                    S.op("pe", lambda e, c=c, s0=s0, osl=osl: e.matmul(Q[3].t[:, osl], s0.t[:], KR.t[:, c, 128:256], start=True, stop=False),
                         reads=[KR.r, s0.r], writes=[Q[3].r])
                    S.op("pe", lambda e, ci=ci, osl=osl: e.matmul(Q[3].t[:, osl], VDT.t[:, ci, :], Gm.t[:, ci, 384:512], start=False, stop=False),
                         reads=[Gm.r, VDT.r], writes=[Q[3].r])
                    S.op("pe", lambda e, ci=ci, osl=osl: e.matmul(Q[3].t[:, osl], nUT.t[:], Gm.t[:, ci, 128:256], start=False, stop=True),
                         reads=[Gm.r, nUT.r], writes=[Q[3].r])
                    S.op("pe", lambda e, ci=ci: e.matmul(Q[2].t[:, 256:384], KhT.t[:, ci, :], VDT.t[:, ci, :], start=True, stop=False),
                         reads=[KhT.r, VDT.r], writes=[Q[2].r])
                    S.op("pe", lambda e, ci=ci: e.matmul(Q[2].t[:, 256:384], BhT.t[:, ci, :], nUT.t[:], start=False, stop=True),
                         reads=[BhT.r, nUT.r], writes=[Q[2].r])
                    S.op("dve", lambda e, c=c: e.scalar_tensor_tensor(out=st_.t[:], in0=st_.t[:], scalar=WC.t[:, c:c + 1], in1=Q[2].t[:, 256:384],
                                                                      op0=ALU.mult, op1=ALU.add),
                         reads=[st_.r, WC.r, Q[2].r], writes=[st_.r])
                    if l == 0 and T.idx == 0 and hp == 0 and c == 0:
                        dump(st_, st_.t[:], 128, 128)
                    if T.kind == "S":
                        S.op("act", lambda e, c=c: e.dma_start(out=wkv_out[l, 1 + c, hp], in_=st_.t[:]), reads=[st_.r], dma=True)
                    elif T.last and c == nch - 1:
                        S.op("act", lambda e: e.dma_start(out=wkv_out[l, 0, hp], in_=st_.t[:]), reads=[st_.r], dma=True)
                for h in range(2):
                    hs = slice(64 * h, 64 * h + 64)
                    src = Q[3].t[hs, 0:n * 128].rearrange("p (c j) -> p c j", j=128)[:, :, 64 * h:64 * h + 64]
                    dst = A["osp"].t[hs, g0 * 64:(g0 + n) * 64].rearrange("p (c j) -> p c j", j=64)
                    copy_op("act" if h == 0 else "dve", dst, src, [Q[3].r], [A["osp"].r])
                copy_op("dve", odb.t[:, g0:g0 + n, :], Q[3].t[:, 0:n * 128].rearrange("p (c j) -> p c j", j=128), [Q[3].r], [odb.r])

        def spill(T, hp, part):
            w, nch = T.w, T.nch
            i = T.idx
            if part == 0:
                S.op("act", lambda e: e.dma_start(out=bg_sp[i, hp, :, 0:w], in_=A["bg"].t[:, 0:w]), reads=[A["bg"].r], writes=[spres[(i, hp)]], dma=True)
                S.op("act", lambda e: e.dma_start(out=g_sp[i, hp, :, 0:w], in_=A["gg"].t[:, 0:w]), reads=[A["gg"].r], writes=[spres[(i, hp)]], dma=True)
            else:
                S.op("act", lambda e: e.dma_start(out=o_sp[i, hp, :, 0:w], in_=A["osp"].t[:, 0:w]), reads=[A["osp"].r], writes=[spres[(i, hp)]], dma=True)
                S.op("act", lambda e: e.dma_start(out=od_sp[i, hp, :, 0:nch * 128], in_=odb.t[:, 0:nch, :]), reads=[odb.r], writes=[spres[(i, hp)]], dma=True)

        spres = {}

        def phaseA(l, T):
            w = T.w
            xsrc = (xT_in if l == 0 else x1_scr).rearrange("(c p) t -> p c t", p=128)
            pbn = gbank()
            for ps_ in range(2):
                for gq in range(8):
                    xs_ = xstg[gq % 2]
                    rd = [] if l == 0 else [x1res[(T.idx, 2 * gq)], x1res[(T.idx, 2 * gq + 1)]]
                    S.op("act", lambda e, gq=gq, xs_=xs_: e.dma_start(out=xs_.t[:, :, 0:w], in_=xsrc[:, 2 * gq:2 * gq + 2, T.c0:T.c0 + w]),
                         reads=rd, writes=[xs_.r], dma=True)
                    for cc in range(2):
                        c = 2 * gq + cc
                        if ps_ == 0:
                            s_ = sq[c % 4]
                            S.op("act", lambda e, cc=cc, s_=s_, xs_=xs_: e.activation(out=s_.t[:, 0:w], in_=xs_.t[:, cc, 0:w], func=AF.Square),
                                 reads=[xs_.r], writes=[s_.r])
                            S.op("pe", lambda e, c=c, s_=s_: e.matmul(pbn.t[:, 0:w], cb("ones"), s_.t[:, 0:w], start=(c == 0), stop=(c == 15)),
                                 reads=[s_.r, cstb.r], writes=[pbn.r])
                        else:
                            S.op("dve", lambda e, c=c, cc=cc, xs_=xs_: e.scalar_tensor_tensor(out=hT.t[:, c, 0:w], in0=xs_.t[:, cc, 0:w], scalar=pv("norm1", l, c),
                                                                                             in1=rstd.t[:, 0:w], op0=ALU.mult, op1=ALU.mult),
                                 reads=[xs_.r, rstd.r, pvec.r], writes=[hT.r])
                if ps_ == 0:
                    S.op("dve", lambda e: e.tensor_scalar(out=rstd.t[:, 0:w], in0=pbn.t[:, 0:w], scalar1=1.0 / D, scalar2=RMS_EPS,
                                                          op0=ALU.mult, op1=ALU.add), reads=[pbn.r], writes=[rstd.r])
                    S.op("act", lambda e: e.activation(out=rstd.t[:, 0:w], in_=rstd.t[:, 0:w], func=AF.Sqrt), reads=[rstd.r], writes=[rstd.r])
                    S.op("dve", lambda e: e.reciprocal(out=rstd.t[:, 0:w], in_=rstd.t[:, 0:w]), reads=[rstd.r], writes=[rstd.r])
            if T.first:
                if l == 0:
                    S.op("act", lambda e: e.dma_start(out=xprev.t[:, 0:16], in_=xprev_in), writes=[xprev.r], dma=True)
                S.op("act", lambda e: e.activation(out=sq[0].t[:, 0:16], in_=xprev.t[:, 0:16], func=AF.Square), reads=[xprev.r], writes=[sq[0].r])
                pbk = gbank()
                S.op("pe", lambda e: e.matmul(pbk.t[:, 0:16], cb("ones"), sq[0].t[:, 0:16], start=True, stop=True), reads=[sq[0].r, cstb.r], writes=[pbk.r])
                S.op("dve", lambda e: e.reduce_sum(out=xprev.t[:, 16:17], in_=pbk.t[:, 0:16], axis=mybir.AxisListType.X), reads=[pbk.r], writes=[xprev.r])
                S.op("dve", lambda e: e.tensor_scalar(out=xprev.t[:, 16:17], in0=xprev.t[:, 16:17], scalar1=1.0 / D, scalar2=RMS_EPS, op0=ALU.mult, op1=ALU.add),
                     reads=[xprev.r], writes=[xprev.r])
                S.op("act", lambda e: e.activation(out=xprev.t[:, 16:17], in_=xprev.t[:, 16:17], func=AF.Sqrt), reads=[xprev.r], writes=[xprev.r])
                S.op("dve", lambda e: e.reciprocal(out=xprev.t[:, 16:17], in_=xprev.t[:, 16:17]), reads=[xprev.r], writes=[xprev.r])
                n1 = PV[("norm1", l)]
                S.op("dve", lambda e: e.tensor_tensor(out=xprev.t[:, 20:36], in0=xprev.t[:, 0:16], in1=pvec.t[:, n1:n1 + 16], op=ALU.mult),
                     reads=[xprev.r, pvec.r], writes=[xprev.r])
                S.op("dve", lambda e: e.tensor_scalar(out=hprev.t[:, 0:16], in0=xprev.t[:, 20:36], scalar1=xprev.t[:, 16:17], scalar2=None, op0=ALU.mult),
                     reads=[xprev.r], writes=[hprev.r])
            blk, br = wget(l, "A0")
            outs = [(txw, AF.Tanh), (xab, AF.Copy), (sxg0, AF.Sigmoid), (sxg1, AF.Sigmoid)]
            for (cid, off, rows), (ob, fn) in zip(LORA, outs):
                def post(xs, ob=ob, fn=fn, rows=rows):
                    S.op("act", lambda e: e.activation(out=ob.t[0:rows, 0:w], in_=xs, func=fn), reads=[ltmp.r], writes=[ob.r])
                p_chunk(l, T, blk, br, off - 3072, rows, cid, None, None, post=post)
            def front(hp):
                blk, br = wget(l, "Ahp%d" % hp)
                bufs = [RKV[j][hp % 2] for j in range(3)]
                for j in range(3):
                    p_chunk(l, T, blk, br, 128 * j, 128, j * 8 + hp, lambda j=j: bufs[j].t[:, 0:w], bufs[j].r)
                prep(l, T, hp, *bufs)
                spres[(T.idx, hp)] = Res("sp")
                spill(T, hp, 0)

            def back(hp):
                scan(l, T, hp)
                spill(T, hp, 1)

            S.replay(S.capture(lambda: front(0)))
            for hp in range(8):
                sb_ = S.capture(lambda: back(hp))
                sf_ = S.capture(lambda: front(hp + 1)) if hp < 7 else []
                S.replay(sb_, sf_)

        uya = sb("uya", [128, 16, 512], BF16, reg="C")
        uT = Buf(uya.t[:, 0:8, :], "uT")
        ya = Buf(uya.t[:, 8:16, :], "ya")
        uT.r = uya.r
        ya.r = uya.r
        hid = uya
        yb = sb("yb", [128, 8, 512], BF16, reg="C")
        vg = sb("vg", [128, 4, 1024], reg="C")
        vnb = [sb("vnb%d" % i, [128, 1024], BF16, reg="C") for i in range(2)]
        mixT = sb("mixT", [128, 16, 512], BF16, reg="C")
        lng = sb("lng", [128, DA], reg="C")
        lnb = sb("lnb", [128, DA], reg="C")
        bsf = Buf(vg.t[0:1, 0, :], "bsf")
        bsf.r = vg.r
        bhf = Buf(vg.t[0:1, 1, :], "bhf")
        bhf.r = vg.r
        bhi = sb("bhi", [1, 1024], BF16, reg="C")
        blo = sb("blo", [1, 1024], BF16, reg="C")
        wsb = sb("wsb", [128, 8, 128], BF16, reg="C")
        wsb2 = sb("wsb2", [128, 8, 64], BF16, reg="C")
        bnst = sb("bnst", [128, 2, 6], reg="C")
        mv = sb("mv", [128, 2], reg="C")
        Ct = [sb("Ct%d" % i, [128, 512], reg="C") for i in range(6)]
        ob_ = sb("Co", [128, 512], reg="C")
        odl = None
        bgl = sb("Cbg", [128, 512], reg="C")
        ggl = sb("Cgg", [128, 512], reg="C")
        sqb = sb("sqb", [128, 512], BF16, reg="C")
        obf = sb("obf", [128, 512], BF16, reg="C")

        def layer_params_C(l):
            S.op("act", lambda e: e.dma_start(out=lng.t[:], in_=lng_in[l:l + 1, :].partition_broadcast(128)), writes=[lng.r], dma=True)
            S.op("act", lambda e: e.dma_start(out=lnb.t[:], in_=lnb_in[l:l + 1, :].partition_broadcast(128)), writes=[lnb.r], dma=True)
            S.op("act", lambda e: e.dma_start(out=bsf.t[0:1, :], in_=bs_in[l]), writes=[bsf.r], dma=True)
            S.op("dve", lambda e: e.tensor_copy(out=bhi.t[0:1, :], in_=bsf.t[0:1, :]), reads=[bsf.r], writes=[bhi.r])
            S.op("dve", lambda e: e.tensor_copy(out=bhf.t[0:1, :], in_=bhi.t[0:1, :]), reads=[bhi.r], writes=[bhf.r])
            S.op("dve", lambda e: e.tensor_tensor(out=blo.t[0:1, :], in0=bsf.t[0:1, :], in1=bhf.t[0:1, :], op=ALU.subtract),
                 reads=[bsf.r, bhf.r], writes=[blo.r])
            S.op("pool", lambda e: e.dma_start(out=wsb.t[:], in_=wsT_in[l]), writes=[wsb.r], dma=True)
            trb = cb("triu").unsqueeze(1).broadcast_to([128, 8, 128])
            S.op("dve", lambda e: e.tensor_tensor(out=wsb.t[:], in0=wsb.t[:], in1=trb, op=ALU.mult), reads=[wsb.r, cstb.r], writes=[wsb.r])
            S.op("pool", lambda e: e.dma_start(out=wsb2.t[64:128, :, :], in_=wsT_in[l, 0:64, :, 0:64]), writes=[wsb2.r], dma=True)
            trb2 = cb("triu")[64:128, 64:128].unsqueeze(1).broadcast_to([64, 8, 64])
            S.op("dve", lambda e: e.tensor_tensor(out=wsb2.t[64:128, :, :], in0=wsb2.t[64:128, :, :], in1=trb2, op=ALU.mult),
                 reads=[wsb2.r, cstb.r], writes=[wsb2.r])

        def gemm_fm(blk, blkres, moff, rhs_fn, nk, rhsres, w):
            pb = gbank()
            for kc in range(nk):
                S.op("pe", lambda e, kc=kc: e.matmul(pb.t[:, 0:w], blk[:, kc, moff:moff + 128], rhs_fn(kc), start=(kc == 0), stop=(kc == nk - 1)),
                     reads=[blkres] + rhsres, writes=[pb.r])
            return pb

        def phaseC(l, T, last_layer):
            w, nseq = T.w, T.nseq
            load_x(l, T)
            rmsnorm(xTr, lambda c: xT.t[:, c, 0:w], w, "norm1", l, lambda c: hT.t[:, c, 0:w], [hT.r] * 16)
            ntb = (w + 127) // 128
            def post_hp(hp):
                i = T.idx
                o_, od_, bg_, gg_ = ob_, odl, bgl, ggl
                rs = [spres[(i, hp)]]
                S.op("act", lambda e, o_=o_: e.dma_start(out=o_.t[:, 0:w], in_=o_sp[i, hp, :, 0:w]), reads=rs, writes=[o_.r], dma=True)
                S.op("act", lambda e, bg_=bg_: e.dma_start(out=bg_.t[:, 0:w], in_=bg_sp[i, hp, :, 0:w]), reads=rs, writes=[bg_.r], dma=True)
                S.op("act", lambda e, gg_=gg_: e.dma_start(out=gg_.t[:, 0:w], in_=g_sp[i, hp, :, 0:w]), reads=rs, writes=[gg_.r], dma=True)
                S.op("act", lambda e, o_=o_: e.activation(out=obf.t[:, 0:w], in_=o_.t[:, 0:w], func=AF.Copy), reads=[o_.r], writes=[obf.r])
                pb = gbank()
                S.op("pe", lambda e, pb=pb: e.matmul(pb.t[:, 0:w], cb("bones"), obf.t[:, 0:w], start=True, stop=True), reads=[cstb.r, obf.r], writes=[pb.r])
                oc = Ct[0]
                S.op("dve", lambda e, pb=pb, o_=o_: e.scalar_tensor_tensor(out=oc.t[:, 0:w], in0=pb.t[:, 0:w], scalar=-1.0 / 64, in1=o_.t[:, 0:w],
                                                                          op0=ALU.mult, op1=ALU.add), reads=[pb.r, o_.r], writes=[oc.r])
                S.op("act", lambda e: e.activation(out=sqb.t[:, 0:w], in_=oc.t[:, 0:w], func=AF.Square), reads=[oc.r], writes=[sqb.r])
                pb2 = gbank()
                S.op("pe", lambda e, pb2=pb2: e.matmul(pb2.t[:, 0:w], cb("bones"), sqb.t[:, 0:w], start=True, stop=True), reads=[cstb.r, sqb.r], writes=[pb2.r])
                rs_ = Ct[1]
                S.op("dve", lambda e, pb2=pb2: e.tensor_scalar(out=rs_.t[:, 0:w], in0=pb2.t[:, 0:w], scalar1=1.0 / 64, scalar2=GN_EPS, op0=ALU.mult, op1=ALU.add),
                     reads=[pb2.r], writes=[rs_.r])
                S.op("act", lambda e: e.activation(out=rs_.t[:, 0:w], in_=rs_.t[:, 0:w], func=AF.Sqrt), reads=[rs_.r], writes=[rs_.r])
                S.op("dve", lambda e: e.reciprocal(out=rs_.t[:, 0:w], in_=rs_.t[:, 0:w]), reads=[rs_.r], writes=[rs_.r])
                S.op("dve", lambda e: e.tensor_tensor(out=oc.t[:, 0:w], in0=oc.t[:, 0:w], in1=rs_.t[:, 0:w], op=ALU.mult), reads=[oc.r, rs_.r], writes=[oc.r])
                S.op("dve", lambda e, hp=hp: e.tensor_scalar(out=oc.t[:, 0:w], in0=oc.t[:, 0:w], scalar1=pv("gn_g", l, hp), scalar2=pv("gn_b", l, hp),
                                                             op0=ALU.mult, op1=ALU.add), reads=[oc.r, pvec.r], writes=[oc.r])
                S.op("dve", lambda e, gg_=gg_: e.tensor_tensor(out=oc.t[:, 0:w], in0=oc.t[:, 0:w], in1=gg_.t[:, 0:w], op=ALU.mult), reads=[oc.r, gg_.r], writes=[oc.r])
                S.op("dve", lambda e, bg_=bg_, hp=hp: e.tensor_tensor(out=yb.t[:, hp, 0:w], in0=oc.t[:, 0:w], in1=bg_.t[:, 0:w], op=ALU.add),
                     reads=[oc.r, bg_.r], writes=[yb.r])


            pending = list(range(8))
            for j in range(2):
                blk, br = wget(l, "Cv%d" % j)
                for tb in range(ntb):
                    pb = gbank()
                    for kc in range(16):
                        S.op("pe", lambda e, kc=kc, tb=tb, pb=pb, blk=blk: e.matmul(pb.t[:, 0:512], hT.t[:, kc, tb * 128:(tb + 1) * 128], blk[:, kc, 0:512],
                                                                                   start=(kc == 0), stop=(kc == 15)),
                             reads=[br, hT.r], writes=[pb.r])
                    S.op("act", lambda e, tb=tb, pb=pb, j=j: e.activation(out=vg.t[:, tb, j * 512:(j + 1) * 512], in_=pb.t[:, 0:512], func=AF.Gelu_apprx_tanh),
                         reads=[pb.r], writes=[vg.r])
                    if pending:
                        post_hp(pending.pop(0))
            for j in range(2):
                blk, br = wget(l, "Cu%d" % j)
                for m in range(4):
                    pb = gemm_fm(blk, br, m * 128, lambda kc: hT.t[:, kc, 0:w], 16, [hT.r], w)
                    S.op("act", lambda e, pb=pb, j=j, m=m: e.activation(out=uT.t[:, j * 4 + m, 0:w], in_=pb.t[:, 0:w], func=AF.Gelu_apprx_tanh),
                         reads=[pb.r], writes=[uT.r])
            while pending:
                post_hp(pending.pop(0))
            def ln_block(tb):
                vn_ = vnb[tb % 2]
                for j in range(2):
                    S.op("dve", lambda e, tb=tb, j=j: e.bn_stats(out=bnst.t[:, j, :], in_=vg.t[:, tb, j * 512:(j + 1) * 512]), reads=[vg.r], writes=[bnst.r])
                S.op("dve", lambda e: e.bn_aggr(out=mv.t[:], in_=bnst.t[:].rearrange("p a b -> p (a b)")), reads=[bnst.r], writes=[mv.r])
                S.op("dve", lambda e: e.tensor_scalar(out=mv.t[:, 1:2], in0=mv.t[:, 1:2], scalar1=LN_EPS, scalar2=None, op0=ALU.add), reads=[mv.r], writes=[mv.r])
                S.op("act", lambda e: e.activation(out=mv.t[:, 1:2], in_=mv.t[:, 1:2], func=AF.Sqrt), reads=[mv.r], writes=[mv.r])
                S.op("dve", lambda e: e.reciprocal(out=mv.t[:, 1:2], in_=mv.t[:, 1:2]), reads=[mv.r], writes=[mv.r])
                S.op("dve", lambda e, tb=tb: e.tensor_scalar(out=vg.t[:, tb, :], in0=vg.t[:, tb, :], scalar1=mv.t[:, 0:1], scalar2=mv.t[:, 1:2],
                                                             op0=ALU.subtract, op1=ALU.mult), reads=[vg.r, mv.r], writes=[vg.r])
                S.op("pool", lambda e, tb=tb: e.tensor_tensor(out=vg.t[:, tb, :], in0=vg.t[:, tb, :], in1=lng.t[:], op=ALU.mult), reads=[vg.r, lng.r], writes=[vg.r])
                S.op("dve", lambda e, tb=tb: e.tensor_tensor(out=vg.t[:, tb, :], in0=vg.t[:, tb, :], in1=lnb.t[:], op=ALU.add), reads=[vg.r, lnb.r], writes=[vg.r])
                S.op("act", lambda e, tb=tb, vn_=vn_: e.activation(out=vn_.t[:], in_=vg.t[:, tb, :], func=AF.Copy), reads=[vg.r], writes=[vn_.r])
                if T.kind == "S":
                    S.op("act", lambda e, tb=tb: e.dma_start(out=vn_out[l, tb], in_=vg.t[:, tb, :]), reads=[vg.r], dma=True)

            def spatial_block(tb):
                vn_ = vnb[tb % 2]
                if T.kind == "P":
                    blocks = [(0, 128, tb * 128)]
                else:
                    blocks = [(64 * b, 64, tb * 128 + 64 * b) for b in range(2)]
                for half in range(2):
                    pb = gbank()
                    for gi in range(4):
                        g = half * 4 + gi
                        for (prow, bl, qoff) in blocks:
                            wsrc = wsb if prow == 0 else wsb2
                            oc_ = gi * 128 + (qoff - tb * 128)
                            S.op("pe", lambda e, g=g, prow=prow, bl=bl, oc_=oc_, wsrc=wsrc, pb=pb, vn_=vn_: e.matmul(
                                pb.t[:, oc_:oc_ + bl], vn_.t[prow:prow + bl, g * 128:(g + 1) * 128], wsrc.t[prow:prow + bl, g, 0:bl], start=True, stop=False),
                                reads=[vn_.r, wsrc.r], writes=[pb.r])
                            S.op("pe", lambda e, g=g, bl=bl, oc_=oc_, pb=pb: e.matmul(pb.t[:, oc_:oc_ + bl], cb("ones")[0:1, :], bhi.t[0:1, g * 128:g * 128 + bl],
                                                                                     start=False, stop=False), reads=[cstb.r, bhi.r], writes=[pb.r])
                            S.op("pe", lambda e, g=g, bl=bl, oc_=oc_, pb=pb: e.matmul(pb.t[:, oc_:oc_ + bl], cb("ones")[0:1, :], blo.t[0:1, g * 128:g * 128 + bl],
                                                                                     start=False, stop=True), reads=[cstb.r, blo.r], writes=[pb.r])
                    S.op("dve", lambda e, pb=pb, half=half, tb=tb: e.tensor_tensor(
                        out=ya.t[:, half * 4:half * 4 + 4, tb * 128:(tb + 1) * 128], in0=pb.t[:, 0:512].rearrange("p (g q) -> p g q", g=4),
                        in1=uT.t[:, half * 4:half * 4 + 4, tb * 128:(tb + 1) * 128], op=ALU.mult),
                        reads=[pb.r, uT.r], writes=[ya.r])

            ln_block(0)
            for tb in range(ntb):
                if tb + 1 < ntb:
                    ln_block(tb + 1)
                spatial_block(tb)
            for mg in range(8):
                G, gr = wget(l, "G%d" % mg)
                P_, pr = wget(l, "P%d" % mg)
                for mm in range(2):
                    m = 2 * mg + mm
                    pga = gemm_fm(G, gr, mm * 128, lambda kc: hT.t[:, kc, 0:w], 16, [hT.r], w)
                    pgb = gemm_fm(G, gr, 256 + mm * 128, lambda kc: hT.t[:, kc, 0:w], 16, [hT.r], w)
                    ppa = gemm_fm(P_, pr, mm * 128, lambda kc: ya.t[:, kc, 0:w], 8, [ya.r], w)
                    ppb = gemm_fm(P_, pr, 256 + mm * 128, lambda kc: yb.t[:, kc, 0:w], 8, [yb.r], w)
                    sa, sb_ = Ct[2 + (m % 2) * 2], Ct[3 + (m % 2) * 2]
                    S.op("act", lambda e, pga=pga, sa=sa: e.activation(out=sa.t[:, 0:w], in_=pga.t[:, 0:w], func=AF.Sigmoid), reads=[pga.r], writes=[sa.r])
                    S.op("act", lambda e, pgb=pgb, sb_=sb_: e.activation(out=sb_.t[:, 0:w], in_=pgb.t[:, 0:w], func=AF.Sigmoid), reads=[pgb.r], writes=[sb_.r])
                    S.op("dve", lambda e, ppa=ppa, sa=sa: e.tensor_tensor(out=sa.t[:, 0:w], in0=ppa.t[:, 0:w], in1=sa.t[:, 0:w], op=ALU.mult),
                         reads=[ppa.r, sa.r], writes=[sa.r])
                    S.op("dve", lambda e, ppb=ppb, sb_=sb_: e.tensor_tensor(out=sb_.t[:, 0:w], in0=ppb.t[:, 0:w], in1=sb_.t[:, 0:w], op=ALU.mult),
                         reads=[ppb.r, sb_.r], writes=[sb_.r])
                    S.op("pool", lambda e, sa=sa, sb_=sb_, m=m: e.tensor_tensor(out=mixT.t[:, m, 0:w], in0=sa.t[:, 0:w], in1=sb_.t[:, 0:w], op=ALU.add),
                         reads=[sa.r, sb_.r], writes=[mixT.r])
            for j in range(4):
                blk, br = wget(l, "O%d" % j)
                for m in range(4):
                    pb = gemm_fm(blk, br, m * 128, lambda kc: mixT.t[:, kc, 0:w], 16, [mixT.r], w)
                    c = 4 * j + m
                    S.op("dve", lambda e, pb=pb, c=c: e.tensor_tensor(out=xT.t[:, c, 0:w], in0=pb.t[:, 0:w], in1=xT.t[:, c, 0:w], op=ALU.add),
                         reads=[pb.r, xTr[c]], writes=[xTr[c]])
            rmsnorm(xTr, lambda c: xT.t[:, c, 0:w], w, "norm2", l, lambda c: hT.t[:, c, 0:w], [hT.r] * 16)
            for q in range(4):
                for j in range(4):
                    blk, br = wget(l, "U%d_%d" % (q, j))
                    for m in range(4):
                        pb = gemm_fm(blk, br, m * 128, lambda kc: hT.t[:, kc, 0:w], 16, [hT.r], w)
                        t = Ct[(4 * j + m) % 2]
                        S.op("act", lambda e, pb=pb, t=t: e.activation(out=t.t[:, 0:w], in_=pb.t[:, 0:w], func=AF.Relu), reads=[pb.r], writes=[t.r])
                        S.op("pool", lambda e, t=t, j=j, m=m: e.tensor_tensor(out=hid.t[:, 4 * j + m, 0:w], in0=t.t[:, 0:w], in1=t.t[:, 0:w], op=ALU.mult),
                             reads=[t.r], writes=[hid.r])
                for j in range(4):
                    blk, br = wget(l, "D%d_%d" % (q, j))
                    for m in range(4):
                        pb = gemm_fm(blk, br, m * 128, lambda kc: hid.t[:, kc, 0:w], 16, [hid.r], w)
                        c = 4 * j + m
                        S.op("dve", lambda e, pb=pb, c=c: e.tensor_tensor(out=xT.t[:, c, 0:w], in0=pb.t[:, 0:w], in1=xT.t[:, c, 0:w], op=ALU.add),
                             reads=[pb.r, xTr[c]], writes=[xTr[c]])
            if not last_layer:
                dst = x1_scr.rearrange("(c p) t -> p c t", p=128)
                for c in range(16):
                    x1res[(T.idx, c)] = Res("x1")
                    S.op("act", lambda e, c=c: e.dma_start(out=dst[:, c, T.c0:T.c0 + w], in_=xT.t[:, c, 0:w]), reads=[xTr[c]], writes=[x1res[(T.idx, c)]], dma=True)
            else:
                rmsnorm(xTr, lambda c: xT.t[:, c, 0:w], w, "norm_f", 0, lambda c: xT.t[:, c, 0:w], xTr)
                dst = yT_out.rearrange("(c p) t -> p c t", p=128)
                for c in range(16):
                    S.op("act", lambda e, c=c: e.dma_start(out=dst[:, c, T.c0:T.c0 + w], in_=xT.t[:, c, 0:w]), reads=[xTr[c]], dma=True)

        for l in range(n_layers):
            phase[0] = 0
            layer_params(l)
            for T in tiles:
                phaseA(l, T)
            S.op("act", lambda e, l=l: e.dma_start(out=tsh_out[l], in_=plast.t[:]), reads=[plast.r], dma=True)
            S.barrier()
            phase[0] = 1
            layer_params_C(l)
            for T in tiles:
                phaseC(l, T, l == n_layers - 1)
            S.barrier()
        S.emit()
    return nc


def _cols(v):
    v = np.asarray(v, np.float32).reshape(-1, 128)
    return np.ascontiguousarray(v.T)


def _pvec(inp):
    pvv = np.zeros((128, NPV), np.float32)
    for l in range(NL):
        pvv[:, PV[("norm1", l)]:PV[("norm1", l)] + 16] = _cols(inp["norm1"][l])
        pvv[:, PV[("norm2", l)]:PV[("norm2", l)] + 16] = _cols(inp["norm2"][l])
        mu = np.asarray(inp["mu_shift"][l], np.float32)
        mo = PV[("mu", l)]
        pvv[:, mo:mo + 24] = _cols(mu[0:3072])
        for (cid, off, rows) in LORA:
            pvv[0:rows, mo + cid] = mu[off:off + rows]
        for nm in ("w0", "a0", "k_k", "k_a", "gn_g", "gn_b"):
            pvv[:, PV[(nm, l)]:PV[(nm, l)] + 8] = _cols(inp[nm][l])
        pvv[:, PV[("r_k", l)]:PV[("r_k", l)] + 8] = _cols(np.asarray(inp["r_k"][l]).reshape(-1))
    pvv[:, PV[("norm_f", 0)]:PV[("norm_f", 0)] + 16] = _cols(inp["norm_f"])
    return pvv


def _tsh_layout(ts):
    L, n, _ = ts.shape
    out = np.zeros((L, 128, NPCH, n), np.float32)
    out[:, :, 0:24, :] = ts[:, :, 0:3072].reshape(L, n, 24, 128).transpose(0, 3, 2, 1)
    for (cid, off, rows) in LORA:
        out[:, 0:rows, cid, :] = ts[:, :, off:off + rows].transpose(0, 2, 1)
    return out


def _tsh_unlayout(t):
    L, _, _, n = t.shape
    out = np.zeros((L, n, DSH), np.float32)
    out[:, :, 0:3072] = t[:, :, 0:24, :].transpose(0, 3, 2, 1).reshape(L, n, 3072)
    for (cid, off, rows) in LORA:
        out[:, :, off:off + rows] = t[:, 0:rows, cid, :].transpose(0, 2, 1)
    return out


def _wkv_layout(s):
    L, n = s.shape[:2]
    out = np.zeros((L, n, 8, 128, 128), np.float32)
    sT = s.transpose(0, 1, 2, 4, 3).reshape(L, n, 8, 2, 64, 64)
    out[:, :, :, 0:64, 0:64] = sT[:, :, :, 0]
    out[:, :, :, 64:128, 64:128] = sT[:, :, :, 1]
    return out


def _wkv_unlayout(o):
    L, n = o.shape[:2]
    s = np.zeros((L, n, 8, 2, 64, 64), np.float32)
    s[:, :, :, 0] = o[:, :, :, 0:64, 0:64]
    s[:, :, :, 1] = o[:, :, :, 64:128, 64:128]
    return np.ascontiguousarray(s.reshape(L, n, 16, 64, 64).transpose(0, 1, 2, 4, 3))


_PROG = {}


def _program(key, **kw):
    if key not in _PROG:
        _PROG[key] = build_program(**kw)
    return _PROG[key]


def make_in_maps(inp, seg_x, seg_prev, samp_ids):
    f = lambda a: np.ascontiguousarray(np.asarray(a, np.float32))
    shared = {
        "pvec": _pvec(inp), "cst": CST.copy(),
        "ln_v_g": f(inp["ln_v_g"]), "ln_v_b": f(inp["ln_v_b"]),
        "b_s": f(inp["b_s"]).reshape(NL, 1, 1024),
        "w_sT": np.ascontiguousarray(f(inp["w_s"]).transpose(0, 3, 1, 2)),
        "w2": f(inp["w2"]), "a2": f(inp["a2"]), "g2": f(inp["g2"]),
    }
    for k in WSHAPES:
        shared[k] = f(inp[k])
    xs = f(inp["x_sample"])
    ts = f(inp["state_tshift"])
    wk = f(inp["state_wkv"])
    maps = []
    for c in range(len(seg_x)):
        ids = samp_ids[c]
        xa = np.concatenate([seg_x[c]] + [xs[i] for i in ids], axis=0)
        m = dict(shared)
        m["xT"] = np.ascontiguousarray(xa.T)
        m["xprev"] = _cols(seg_prev[c])
        m["tsh_s"] = _tsh_layout(ts[:, ids, :])
        m["wkv_s"] = _wkv_layout(wk[:, ids])
        maps.append(m)
    return maps


def kernel(**inp):
    xp = np.asarray(inp["x_prompt"], np.float32)
    B, SEQ, _ = xp.shape
    nb = np.asarray(inp["x_sample"]).shape[0]
    ncores = 8
    NP = SEQ
    NPT = NP // 512
    NS = nb // ncores
    seg_x, seg_prev, samp = [], [], []
    zeros = np.zeros((NP, D), np.float32)
    for c in range(ncores):
        seg_x.append(xp[c] if c < B else zeros)
        seg_prev.append(np.zeros(D, np.float32))
        samp.append(list(range(c * NS, (c + 1) * NS)))
    maps = make_in_maps(inp, seg_x, seg_prev, samp)
    nc = _program(("full", NPT, NS), NPT=NPT, NS=NS)
    res = run_bass_kernel_spmd(nc, maps, core_ids=list(range(ncores))).results
    y_p = np.zeros((B, SEQ, D), np.float32)
    y_s = np.zeros((nb, 64, D), np.float32)
    tsh_p = np.zeros((NL, B, DSH), np.float32)
    wkv_p = np.zeros((NL, B, 16, 64, 64), np.float32)
    tsh_s = np.zeros((NL, nb, DSH), np.float32)
    wkv_s = np.zeros((NL, nb, 16, 64, 64), np.float32)
    vn_s = np.zeros((NL, nb, 64, DA), np.float32)
    for c in range(ncores):
        r = res[c]
        yT = r["yT"]
        ids = samp[c]
        y_s[ids] = yT[:, NP:].T.reshape(NS, 64, D)
        tso = _tsh_unlayout(r["tsh_o"])
        wko = _wkv_unlayout(r["wkv_o"])
        tsh_s[:, ids] = tso[:, 1:]
        wkv_s[:, ids] = wko[:, 1:]
        vn_s[:, ids] = r["vn_o"].reshape(NL, NS, 64, DA)
        if c < B:
            y_p[c] = yT[:, 0:NP].T
            tsh_p[:, c] = tso[:, 0]
            wkv_p[:, c] = wko[:, 0]
    return (y_p, y_s, tsh_p, wkv_p, tsh_s, wkv_s, vn_s)
```

```python
import contextlib
import types
import numpy as np
import concourse.bass as bass
import concourse.mybir as mybir
from concourse.bass_utils import run_bass_kernel_spmd

F32 = mybir.dt.float32
BF16 = mybir.dt.bfloat16
ALU = mybir.AluOpType
AF = mybir.ActivationFunctionType

D = 2048
DA = 1024
DB = 1024
DIN = 5408
DFF = 8192
NL = 2
PC0 = 2048
DSH = 3360
NPCH = 28
C0 = 0.6065306597126334
RMS_EPS = 1e-5
LN_EPS = 1e-5
GN_EPS = 64e-5
ENGS = ("pe", "act", "dve", "pool", "sp")


class Res:
    __slots__ = ("name", "w", "r", "rd")

    def __init__(self, name=""):
        self.name = name
        self.w = None
        self.r = {}
        self.rd = []


class Op:
    __slots__ = ("eng", "fn", "deps", "dma", "sig", "sigval", "dsem", "dval", "idx")


def _freeze(fn):
    if fn is None or fn.__closure__ is None:
        return fn
    cells = []
    for c in fn.__closure__:
        try:
            cells.append(types.CellType(c.cell_contents))
        except ValueError:
            cells.append(c)
    return types.FunctionType(fn.__code__, fn.__globals__, fn.__name__, fn.__defaults__, tuple(cells))


class Sched:
    def __init__(self, nc, n_dma_sems=20):
        self.nc = nc
        self.ops = []
        self.n_dma_sems = n_dma_sems
        self.dma_count = {e: 0 for e in ENGS}
        self.last = {}
        self.last_dma = {}
        self.cap = None

    def capture(self, gen):
        old = self.cap
        self.cap = []
        gen()
        out = self.cap
        self.cap = old
        return out

    def replay(self, *streams):
        streams = [st_ for st_ in streams if st_]
        pos = [0] * len(streams)
        tot = sum(len(st_) for st_ in streams)
        for _ in range(tot):
            best, bi = None, 0
            for i, st_ in enumerate(streams):
                if pos[i] < len(st_):
                    frac = pos[i] / len(st_)
                    if best is None or frac < best:
                        best, bi = frac, i
            a = streams[bi][pos[bi]]
            pos[bi] += 1
            self.op(*a[:2], reads=a[2], writes=a[3], dma=a[4], extra_deps=a[5], frozen=True)

    def op(self, eng, fn, reads=(), writes=(), dma=False, extra_deps=(), frozen=False):
        if not frozen:
            fn = _freeze(fn)
        if self.cap is not None:
            self.cap.append((eng, fn, list(reads), list(writes), dma, tuple(extra_deps)))
            return None
        o = Op()
        o.eng, o.fn, o.dma = eng, fn, dma
        o.idx = len(self.ops)
        o.sig = False
        o.sigval = 0
        deps = set(extra_deps)
        for r in reads:
            if r.w is not None:
                deps.add(r.w)
        for r in writes:
            if r.w is not None:
                deps.add(r.w)
            deps.update(r.r.values())
            deps.update(r.rd)
        ops = self.ops
        if eng == "pe" and not dma:
            deps = {d for d in deps if not (ops[d].eng == "pe" and not ops[d].dma)}
        o.deps = deps
        for r in reads:
            if dma:
                r.rd.append(o.idx)
            else:
                r.r[eng] = o.idx
        for r in writes:
            r.w = o.idx
            r.r = {}
            r.rd = []
        if dma:
            n = self.dma_count[eng]
            self.dma_count[eng] = n + 1
            o.dsem = (eng, n % self.n_dma_sems)
            o.dval = 16 * (n // self.n_dma_sems + 1)
            if o.dsem in self.last_dma:
                o.deps.add(self.last_dma[o.dsem])
            self.last_dma[o.dsem] = o.idx
        else:
            self.last[eng] = o.idx
        self.ops.append(o)
        return o

    def barrier(self, engines=("pe", "act", "dve", "pool")):
        deps = set(self.last.values()) | set(self.last_dma.values())
        for e in engines:
            self.op(e, None, extra_deps=deps)

    def emit(self, final_wait_eng="sp"):
        nc = self.nc
        ops = self.ops
        for o in ops:
            for d in o.deps:
                if not ops[d].dma:
                    ops[d].sig = True
        cnt = {e: 0 for e in ENGS}
        for o in ops:
            if o.sig and not o.dma:
                if o.fn is None:
                    o.sig = False
                    continue
                cnt[o.eng] += 1
                o.sigval = cnt[o.eng]
        last_dma = {}
        for o in ops:
            if o.dma:
                last_dma[o.dsem] = max(last_dma.get(o.dsem, 0), o.dval)
        with contextlib.ExitStack() as st:
            esem = {e: st.enter_context(nc.semaphore("s_" + e)) for e in ENGS if e != "sp"}
            dsem = {}
            for e in ENGS:
                for i in range(min(self.n_dma_sems, self.dma_count[e])):
                    dsem[(e, i)] = st.enter_context(nc.semaphore("d_%s%d" % (e, i)))
            block = st.enter_context(nc.Block())
            per_eng = {e: [o for o in ops if o.eng == e] for e in ENGS}

            def run(e, engobj):
                frontier = {}
                for o in per_eng[e]:
                    need = {}
                    for d in o.deps:
                        p = ops[d]
                        if p.dma:
                            key, val = ("d",) + p.dsem, p.dval
                        else:
                            if p.fn is None:
                                continue
                            key, val = ("e", p.eng), p.sigval
                        if val > need.get(key, 0):
                            need[key] = val
                    for key, val in need.items():
                        if frontier.get(key, 0) >= val:
                            continue
                        frontier[key] = val
                        s = dsem[key[1:]] if key[0] == "d" else esem[key[1]]
                        engobj.wait_ge(s, val)
                    if o.fn is None:
                        continue
                    ins = o.fn(engobj)
                    if o.dma:
                        ins.then_inc(dsem[o.dsem], 16)
                    elif o.sig:
                        ins.then_inc(esem[o.eng], 1)
                if e == final_wait_eng:
                    for key, val in last_dma.items():
                        engobj.wait_ge(dsem[key], val)

            @block.tensor
            def _(eng):
                run("pe", eng)

            @block.scalar
            def _(eng):
                run("act", eng)

            @block.vector
            def _(eng):
                run("dve", eng)

            @block.gpsimd
            def _(eng):
                run("pool", eng)

            @block.sync
            def _(eng):
                run("sp", eng)


class Buf:
    def __init__(self, t, name=""):
        self.t = t
        self.r = Res(name)


def _consts():
    p = np.arange(128)[:, None]
    c = np.arange(128)[None, :]
    same = (p // 64) == (c // 64)
    ident = (p == c).astype(np.float32)
    J = (c == (p + 64) % 128).astype(np.float32)
    su = (same & ((p % 64) < (c % 64))).astype(np.float32)
    ui = (same & ((p % 64) <= (c % 64))).astype(np.float32)
    sl = (same & ((p % 64) > (c % 64))).astype(np.float32)
    mgram = np.concatenate([su, ui, su, ui], axis=1)
    bones = same.astype(np.float32)
    triu = (p <= c).astype(np.float32)
    cm = np.ones((128, 512), np.float32)
    cm[:, ::64] = 0.0
    ones = np.ones((128, 128), np.float32)
    parts = [("ident", ident), ("J", J), ("mgram", mgram), ("msl", sl), ("bones", bones),
             ("triu", triu), ("cmask", cm), ("ones", ones)]
    off = {}
    o = 0
    for k, v in parts:
        off[k] = (o, v.shape[1])
        o += v.shape[1]
    return np.concatenate([v for _, v in parts], axis=1), off


CST, CST_OFF = _consts()

PV = {}
_o = 0
for _l in range(NL):
    for _nm, _n in (("norm1", 16), ("norm2", 16), ("mu", NPCH), ("w0", 8), ("a0", 8), ("k_k", 8),
                    ("k_a", 8), ("r_k", 8), ("gn_g", 8), ("gn_b", 8)):
        PV[(_nm, _l)] = _o
        _o += _n
PV[("norm_f", 0)] = _o
_o += 16
NPV = _o

LORA = [(24, 3072, 64), (25, 3136, 64), (26, 3200, 128), (27, 3328, 32)]


def _blocks():
    bl = []
    bl.append(("A0", 16, 288, [("w_in", 0, PC0 + 3072, 288, 0)]))
    for hp in range(8):
        bl.append(("Ahp%d" % hp, 16, 384, [("w_in", 0, PC0 + j * 1024 + hp * 128, 128, j * 128) for j in range(3)]))
    for j in range(2):
        bl.append(("Cv%d" % j, 16, 512, [("w_in", 0, 1024 + 512 * j, 512, 0)]))
    for j in range(2):
        bl.append(("Cu%d" % j, 16, 512, [("w_in", 0, 512 * j, 512, 0)]))
    for mg in range(8):
        bl.append(("G%d" % mg, 16, 512, [("w_gate", 0, 256 * mg, 256, 0), ("w_gate", 0, 2048 + 256 * mg, 256, 256)]))
        bl.append(("P%d" % mg, 8, 512, [("w_pa", 0, 256 * mg, 256, 0), ("w_pb", 0, 256 * mg, 256, 256)]))
    for j in range(4):
        bl.append(("O%d" % j, 16, 512, [("w_o", 0, 512 * j, 512, 0)]))
    for q in range(4):
        for j in range(4):
            bl.append(("U%d_%d" % (q, j), 16, 512, [("w_up", 0, 2048 * q + 512 * j, 512, 0)]))
        for j in range(4):
            bl.append(("D%d_%d" % (q, j), 16, 512, [("w_down", 2048 * q, 512 * j, 512, 0)]))
    return bl


BLOCKS = _blocks()
WSHAPES = {"w_in": (D, DIN), "w_gate": (D, 2 * D), "w_pa": (DA, D), "w_pb": (DB, D), "w_o": (D, D),
           "w_up": (D, DFF), "w_down": (DFF, D)}


class Tile:
    def __init__(self, c0, w, nseq, kind, first, last, idx):
        self.c0, self.w, self.nseq, self.kind = c0, w, nseq, kind
        self.Lq = w // nseq
        self.first, self.last, self.idx = first, last, idx
        self.nch = w // 64


def build_program(NPT=4, NS=4, GS=4, dbg=False, n_layers=NL):
    NP = NPT * 512
    NSW = NS * 64
    NTOK = NP + NSW
    tiles = [Tile(512 * i, 512, 1, "P", i == 0, i == NPT - 1, i) for i in range(NPT)]
    if NS:
        tiles.append(Tile(NP, NSW, NS, "S", False, False, NPT))
    NT = len(tiles)
    nc = bass.Bass("TRN2", target_bir_lowering=False)

    def din(name, shape, dt=F32):
        return nc.dram_tensor(name, list(shape), dt, kind="ExternalInput").ap()

    def dout(name, shape, dt=F32):
        return nc.dram_tensor(name, list(shape), dt, kind="ExternalOutput").ap()

    def dscr(name, shape, dt=F32):
        return nc.dram_tensor(name, list(shape), dt, kind="Internal").ap()

    xT_in = din("xT", [D, NTOK])
    xprev_in = din("xprev", [128, 16])
    pvec_in = din("pvec", [128, NPV])
    cst_in = din("cst", [128, CST.shape[1]])
    lng_in = din("ln_v_g", [NL, DA])
    lnb_in = din("ln_v_b", [NL, DA])
    bs_in = din("b_s", [NL, 1, 1024])
    wsT_in = din("w_sT", [NL, 128, 8, 128])
    w2_in = din("w2", [NL, 64, DB])
    a2_in = din("a2", [NL, 64, DB])
    g2_in = din("g2", [NL, 160, DB])
    tsh_in = din("tsh_s", [NL, 128, NPCH, max(NS, 1)])
    wkv_in = din("wkv_s", [NL, max(NS, 1), 8, 128, 128])
    W_in = {k: din(k, [NL] + list(v)) for k, v in WSHAPES.items()}

    yT_out = dout("yT", [D, NTOK])
    tsh_out = dout("tsh_o", [NL, 128, NPCH, 1 + max(NS, 1)])
    wkv_out = dout("wkv_o", [NL, 1 + max(NS, 1), 8, 128, 128])
    vn_out = dout("vn_o", [NL, max(NSW, 128) // 128, 128, DA])
    dbg_out = dout("dbg", [32, 128, 512]) if dbg else None

    x1_scr = dscr("x1_scr", [D, NTOK])
    o_sp = dscr("o_sp", [NT, 8, 128, 512])
    od_sp = dscr("od_sp", [NT, 8, 128, 1024], BF16)
    bg_sp = dscr("bg_sp", [NT, 8, 128, 512])
    g_sp = dscr("g_sp", [NT, 8, 128, 512])
    wscr = {}
    for l in range(n_layers):
        for (nm, kc, ncol, pieces) in BLOCKS:
            wscr[(l, nm)] = dscr("ws_%d_%s" % (l, nm), [128, kc, ncol], BF16)

    st = contextlib.ExitStack()
    with st:
        S = Sched(nc)
        BLK = {b[0]: b for b in BLOCKS}

        ARENA_BYTES = 130 * 1024
        arena = st.enter_context(nc.sbuf_tensor("arena", [128, ARENA_BYTES // 4], F32))
        aoff = {"A": 0, "C": 0}

        def sb(name, shape, dt=F32, reg="S"):
            shape = list(shape)
            if reg == "S":
                return Buf(st.enter_context(nc.sbuf_tensor("sb_" + name, shape, dt)), name)
            esz = 4 if dt == F32 else 2
            n = 1
            for d_ in shape[1:]:
                n *= d_
            nbytes = (n * esz + 31) // 32 * 32
            o = aoff[reg]
            aoff[reg] = o + nbytes
            assert aoff[reg] <= ARENA_BYTES, (name, reg, aoff[reg])
            ap = arena[0:shape[0], o // 4:(o + nbytes) // 4]
            if dt != F32:
                ap = ap.bitcast(dt)
            ap = ap[:, 0:n]
            if len(shape) == 3:
                ap = ap.rearrange("p (a b) -> p a b", a=shape[1])
            b = Buf(ap, name)
            b.rows = shape[0]
            return b

        def ps(name):
            return Buf(st.enter_context(nc.psum_tensor(name, [128, 512], F32)), name)

        PSB = [ps("ps%d" % i) for i in range(8)]

        pvec = sb("pvec", [128, NPV])
        omk = sb("omk", [128, NL * 8])
        cstb = sb("cstb", [128, CST.shape[1]], BF16)
        cst = cstb

        def cb(name):
            o, n = CST_OFF[name]
            return cstb.t[:, o:o + n]

        cf = cb
        S.op("act", lambda e: e.dma_start(out=pvec.t[:], in_=pvec_in), writes=[pvec.r], dma=True)
        S.op("pool", lambda e: e.dma_start(out=cstb.t[:], in_=cst_in), writes=[cstb.r], dma=True)
        for l in range(NL):
            ka = PV[("k_a", l)]
            S.op("dve", lambda e, l=l, ka=ka: e.tensor_scalar(out=omk.t[:, l * 8:(l + 1) * 8], in0=pvec.t[:, ka:ka + 8],
                                                              scalar1=-1.0, scalar2=1.0, op0=ALU.mult, op1=ALU.add),
                 reads=[pvec.r], writes=[omk.r])

        def pv(name, l, c=0, rows=128):
            o = PV[(name, l)] + c
            return pvec.t[0:rows, o:o + 1]

        wres = {}
        conv_order = [(l, b[0]) for l in range(n_layers) for b in BLOCKS]
        conv_pos = [0]

        def conv_next():
            if conv_pos[0] >= len(conv_order):
                return
            l, nm = conv_order[conv_pos[0]]
            conv_pos[0] += 1
            _, kc, ncol, pieces = BLK[nm]
            rl = []
            for (wn, r0, c0, n, d0) in pieces:
                src = W_in[wn][l, r0:r0 + kc * 128, c0:c0 + n].rearrange("(k p) c -> p k c", p=128)
                dst = wscr[(l, nm)][:, :, d0:d0 + n]
                r = Res("cv")
                S.op("pool", lambda e, s=src, d=dst: e.dma_start(out=d, in_=s), writes=[r], dma=True)
                rl.append(r)
            wres[(l, nm)] = rl

        NSLOT = 3
        slots = [sb("wslot%d" % i, [128, 16 * 512], BF16) for i in range(NSLOT)]
        wctr = [0]

        def wget(l, nm):
            _, kc, ncol, _ = BLK[nm]
            while (l, nm) not in wres:
                conv_next()
            conv_next()
            sl = slots[wctr[0] % NSLOT]
            wctr[0] += 1
            view = sl.t[:, 0:kc * ncol].rearrange("p (k c) -> p k c", k=kc)
            src = wscr[(l, nm)]
            S.op("sp", lambda e, v=view, s=src: e.dma_start(out=v, in_=s), reads=wres[(l, nm)], writes=[sl.r], dma=True)
            return view, sl.r

        xT = sb("xT", [128, 16, 512], reg="C")
        xTr = [Res("xT%d" % c) for c in range(16)]
        xstg = [sb("xstg%d" % i, [128, 2, 512], reg="A") for i in range(2)]
        hT = sb("hT", [128, 16, 512], BF16)
        hprev = sb("hprev", [128, 16], BF16)
        xprev = sb("xprev", [128, 40])
        sq = [sb("sq%d" % i, [128, 512], BF16) for i in range(4)]
        rstd = sb("rstd", [128, 512])
        gctr = [0]
        gbanks = [list(range(0, 3)), list(range(0, 8))]
        phase = [0]

        def gbank():
            bl = gbanks[phase[0]]
            b = PSB[bl[gctr[0] % len(bl)]]
            gctr[0] += 1
            return b

        ectr = [0]

        def evac_eng():
            ectr[0] += 1
            return "act" if ectr[0] % 2 else "dve"

        def copy_op(eng, out, in_, reads, writes):
            if eng == "act":
                S.op("act", lambda e: e.activation(out=out, in_=in_, func=AF.Copy), reads=reads, writes=writes)
            else:
                S.op(eng, lambda e: e.tensor_copy(out=out, in_=in_), reads=reads, writes=writes)

        dbg_i = [0]
        dbgbufs = []

        def dump(buf, ap, rows=128, cols=512):
            if not dbg:
                return
            i = dbg_i[0]
            dbg_i[0] += 1
            if not dbgbufs:
                dbgbufs.extend([sb("dbgt%d" % j, [128, 512]) for j in range(2)])
            tmp = dbgbufs[i % 2]
            S.op("dve", lambda e: e.memset(tmp.t[:], 0.0), writes=[tmp.r])
            S.op("dve", lambda e: e.tensor_copy(out=tmp.t[0:rows, 0:cols], in_=ap), reads=[buf.r], writes=[tmp.r])
            S.op("act", lambda e: e.dma_start(out=dbg_out[i], in_=tmp.t[:]), reads=[tmp.r], dma=True)

        def load_x(l, T):
            src = (xT_in if l == 0 else x1_scr).rearrange("(c p) t -> p c t", p=128)
            for c in range(16):
                rd = [] if l == 0 else [x1res[(T.idx, c)]]
                S.op("act", lambda e, c=c: e.dma_start(out=xT.t[:, c, 0:T.w], in_=src[:, c, T.c0:T.c0 + T.w]), reads=rd, writes=[xTr[c]], dma=True)

        def rmsnorm(xres, xap_fn, w, gname, l, out_fn, ores):
            pb = gbank()
            for c in range(16):
                s = sq[c % 4]
                S.op("act", lambda e, c=c, s=s: e.activation(out=s.t[:, 0:w], in_=xap_fn(c), func=AF.Square),
                     reads=[xres[c]], writes=[s.r])
                S.op("pe", lambda e, c=c, s=s: e.matmul(pb.t[:, 0:w], cb("ones"), s.t[:, 0:w], start=(c == 0), stop=(c == 15)),
                     reads=[s.r, cstb.r], writes=[pb.r])
            S.op("dve", lambda e: e.tensor_scalar(out=rstd.t[:, 0:w], in0=pb.t[:, 0:w], scalar1=1.0 / D, scalar2=RMS_EPS,
                                                  op0=ALU.mult, op1=ALU.add), reads=[pb.r], writes=[rstd.r])
            S.op("act", lambda e: e.activation(out=rstd.t[:, 0:w], in_=rstd.t[:, 0:w], func=AF.Sqrt), reads=[rstd.r], writes=[rstd.r])
            S.op("dve", lambda e: e.reciprocal(out=rstd.t[:, 0:w], in_=rstd.t[:, 0:w]), reads=[rstd.r], writes=[rstd.r])
            for c in range(16):
                S.op("dve", lambda e, c=c: e.scalar_tensor_tensor(out=out_fn(c), in0=xap_fn(c), scalar=pv(gname, l, c),
                                                                  in1=rstd.t[:, 0:w], op0=ALU.mult, op1=ALU.mult),
                     reads=[xres[c], rstd.r, pvec.r], writes=[ores[c]])

        x1res = {}
        hTr16 = None

        A = {}
        for nm in ("sw", "aa", "gg", "cs", "cse", "e1", "e2", "e3", "bg", "osp"):
            A[nm] = sb("A_" + nm, [128, 512], reg="A")
        A["tmp"] = A["cse"]
        A["kk"] = A["sw"]
        A["t1"] = A["cs"]
        A["kkn"] = A["e1"]
        A["kh"] = A["e2"]
        A["beta"] = A["e3"]
        A["sqk"] = sb("A_sqk", [128, 512], BF16, reg="A")
        A["rkk"] = sb("A_rkk", [128, 512], BF16, reg="A")
        RKV = [[sb("A_rkv%d_%d" % (j, i), [128, 512], reg="A") for i in range(2)] for j in range(3)]
        pbufs = [sb("pbuf%d" % i, [128, 520], reg="A") for i in range(2)]
        dtmp = [sb("dtmp%d" % i, [128, 512], reg="A") for i in range(2)]
        ltmp = sb("ltmp", [128, 512], reg="A")
        txw = sb("txw", [64, 512], BF16, reg="A")
        xab = sb("xab", [64, 512], BF16, reg="A")
        sxg0 = sb("sxg0", [128, 512], BF16, reg="A")
        sxg1 = sb("sxg1", [32, 512], BF16, reg="A")
        plast = sb("plast", [128, NPCH, 1 + max(NS, 1)], reg="A")
        tshs = sb("tshs", [128, NPCH, max(NS, 1)], reg="A")
        w2b = sb("w2b", [64, DB], BF16, reg="A")
        a2b = sb("a2b", [64, DB], BF16, reg="A")
        g2b0 = sb("g2b0", [128, DB], BF16, reg="A")
        g2b1 = sb("g2b1", [32, DB], BF16, reg="A")
        eBD = [sb("eBD%d" % i, [128, 1024], BF16, reg="A") for i in range(3)]
        BDS = []
        for i in range(2):
            BDS.append(dict(KR=sb("KR%d" % i, [128, 8, 256], BF16, reg="A"), KkD=sb("KkD%d" % i, [128, 8, 128], BF16, reg="A"),
                            BtD=sb("BtD%d" % i, [128, 8, 128], BF16, reg="A"), KhD=sb("KhD%d" % i, [128, 8, 128], BF16, reg="A"),
                            BhD=sb("BhD%d" % i, [128, 8, 128], BF16, reg="A"), VD=sb("VD%d" % i, [128, 8, 128], BF16, reg="A"),
                            WC=sb("WC%d" % i, [128, 8], reg="A")))
        GC = 4
        Gm = sb("Gm", [128, GC, 512], BF16, reg="A")
        NTb = sb("NTb", [128, GC, 128], BF16, reg="A")
        ABb = [sb("AB%d" % i, [128, GC, 256], BF16, reg="A") for i in range(2)]
        Pb = [sb("Pb%d" % i, [128, GC, 128], BF16, reg="A") for i in range(2)]
        VDT = sb("VDT", [128, GC, 128], BF16, reg="A")
        KhT = sb("KhT", [128, GC, 128], BF16, reg="A")
        BhT = sb("BhT", [128, GC, 128], BF16, reg="A")
        ZT = sb("ZT", [128, 128], BF16, reg="A")
        nUT = sb("nUT", [128, 128], BF16, reg="A")
        S0bf = [sb("S0bf%d" % i, [128, 128], BF16, reg="A") for i in range(2)]
        ST = [sb("ST%d" % hp, [128, 128], reg="A") for hp in range(8)]
        stg = [sb("stg%d" % i, [128, 128], reg="A") for i in range(2)]
        odb = sb("odb", [128, 8, 128], BF16, reg="A")
        s0ctr = [0]
        stgctr = [0]

        def layer_params(l):
            S.op("pool", lambda e: e.dma_start(out=w2b.t[:], in_=w2_in[l]), writes=[w2b.r], dma=True)
            S.op("pool", lambda e: e.dma_start(out=a2b.t[:], in_=a2_in[l]), writes=[a2b.r], dma=True)
            S.op("pool", lambda e: e.dma_start(out=g2b0.t[:], in_=g2_in[l, 0:128, :]), writes=[g2b0.r], dma=True)
            S.op("pool", lambda e: e.dma_start(out=g2b1.t[:], in_=g2_in[l, 128:160, :]), writes=[g2b1.r], dma=True)
            if NS:
                S.op("act", lambda e: e.dma_start(out=tshs.t[:], in_=tsh_in[l]), writes=[tshs.r], dma=True)

        pctr = [0]

        def p_chunk(l, T, blk, blkres, off, rows, cid, out_fn, out_res, post=None):
            w, Lq, nseq = T.w, T.Lq, T.nseq
            pb = gbank()
            for kc in range(16):
                S.op("pe", lambda e, kc=kc: e.matmul(pb.t[0:rows, 0:w], blk[:, kc, off:off + rows], hT.t[:, kc, 0:w],
                                                     start=(kc == 0), stop=(kc == 15)),
                     reads=[blkres, hT.r], writes=[pb.r])
            P = pbufs[pctr[0] % 2]
            dt_ = dtmp[pctr[0] % 2]
            pctr[0] += 1
            pv3 = P.t[0:rows, 0:nseq * (Lq + 1)].rearrange("p (s l) -> p s l", s=nseq)
            S.op("act", lambda e: e.activation(out=pv3[:, :, 1:Lq + 1], in_=pb.t[0:rows, 0:w].rearrange("p (s l) -> p s l", s=nseq),
                                               func=AF.Copy), reads=[pb.r], writes=[P.r])
            if T.kind == "P":
                if T.first:
                    ph = PSB[3]
                    for kc in range(16):
                        S.op("pe", lambda e, kc=kc: e.matmul(ph.t[0:rows, cid:cid + 1], blk[:, kc, off:off + rows], hprev.t[:, kc:kc + 1],
                                                             start=(kc == 0), stop=(kc == 15)),
                             reads=[blkres, hprev.r], writes=[ph.r])
                    S.op("act", lambda e: e.activation(out=pv3[:, 0, 0:1], in_=ph.t[0:rows, cid:cid + 1], func=AF.Copy),
                         reads=[ph.r], writes=[P.r])
                else:
                    S.op("act", lambda e: e.activation(out=pv3[:, 0, 0:1], in_=plast.t[0:rows, cid, 0:1], func=AF.Copy),
                         reads=[plast.r], writes=[P.r])
                S.op("act", lambda e: e.activation(out=plast.t[0:rows, cid, 0:1], in_=pv3[:, 0, Lq:Lq + 1], func=AF.Copy),
                     reads=[P.r], writes=[plast.r])
            else:
                S.op("act", lambda e: e.activation(out=pv3[:, :, 0], in_=tshs.t[0:rows, cid, 0:nseq], func=AF.Copy),
                     reads=[tshs.r], writes=[P.r])
                S.op("act", lambda e: e.activation(out=plast.t[0:rows, cid, 1:1 + nseq], in_=pv3[:, :, Lq], func=AF.Copy),
                     reads=[P.r], writes=[plast.r])
            d3 = dt_.t[0:rows, 0:w].rearrange("p (s l) -> p s l", s=nseq)
            S.op("dve", lambda e: e.tensor_tensor(out=d3, in0=pv3[:, :, 0:Lq], in1=pv3[:, :, 1:Lq + 1], op=ALU.subtract),
                 reads=[P.r], writes=[dt_.r])
            if post is None:
                o3 = out_fn().rearrange("p (s l) -> p s l", s=nseq)
                S.op("dve", lambda e: e.scalar_tensor_tensor(out=o3, in0=d3, scalar=pv("mu", l, cid, rows), in1=pv3[:, :, 1:Lq + 1],
                                                             op0=ALU.mult, op1=ALU.add),
                     reads=[dt_.r, P.r, pvec.r], writes=[out_res])
            else:
                l3 = ltmp.t[0:rows, 0:w].rearrange("p (s l) -> p s l", s=nseq)
                S.op("dve", lambda e: e.scalar_tensor_tensor(out=l3, in0=d3, scalar=pv("mu", l, cid, rows), in1=pv3[:, :, 1:Lq + 1],
                                                             op0=ALU.mult, op1=ALU.add),
                     reads=[dt_.r, P.r, pvec.r], writes=[ltmp.r])
                post(ltmp.t[0:rows, 0:w])

        def bd4(t, w):
            nch = w // 64
            return t[:, 0:w].rearrange("p (c j) -> p c j", j=64).unsqueeze(2).broadcast_to([128, nch, 2, 64])

        def prep(l, T, hp, rb, kb, vb):
            w, nch = T.w, T.nch
            bs_ = BDS[hp % 2]
            KR, KkD, BtD, KhD, BhD, VD, WC = (bs_[k_] for k_ in ("KR", "KkD", "BtD", "KhD", "BhD", "VD", "WC"))
            hc = slice(hp * 128, (hp + 1) * 128)
            a = A
            pb = gbank()
            S.op("pe", lambda e: e.matmul(pb.t[:, 0:w], w2b.t[0:64, hc], txw.t[0:64, 0:w], start=True, stop=True),
                 reads=[w2b.r, txw.r], writes=[pb.r])
            S.op("act", lambda e: e.activation(out=a["sw"].t[:, 0:w], in_=pb.t[:, 0:w], func=AF.Sigmoid, bias=pv("w0", l, hp)),
                 reads=[pb.r, pvec.r], writes=[a["sw"].r])
            pb2 = gbank()
            S.op("pe", lambda e: e.matmul(pb2.t[:, 0:w], a2b.t[0:64, hc], xab.t[0:64, 0:w], start=True, stop=True),
                 reads=[a2b.r, xab.r], writes=[pb2.r])
            S.op("act", lambda e: e.activation(out=a["aa"].t[:, 0:w], in_=pb2.t[:, 0:w], func=AF.Sigmoid, bias=pv("a0", l, hp)),
                 reads=[pb2.r, pvec.r], writes=[a["aa"].r])
            pb3 = gbank()
            S.op("pe", lambda e: e.matmul(pb3.t[:, 0:w], g2b0.t[:, hc], sxg0.t[:, 0:w], start=True, stop=False),
                 reads=[g2b0.r, sxg0.r], writes=[pb3.r])
            S.op("pe", lambda e: e.matmul(pb3.t[:, 0:w], g2b1.t[0:32, hc], sxg1.t[0:32, 0:w], start=False, stop=True),
                 reads=[g2b1.r, sxg1.r], writes=[pb3.r])
            S.op("act", lambda e: e.activation(out=a["gg"].t[:, 0:w], in_=pb3.t[:, 0:w], func=AF.Copy),
                 reads=[pb3.r], writes=[a["gg"].r])
            S.op("dve", lambda e: e.tensor_tensor_scan(out=a["cs"].t[:, 0:w], data0=cf("cmask")[:, 0:w], data1=a["sw"].t[:, 0:w],
                                                       initial=0.0, op0=ALU.mult, op1=ALU.add),
                 reads=[a["sw"].r, cst.r], writes=[a["cs"].r])
            S.op("dve", lambda e: e.tensor_tensor(out=a["cse"].t[:, 0:w], in0=a["cs"].t[:, 0:w], in1=a["sw"].t[:, 0:w], op=ALU.subtract),
                 reads=[a["cs"].r, a["sw"].r], writes=[a["cse"].r])
            S.op("act", lambda e: e.activation(out=a["e1"].t[:, 0:w], in_=a["cs"].t[:, 0:w], func=AF.Exp, scale=-C0),
                 reads=[a["cs"].r], writes=[a["e1"].r])
            S.op("act", lambda e: e.activation(out=a["e2"].t[:, 0:w], in_=a["cs"].t[:, 0:w], func=AF.Exp, scale=C0),
                 reads=[a["cs"].r], writes=[a["e2"].r])
            S.op("act", lambda e: e.activation(out=a["e3"].t[:, 0:w], in_=a["cse"].t[:, 0:w], func=AF.Exp, scale=-C0),
                 reads=[a["cse"].r], writes=[a["e3"].r])
            S.op("dve", lambda e: e.tensor_copy(out=WC.t[:, 0:nch], in_=a["e1"].t[:, 0:w].rearrange("p (c j) -> p c j", j=64)[:, :, 63]),
                 reads=[a["e1"].r], writes=[WC.r])
            mbd = cf("bones").rearrange("p (h j) -> p h j", h=2).unsqueeze(1).broadcast_to([128, nch, 2, 64])

            def v4(buf, n=1024):
                return buf.t[:, 0:nch * 128].rearrange("p (c h j) -> p c h j", h=2, j=64)

            for i, src in enumerate(("e1", "e2", "e3")):
                eng = "pool" if i == 1 else "dve"
                S.op(eng, lambda e, i=i, src=src: e.tensor_tensor(out=v4(eBD[i]), in0=bd4(a[src].t, w), in1=mbd, op=ALU.mult),
                     reads=[a[src].r, cst.r], writes=[eBD[i].r])
            S.op("dve", lambda e: e.tensor_scalar(out=a["kk"].t[:, 0:w], in0=kb.t[:, 0:w], scalar1=pv("k_k", l, hp), scalar2=None, op0=ALU.mult),
                 reads=[kb.r, pvec.r], writes=[a["kk"].r])
            S.op("act", lambda e: e.activation(out=a["sqk"].t[:, 0:w], in_=a["kk"].t[:, 0:w], func=AF.Square),
                 reads=[a["kk"].r], writes=[a["sqk"].r])
            pb4 = gbank()
            S.op("pe", lambda e: e.matmul(pb4.t[:, 0:w], cb("bones"), a["sqk"].t[:, 0:w], start=True, stop=True),
                 reads=[cstb.r, a["sqk"].r], writes=[pb4.r])
            S.op("dve", lambda e: e.tensor_scalar(out=a["t1"].t[:, 0:w], in0=pb4.t[:, 0:w], scalar1=1e-24, scalar2=None, op0=ALU.max),
                 reads=[pb4.r], writes=[a["t1"].r])
            S.op("act", lambda e: e.activation(out=a["t1"].t[:, 0:w], in_=a["t1"].t[:, 0:w], func=AF.Sqrt), reads=[a["t1"].r], writes=[a["t1"].r])
            S.op("dve", lambda e: e.reciprocal(out=a["t1"].t[:, 0:w], in_=a["t1"].t[:, 0:w]), reads=[a["t1"].r], writes=[a["t1"].r])
            S.op("dve", lambda e: e.tensor_tensor(out=a["kkn"].t[:, 0:w], in0=a["kk"].t[:, 0:w], in1=a["t1"].t[:, 0:w], op=ALU.mult),
                 reads=[a["kk"].r, a["t1"].r], writes=[a["kkn"].r])
            S.op("dve", lambda e: e.tensor_scalar(out=a["tmp"].t[:, 0:w], in0=a["aa"].t[:, 0:w], scalar1=pv("k_a", l, hp),
                                                  scalar2=omk.t[:, l * 8 + hp:l * 8 + hp + 1], op0=ALU.mult, op1=ALU.add),
                 reads=[a["aa"].r, pvec.r, omk.r], writes=[a["tmp"].r])
            S.op("dve", lambda e: e.tensor_tensor(out=a["kh"].t[:, 0:w], in0=kb.t[:, 0:w], in1=a["tmp"].t[:, 0:w], op=ALU.mult),
                 reads=[kb.r, a["tmp"].r], writes=[a["kh"].r])
            S.op("dve", lambda e: e.tensor_tensor(out=a["beta"].t[:, 0:w], in0=a["kkn"].t[:, 0:w], in1=a["aa"].t[:, 0:w], op=ALU.mult),
                 reads=[a["kkn"].r, a["aa"].r], writes=[a["beta"].r])
            S.op("dve", lambda e: e.scalar_tensor_tensor(out=a["rkk"].t[:, 0:w], in0=rb.t[:, 0:w], scalar=pv("r_k", l, hp), in1=a["kh"].t[:, 0:w],
                                                         op0=ALU.mult, op1=ALU.mult),
                 reads=[rb.r, a["kh"].r, pvec.r], writes=[a["rkk"].r])
            pb5 = gbank()
            S.op("pe", lambda e: e.matmul(pb5.t[:, 0:w], cb("bones"), a["rkk"].t[:, 0:w], start=True, stop=True),
                 reads=[cstb.r, a["rkk"].r], writes=[pb5.r])
            S.op("dve", lambda e: e.tensor_tensor(out=a["bg"].t[:, 0:w], in0=pb5.t[:, 0:w], in1=vb.t[:, 0:w], op=ALU.mult),
                 reads=[pb5.r, vb.r], writes=[a["bg"].r])
            S.op("dve", lambda e: e.tensor_tensor(out=a["bg"].t[:, 0:w], in0=a["bg"].t[:, 0:w], in1=a["gg"].t[:, 0:w], op=ALU.mult),
                 reads=[a["bg"].r, a["gg"].r], writes=[a["bg"].r])
            if l == 0 and T.idx == 0 and hp == 0:
                for nm in ("sw", "aa", "cs", "e1", "e2", "e3", "kkn", "kh", "beta", "gg"):
                    dump(a[nm], a[nm].t[:, 0:w], 128, w)
                dump(rb, rb.t[:, 0:w], 128, w)
                dump(vb, vb.t[:, 0:w], 128, w)
            kr4 = KR.t[:, 0:nch, :].rearrange("p c (s h j) -> p c s h j", s=2, h=2)
            S.op("dve", lambda e: e.tensor_tensor(out=kr4[:, :, 1], in0=bd4(rb.t, w), in1=v4(eBD[0]), op=ALU.mult),
                 reads=[rb.r, eBD[0].r], writes=[KR.r])
            S.op("pool", lambda e: e.tensor_tensor(out=kr4[:, :, 0], in0=bd4(a["kkn"].t, w), in1=v4(eBD[2]), op=ALU.mult),
                 reads=[a["kkn"].r, eBD[2].r, KR.r], writes=[KR.r])

            def t4(buf):
                return buf.t[:, 0:nch, :].rearrange("p c (h j) -> p c h j", h=2)

            S.op("dve", lambda e: e.tensor_tensor(out=t4(KkD), in0=bd4(a["kh"].t, w), in1=v4(eBD[1]), op=ALU.mult),
                 reads=[a["kh"].r, eBD[1].r], writes=[KkD.r])
            S.op("pool", lambda e: e.tensor_tensor(out=t4(BtD), in0=bd4(a["beta"].t, w), in1=v4(eBD[1]), op=ALU.mult),
                 reads=[a["beta"].r, eBD[1].r], writes=[BtD.r])
            wcb = WC.t[:, 0:nch].unsqueeze(2).broadcast_to([128, nch, 128])
            S.op("dve", lambda e: e.tensor_tensor(out=KhD.t[:, 0:nch, :], in0=KkD.t[:, 0:nch, :], in1=wcb, op=ALU.mult),
                 reads=[KkD.r, WC.r], writes=[KhD.r])
            S.op("pool", lambda e: e.tensor_tensor(out=BhD.t[:, 0:nch, :], in0=BtD.t[:, 0:nch, :], in1=wcb, op=ALU.mult),
                 reads=[BtD.r, WC.r], writes=[BhD.r])
            S.op("pool", lambda e: e.tensor_tensor(out=t4(VD), in0=bd4(vb.t, w), in1=mbd, op=ALU.mult),
                 reads=[vb.r, cst.r], writes=[VD.r])

        def scan(l, T, hp):
            w, nch = T.w, T.nch
            bs_ = BDS[hp % 2]
            KR, KkD, BtD, KhD, BhD, VD, WC = (bs_[k_] for k_ in ("KR", "KkD", "BtD", "KhD", "BhD", "VD", "WC"))
            Q = PSB[4:8]
            st_ = ST[hp]
            for g0 in range(0, nch, GC):
                n = min(GC, nch - g0)
                for ci in range(n):
                    c = g0 + ci
                    q = Q[ci % 2]
                    S.op("pe", lambda e, c=c, q=q: e.matmul(q.t[:, 0:256], BtD.t[:, c, :], KR.t[:, c, :], start=True, stop=True),
                         reads=[BtD.r, KR.r], writes=[q.r])
                    S.op("pe", lambda e, c=c, q=q: e.matmul(q.t[:, 256:512], KkD.t[:, c, :], KR.t[:, c, :], start=True, stop=True),
                         reads=[KkD.r, KR.r], writes=[q.r])
                    S.op("dve", lambda e, ci=ci, q=q: e.tensor_tensor(out=Gm.t[:, ci, :], in0=q.t[:, :], in1=cb("mgram"), op=ALU.mult),
                         reads=[q.r, cstb.r], writes=[Gm.r])
                    S.op("pe", lambda e, c=c, ci=ci: e.matmul(Q[2].t[:, ci * 128:(ci + 1) * 128], KR.t[:, c, 0:128], BtD.t[:, c, :], start=True, stop=True),
                         reads=[BtD.r, KR.r], writes=[Q[2].r])
                mslb = cb("msl").unsqueeze(1).broadcast_to([128, n, 128])
                S.op("dve", lambda e: e.tensor_tensor(out=NTb.t[:, 0:n, :], in0=Q[2].t[:, 0:n * 128].rearrange("p (c j) -> p c j", j=128),
                                                      in1=mslb, op=ALU.mult), reads=[Q[2].r, cstb.r], writes=[NTb.r])
                idb = cb("ident").unsqueeze(1).broadcast_to([128, n, 128])
                S.op("pool", lambda e: e.tensor_tensor(out=Pb[0].t[:, 0:n, :], in0=idb, in1=Gm.t[:, 0:n, 0:128], op=ALU.subtract),
                     reads=[Gm.r, cstb.r], writes=[Pb[0].r])
                for k in range(1, 6):
                    ab_o = ABb[k % 2]
                    ab_i = ABb[(k - 1) % 2]
                    p_o, p_i = Pb[k % 2], Pb[(k - 1) % 2]
                    for ci in range(n):
                        q = Q[ci // 2]
                        co = (ci % 2) * 256
                        if k == 1:
                            Ai = Gm.t[:, ci, 0:128]
                            Bi = NTb.t[:, ci, :]
                            rdi = [Gm.r, NTb.r]
                        else:
                            Ai = ab_i.t[:, ci, 0:128]
                            Bi = ab_i.t[:, ci, 128:256]
                            rdi = [ab_i.r]
                        if k < 5:
                            S.op("pe", lambda e, q=q, co=co, Ai=Ai, Bi=Bi: e.matmul(q.t[:, co:co + 128], Bi, Ai, start=True, stop=True),
                                 reads=rdi, writes=[q.r])
                        S.op("pe", lambda e, q=q, co=co, Ai=Ai, Bi=Bi: e.matmul(q.t[:, co + 128:co + 256], Ai, Bi, start=True, stop=True),
                             reads=rdi, writes=[q.r])
                    for qi in range((n + 1) // 2):
                        m = min(2, n - 2 * qi)
                        eng = "act" if qi == 0 else "dve"
                        copy_op(eng, ab_o.t[:, 2 * qi:2 * qi + m, :], Q[qi].t[:, 0:m * 256].rearrange("p (c j) -> p c j", j=256),
                                [Q[qi].r], [ab_o.r])
                    for ci in range(n):
                        S.op("pe", lambda e, ci=ci: e.matmul(Q[2].t[:, ci * 128:(ci + 1) * 128], ab_o.t[:, ci, 128:256], p_i.t[:, ci, :],
                                                             start=True, stop=True),
                             reads=[ab_o.r, p_i.r], writes=[Q[2].r])
                    if l == 0 and T.idx == 0 and hp == 0 and g0 == 0 and k == 1:
                        dump(NTb, NTb.t[:, 0, :], 128, 128)
                        dump(ab_o, ab_o.t[:, 0, :], 128, 256)
                        dump(Q[2], Q[2].t[:, 0:128], 128, 128)
                        dump(p_i, p_i.t[:, 0, :], 128, 128)
                    S.op("dve", lambda e, p_o=p_o, p_i=p_i: e.tensor_tensor(out=p_o.t[:, 0:n, :], in0=Q[2].t[:, 0:n * 128].rearrange("p (c j) -> p c j", j=128),
                                                                            in1=p_i.t[:, 0:n, :], op=ALU.add),
                         reads=[Q[2].r, p_i.r], writes=[p_o.r])
                    if l == 0 and T.idx == 0 and hp == 0 and g0 == 0 and k == 1:
                        dump(p_o, p_o.t[:, 0, :], 128, 128)
                Tm = Pb[1]
                if l == 0 and T.idx == 0 and hp == 0 and g0 == 0:
                    dump(Gm, Gm.t[:, 0, :], 128, 512)
                    dump(Tm, Tm.t[:, 0, :], 128, 128)
                for (src, dst, q, eng) in ((VD, VDT, Q[0], "act"), (KhD, KhT, Q[1], "dve"), (BhD, BhT, Q[3], "act")):
                    for ci in range(n):
                        c = g0 + ci
                        S.op("pe", lambda e, src=src, q=q, c=c, ci=ci: e.matmul(q.t[:, ci * 128:(ci + 1) * 128], src.t[:, c, :], cb("ident"),
                                                                                start=True, stop=True),
                             reads=[src.r, cstb.r], writes=[q.r])
                    copy_op(eng, dst.t[:, 0:n, :], q.t[:, 0:n * 128].rearrange("p (c j) -> p c j", j=128), [q.r], [dst.r])
                for ci in range(n):
                    c = g0 + ci
                    if T.kind == "S" or (T.first and c == 0):
                        if T.kind == "S":
                            sg = stg[stgctr[0] % 2]
                            stgctr[0] += 1
                            S.op("act", lambda e, sg=sg, c=c: e.dma_start(out=sg.t[:], in_=wkv_in[l, c, hp]), writes=[sg.r], dma=True)
                            S.op("dve", lambda e, sg=sg: e.tensor_tensor(out=st_.t[:], in0=sg.t[:], in1=cf("J"), op=ALU.add),
                                 reads=[sg.r, cst.r], writes=[st_.r])
                        else:
                            S.op("dve", lambda e: e.tensor_copy(out=st_.t[:], in_=cf("J")), reads=[cst.r], writes=[st_.r])
                    s0 = S0bf[s0ctr[0] % 2]
                    s0ctr[0] += 1
                    S.op("act", lambda e, s0=s0: e.activation(out=s0.t[:], in_=st_.t[:], func=AF.Copy), reads=[st_.r], writes=[s0.r])
                    S.op("pe", lambda e, c=c, s0=s0: e.matmul(Q[2].t[:, 0:128], KR.t[:, c, 0:128], s0.t[:], start=True, stop=False),
                         reads=[KR.r, s0.r], writes=[Q[2].r])
                    S.op("pe", lambda e, ci=ci: e.matmul(Q[2].t[:, 0:128], Gm.t[:, ci, 256:384], VDT.t[:, ci, :], start=False, stop=True),
                         reads=[Gm.r, VDT.r], writes=[Q[2].r])
                    S.op("act", lambda e: e.activation(out=ZT.t[:], in_=Q[2].t[:, 0:128], func=AF.Copy), reads=[Q[2].r], writes=[ZT.r])
                    S.op("pe", lambda e, ci=ci: e.matmul(Q[2].t[:, 128:256], Tm.t[:, ci, :], ZT.t[:], start=True, stop=True),
                         reads=[Tm.r, ZT.r], writes=[Q[2].r])
                    S.op("dve", lambda e: e.tensor_scalar(out=nUT.t[:], in0=Q[2].t[:, 128:256], scalar1=-1.0, scalar2=None, op0=ALU.mult),
                         reads=[Q[2].r], writes=[nUT.r])
                    osl = slice(ci * 128, (ci + 1) * 128)
                    S.op("pe", lambda e, c=c, s0=s0, osl=osl: e.matmul(Q[3].t[:, osl], s0.t[:], KR.t[:, c, 128:256], start=True, stop=False),
                         reads=[KR.r, s0.r], writes=[Q[3].r])
                    S.op("pe", lambda e, ci=ci, osl=osl: e.matmul(Q[3].t[:, osl], VDT.t[:, ci, :], Gm.t[:, ci, 384:512], start=False, stop=False),
                         reads=[Gm.r, VDT.r], writes=[Q[3].r])
                    S.op("pe", lambda e, ci=ci, osl=osl: e.matmul(Q[3].t[:, osl], nUT.t[:], Gm.t[:, ci, 128:256], start=False, stop=True),
                         reads=[Gm.r, nUT.r], writes=[Q[3].r])
                    S.op("pe", lambda e, ci=ci: e.matmul(Q[2].t[:, 256:384], KhT.t[:, ci, :], VDT.t[:, ci, :], start=True, stop=False),
                         reads=[KhT.r, VDT.r], writes=[Q[2].r])
                    S.op("pe", lambda e, ci=ci: e.matmul(Q[2].t[:, 256:384], BhT.t[:, ci, :], nUT.t[:], start=False, stop=True),
                         reads=[BhT.r, nUT.r], writes=[Q[2].r])
                    S.op("dve", lambda e, c=c: e.scalar_tensor_tensor(out=st_.t[:], in0=st_.t[:], scalar=WC.t[:, c:c + 1], in1=Q[2].t[:, 256:384],
                                                                      op0=ALU.mult, op1=ALU.add),
                         reads=[st_.r, WC.r, Q[2].r], writes=[st_.r])
                    if l == 0 and T.idx == 0 and hp == 0 and c == 0:
                        dump(st_, st_.t[:], 128, 128)
                    if T.kind == "S":
                        S.op("act", lambda e, c=c: e.dma_start(out=wkv_out[l, 1 + c, hp], in_=st_.t[:]), reads=[st_.r], dma=True)
                    elif T.last and c == nch - 1:
                        S.op("act", lambda e: e.dma_start(out=wkv_out[l, 0, hp], in_=st_.t[:]), reads=[st_.r], dma=True)
                for h in range(2):
                    hs = slice(64 * h, 64 * h + 64)
                    src = Q[3].t[hs, 0:n * 128].rearrange("p (c j) -> p c j", j=128)[:, :, 64 * h:64 * h + 64]
                    dst = A["osp"].t[hs, g0 * 64:(g0 + n) * 64].rearrange("p (c j) -> p c j", j=64)
                    copy_op("act" if h == 0 else "dve", dst, src, [Q[3].r], [A["osp"].r])
                copy_op("dve", odb.t[:, g0:g0 + n, :], Q[3].t[:, 0:n * 128].rearrange("p (c j) -> p c j", j=128), [Q[3].r], [odb.r])

        def spill(T, hp, part):
            w, nch = T.w, T.nch
            i = T.idx
            if part == 0:
                S.op("act", lambda e: e.dma_start(out=bg_sp[i, hp, :, 0:w], in_=A["bg"].t[:, 0:w]), reads=[A["bg"].r], writes=[spres[(i, hp)]], dma=True)
                S.op("act", lambda e: e.dma_start(out=g_sp[i, hp, :, 0:w], in_=A["gg"].t[:, 0:w]), reads=[A["gg"].r], writes=[spres[(i, hp)]], dma=True)
            else:
                S.op("act", lambda e: e.dma_start(out=o_sp[i, hp, :, 0:w], in_=A["osp"].t[:, 0:w]), reads=[A["osp"].r], writes=[spres[(i, hp)]], dma=True)
                S.op("act", lambda e: e.dma_start(out=od_sp[i, hp, :, 0:nch * 128], in_=odb.t[:, 0:nch, :]), reads=[odb.r], writes=[spres[(i, hp)]], dma=True)

        spres = {}

        def phaseA(l, T):
            w = T.w
            xsrc = (xT_in if l == 0 else x1_scr).rearrange("(c p) t -> p c t", p=128)
            pbn = gbank()
            for ps_ in range(2):
                for gq in range(8):
                    xs_ = xstg[gq % 2]
                    rd = [] if l == 0 else [x1res[(T.idx, 2 * gq)], x1res[(T.idx, 2 * gq + 1)]]
                    S.op("act", lambda e, gq=gq, xs_=xs_: e.dma_start(out=xs_.t[:, :, 0:w], in_=xsrc[:, 2 * gq:2 * gq + 2, T.c0:T.c0 + w]),
                         reads=rd, writes=[xs_.r], dma=True)
                    for cc in range(2):
                        c = 2 * gq + cc
                        if ps_ == 0:
                            s_ = sq[c % 4]
                            S.op("act", lambda e, cc=cc, s_=s_, xs_=xs_: e.activation(out=s_.t[:, 0:w], in_=xs_.t[:, cc, 0:w], func=AF.Square),
                                 reads=[xs_.r], writes=[s_.r])
                            S.op("pe", lambda e, c=c, s_=s_: e.matmul(pbn.t[:, 0:w], cb("ones"), s_.t[:, 0:w], start=(c == 0), stop=(c == 15)),
                                 reads=[s_.r, cstb.r], writes=[pbn.r])
                        else:
                            S.op("dve", lambda e, c=c, cc=cc, xs_=xs_: e.scalar_tensor_tensor(out=hT.t[:, c, 0:w], in0=xs_.t[:, cc, 0:w], scalar=pv("norm1", l, c),
                                                                                             in1=rstd.t[:, 0:w], op0=ALU.mult, op1=ALU.mult),
                                 reads=[xs_.r, rstd.r, pvec.r], writes=[hT.r])
                if ps_ == 0:
                    S.op("dve", lambda e: e.tensor_scalar(out=rstd.t[:, 0:w], in0=pbn.t[:, 0:w], scalar1=1.0 / D, scalar2=RMS_EPS,
                                                          op0=ALU.mult, op1=ALU.add), reads=[pbn.r], writes=[rstd.r])
                    S.op("act", lambda e: e.activation(out=rstd.t[:, 0:w], in_=rstd.t[:, 0:w], func=AF.Sqrt), reads=[rstd.r], writes=[rstd.r])
                    S.op("dve", lambda e: e.reciprocal(out=rstd.t[:, 0:w], in_=rstd.t[:, 0:w]), reads=[rstd.r], writes=[rstd.r])
            if T.first:
                if l == 0:
                    S.op("act", lambda e: e.dma_start(out=xprev.t[:, 0:16], in_=xprev_in), writes=[xprev.r], dma=True)
                S.op("act", lambda e: e.activation(out=sq[0].t[:, 0:16], in_=xprev.t[:, 0:16], func=AF.Square), reads=[xprev.r], writes=[sq[0].r])
                pbk = gbank()
                S.op("pe", lambda e: e.matmul(pbk.t[:, 0:16], cb("ones"), sq[0].t[:, 0:16], start=True, stop=True), reads=[sq[0].r, cstb.r], writes=[pbk.r])
                S.op("dve", lambda e: e.reduce_sum(out=xprev.t[:, 16:17], in_=pbk.t[:, 0:16], axis=mybir.AxisListType.X), reads=[pbk.r], writes=[xprev.r])
                S.op("dve", lambda e: e.tensor_scalar(out=xprev.t[:, 16:17], in0=xprev.t[:, 16:17], scalar1=1.0 / D, scalar2=RMS_EPS, op0=ALU.mult, op1=ALU.add),
                     reads=[xprev.r], writes=[xprev.r])
                S.op("act", lambda e: e.activation(out=xprev.t[:, 16:17], in_=xprev.t[:, 16:17], func=AF.Sqrt), reads=[xprev.r], writes=[xprev.r])
                S.op("dve", lambda e: e.reciprocal(out=xprev.t[:, 16:17], in_=xprev.t[:, 16:17]), reads=[xprev.r], writes=[xprev.r])
                n1 = PV[("norm1", l)]
                S.op("dve", lambda e: e.tensor_tensor(out=xprev.t[:, 20:36], in0=xprev.t[:, 0:16], in1=pvec.t[:, n1:n1 + 16], op=ALU.mult),
                     reads=[xprev.r, pvec.r], writes=[xprev.r])
                S.op("dve", lambda e: e.tensor_scalar(out=hprev.t[:, 0:16], in0=xprev.t[:, 20:36], scalar1=xprev.t[:, 16:17], scalar2=None, op0=ALU.mult),
                     reads=[xprev.r], writes=[hprev.r])
            blk, br = wget(l, "A0")
            outs = [(txw, AF.Tanh), (xab, AF.Copy), (sxg0, AF.Sigmoid), (sxg1, AF.Sigmoid)]
            for (cid, off, rows), (ob, fn) in zip(LORA, outs):
                def post(xs, ob=ob, fn=fn, rows=rows):
                    S.op("act", lambda e: e.activation(out=ob.t[0:rows, 0:w], in_=xs, func=fn), reads=[ltmp.r], writes=[ob.r])
                p_chunk(l, T, blk, br, off - 3072, rows, cid, None, None, post=post)
            def front(hp):
                blk, br = wget(l, "Ahp%d" % hp)
                bufs = [RKV[j][hp % 2] for j in range(3)]
                for j in range(3):
                    p_chunk(l, T, blk, br, 128 * j, 128, j * 8 + hp, lambda j=j: bufs[j].t[:, 0:w], bufs[j].r)
                prep(l, T, hp, *bufs)
                spres[(T.idx, hp)] = Res("sp")
                spill(T, hp, 0)

            def back(hp):
                scan(l, T, hp)
                spill(T, hp, 1)

            S.replay(S.capture(lambda: front(0)))
            for hp in range(8):
                sb_ = S.capture(lambda: back(hp))
                sf_ = S.capture(lambda: front(hp + 1)) if hp < 7 else []
                S.replay(sb_, sf_)

        uya = sb("uya", [128, 16, 512], BF16, reg="C")
        uT = Buf(uya.t[:, 0:8, :], "uT")
        ya = Buf(uya.t[:, 8:16, :], "ya")
        uT.r = uya.r
        ya.r = uya.r
        hid = uya
        yb = sb("yb", [128, 8, 512], BF16, reg="C")
        vg = sb("vg", [128, 4, 1024], reg="C")
        vnb = [sb("vnb%d" % i, [128, 1024], BF16, reg="C") for i in range(2)]
        mixT = sb("mixT", [128, 16, 512], BF16, reg="C")
        lng = sb("lng", [128, DA], reg="C")
        lnb = sb("lnb", [128, DA], reg="C")
        bsf = Buf(vg.t[0:1, 0, :], "bsf")
        bsf.r = vg.r
        bhf = Buf(vg.t[0:1, 1, :], "bhf")
        bhf.r = vg.r
        bhi = sb("bhi", [1, 1024], BF16, reg="C")
        blo = sb("blo", [1, 1024], BF16, reg="C")
        wsb = sb("wsb", [128, 8, 128], BF16, reg="C")
        wsb2 = sb("wsb2", [128, 8, 64], BF16, reg="C")
        bnst = sb("bnst", [128, 2, 6], reg="C")
        mv = sb("mv", [128, 2], reg="C")
        Ct = [sb("Ct%d" % i, [128, 512], reg="C") for i in range(6)]
        ob_ = sb("Co", [128, 512], reg="C")
        odl = None
        bgl = sb("Cbg", [128, 512], reg="C")
        ggl = sb("Cgg", [128, 512], reg="C")
        sqb = sb("sqb", [128, 512], BF16, reg="C")
        obf = sb("obf", [128, 512], BF16, reg="C")

        def layer_params_C(l):
            S.op("act", lambda e: e.dma_start(out=lng.t[:], in_=lng_in[l:l + 1, :].partition_broadcast(128)), writes=[lng.r], dma=True)
            S.op("act", lambda e: e.dma_start(out=lnb.t[:], in_=lnb_in[l:l + 1, :].partition_broadcast(128)), writes=[lnb.r], dma=True)
            S.op("act", lambda e: e.dma_start(out=bsf.t[0:1, :], in_=bs_in[l]), writes=[bsf.r], dma=True)
            S.op("dve", lambda e: e.tensor_copy(out=bhi.t[0:1, :], in_=bsf.t[0:1, :]), reads=[bsf.r], writes=[bhi.r])
            S.op("dve", lambda e: e.tensor_copy(out=bhf.t[0:1, :], in_=bhi.t[0:1, :]), reads=[bhi.r], writes=[bhf.r])
            S.op("dve", lambda e: e.tensor_tensor(out=blo.t[0:1, :], in0=bsf.t[0:1, :], in1=bhf.t[0:1, :], op=ALU.subtract),
                 reads=[bsf.r, bhf.r], writes=[blo.r])
            S.op("pool", lambda e: e.dma_start(out=wsb.t[:], in_=wsT_in[l]), writes=[wsb.r], dma=True)
            trb = cb("triu").unsqueeze(1).broadcast_to([128, 8, 128])
            S.op("dve", lambda e: e.tensor_tensor(out=wsb.t[:], in0=wsb.t[:], in1=trb, op=ALU.mult), reads=[wsb.r, cstb.r], writes=[wsb.r])
            S.op("pool", lambda e: e.dma_start(out=wsb2.t[64:128, :, :], in_=wsT_in[l, 0:64, :, 0:64]), writes=[wsb2.r], dma=True)
            trb2 = cb("triu")[64:128, 64:128].unsqueeze(1).broadcast_to([64, 8, 64])
            S.op("dve", lambda e: e.tensor_tensor(out=wsb2.t[64:128, :, :], in0=wsb2.t[64:128, :, :], in1=trb2, op=ALU.mult),
                 reads=[wsb2.r, cstb.r], writes=[wsb2.r])

        def gemm_fm(blk, blkres, moff, rhs_fn, nk, rhsres, w):
            pb = gbank()
            for kc in range(nk):
                S.op("pe", lambda e, kc=kc: e.matmul(pb.t[:, 0:w], blk[:, kc, moff:moff + 128], rhs_fn(kc), start=(kc == 0), stop=(kc == nk - 1)),
                     reads=[blkres] + rhsres, writes=[pb.r])
            return pb

        def phaseC(l, T, last_layer):
            w, nseq = T.w, T.nseq
            load_x(l, T)
            rmsnorm(xTr, lambda c: xT.t[:, c, 0:w], w, "norm1", l, lambda c: hT.t[:, c, 0:w], [hT.r] * 16)
            ntb = (w + 127) // 128
            def post_hp(hp):
                i = T.idx
                o_, od_, bg_, gg_ = ob_, odl, bgl, ggl
                rs = [spres[(i, hp)]]
                S.op("act", lambda e, o_=o_: e.dma_start(out=o_.t[:, 0:w], in_=o_sp[i, hp, :, 0:w]), reads=rs, writes=[o_.r], dma=True)
                S.op("act", lambda e, bg_=bg_: e.dma_start(out=bg_.t[:, 0:w], in_=bg_sp[i, hp, :, 0:w]), reads=rs, writes=[bg_.r], dma=True)
                S.op("act", lambda e, gg_=gg_: e.dma_start(out=gg_.t[:, 0:w], in_=g_sp[i, hp, :, 0:w]), reads=rs, writes=[gg_.r], dma=True)
                S.op("act", lambda e, o_=o_: e.activation(out=obf.t[:, 0:w], in_=o_.t[:, 0:w], func=AF.Copy), reads=[o_.r], writes=[obf.r])
                pb = gbank()
                S.op("pe", lambda e, pb=pb: e.matmul(pb.t[:, 0:w], cb("bones"), obf.t[:, 0:w], start=True, stop=True), reads=[cstb.r, obf.r], writes=[pb.r])
                oc = Ct[0]
                S.op("dve", lambda e, pb=pb, o_=o_: e.scalar_tensor_tensor(out=oc.t[:, 0:w], in0=pb.t[:, 0:w], scalar=-1.0 / 64, in1=o_.t[:, 0:w],
                                                                          op0=ALU.mult, op1=ALU.add), reads=[pb.r, o_.r], writes=[oc.r])
                S.op("act", lambda e: e.activation(out=sqb.t[:, 0:w], in_=oc.t[:, 0:w], func=AF.Square), reads=[oc.r], writes=[sqb.r])
                pb2 = gbank()
                S.op("pe", lambda e, pb2=pb2: e.matmul(pb2.t[:, 0:w], cb("bones"), sqb.t[:, 0:w], start=True, stop=True), reads=[cstb.r, sqb.r], writes=[pb2.r])
                rs_ = Ct[1]
                S.op("dve", lambda e, pb2=pb2: e.tensor_scalar(out=rs_.t[:, 0:w], in0=pb2.t[:, 0:w], scalar1=1.0 / 64, scalar2=GN_EPS, op0=ALU.mult, op1=ALU.add),
                     reads=[pb2.r], writes=[rs_.r])
                S.op("act", lambda e: e.activation(out=rs_.t[:, 0:w], in_=rs_.t[:, 0:w], func=AF.Sqrt), reads=[rs_.r], writes=[rs_.r])
                S.op("dve", lambda e: e.reciprocal(out=rs_.t[:, 0:w], in_=rs_.t[:, 0:w]), reads=[rs_.r], writes=[rs_.r])
                S.op("dve", lambda e: e.tensor_tensor(out=oc.t[:, 0:w], in0=oc.t[:, 0:w], in1=rs_.t[:, 0:w], op=ALU.mult), reads=[oc.r, rs_.r], writes=[oc.r])
                S.op("dve", lambda e, hp=hp: e.tensor_scalar(out=oc.t[:, 0:w], in0=oc.t[:, 0:w], scalar1=pv("gn_g", l, hp), scalar2=pv("gn_b", l, hp),
                                                             op0=ALU.mult, op1=ALU.add), reads=[oc.r, pvec.r], writes=[oc.r])
                S.op("dve", lambda e, gg_=gg_: e.tensor_tensor(out=oc.t[:, 0:w], in0=oc.t[:, 0:w], in1=gg_.t[:, 0:w], op=ALU.mult), reads=[oc.r, gg_.r], writes=[oc.r])
                S.op("dve", lambda e, bg_=bg_, hp=hp: e.tensor_tensor(out=yb.t[:, hp, 0:w], in0=oc.t[:, 0:w], in1=bg_.t[:, 0:w], op=ALU.add),
                     reads=[oc.r, bg_.r], writes=[yb.r])


            pending = list(range(8))
            for j in range(2):
                blk, br = wget(l, "Cv%d" % j)
                for tb in range(ntb):
                    pb = gbank()
                    for kc in range(16):
                        S.op("pe", lambda e, kc=kc, tb=tb, pb=pb, blk=blk: e.matmul(pb.t[:, 0:512], hT.t[:, kc, tb * 128:(tb + 1) * 128], blk[:, kc, 0:512],
                                                                                   start=(kc == 0), stop=(kc == 15)),
                             reads=[br, hT.r], writes=[pb.r])
                    S.op("act", lambda e, tb=tb, pb=pb, j=j: e.activation(out=vg.t[:, tb, j * 512:(j + 1) * 512], in_=pb.t[:, 0:512], func=AF.Gelu_apprx_tanh),
                         reads=[pb.r], writes=[vg.r])
                    if pending:
                        post_hp(pending.pop(0))
            for j in range(2):
                blk, br = wget(l, "Cu%d" % j)
                for m in range(4):
                    pb = gemm_fm(blk, br, m * 128, lambda kc: hT.t[:, kc, 0:w], 16, [hT.r], w)
                    S.op("act", lambda e, pb=pb, j=j, m=m: e.activation(out=uT.t[:, j * 4 + m, 0:w], in_=pb.t[:, 0:w], func=AF.Gelu_apprx_tanh),
                         reads=[pb.r], writes=[uT.r])
            while pending:
                post_hp(pending.pop(0))
            def ln_block(tb):
                vn_ = vnb[tb % 2]
                for j in range(2):
                    S.op("dve", lambda e, tb=tb, j=j: e.bn_stats(out=bnst.t[:, j, :], in_=vg.t[:, tb, j * 512:(j + 1) * 512]), reads=[vg.r], writes=[bnst.r])
                S.op("dve", lambda e: e.bn_aggr(out=mv.t[:], in_=bnst.t[:].rearrange("p a b -> p (a b)")), reads=[bnst.r], writes=[mv.r])
                S.op("dve", lambda e: e.tensor_scalar(out=mv.t[:, 1:2], in0=mv.t[:, 1:2], scalar1=LN_EPS, scalar2=None, op0=ALU.add), reads=[mv.r], writes=[mv.r])
                S.op("act", lambda e: e.activation(out=mv.t[:, 1:2], in_=mv.t[:, 1:2], func=AF.Sqrt), reads=[mv.r], writes=[mv.r])
                S.op("dve", lambda e: e.reciprocal(out=mv.t[:, 1:2], in_=mv.t[:, 1:2]), reads=[mv.r], writes=[mv.r])
                S.op("dve", lambda e, tb=tb: e.tensor_scalar(out=vg.t[:, tb, :], in0=vg.t[:, tb, :], scalar1=mv.t[:, 0:1], scalar2=mv.t[:, 1:2],
                                                             op0=ALU.subtract, op1=ALU.mult), reads=[vg.r, mv.r], writes=[vg.r])
                S.op("pool", lambda e, tb=tb: e.tensor_tensor(out=vg.t[:, tb, :], in0=vg.t[:, tb, :], in1=lng.t[:], op=ALU.mult), reads=[vg.r, lng.r], writes=[vg.r])
                S.op("dve", lambda e, tb=tb: e.tensor_tensor(out=vg.t[:, tb, :], in0=vg.t[:, tb, :], in1=lnb.t[:], op=ALU.add), reads=[vg.r, lnb.r], writes=[vg.r])
                S.op("act", lambda e, tb=tb, vn_=vn_: e.activation(out=vn_.t[:], in_=vg.t[:, tb, :], func=AF.Copy), reads=[vg.r], writes=[vn_.r])
                if T.kind == "S":
                    S.op("act", lambda e, tb=tb: e.dma_start(out=vn_out[l, tb], in_=vg.t[:, tb, :]), reads=[vg.r], dma=True)

            def spatial_block(tb):
                vn_ = vnb[tb % 2]
                if T.kind == "P":
                    blocks = [(0, 128, tb * 128)]
                else:
                    blocks = [(64 * b, 64, tb * 128 + 64 * b) for b in range(2)]
                for half in range(2):
                    pb = gbank()
                    for gi in range(4):
                        g = half * 4 + gi
                        for (prow, bl, qoff) in blocks:
                            wsrc = wsb if prow == 0 else wsb2
                            oc_ = gi * 128 + (qoff - tb * 128)
                            S.op("pe", lambda e, g=g, prow=prow, bl=bl, oc_=oc_, wsrc=wsrc, pb=pb, vn_=vn_: e.matmul(
                                pb.t[:, oc_:oc_ + bl], vn_.t[prow:prow + bl, g * 128:(g + 1) * 128], wsrc.t[prow:prow + bl, g, 0:bl], start=True, stop=False),
                                reads=[vn_.r, wsrc.r], writes=[pb.r])
                            S.op("pe", lambda e, g=g, bl=bl, oc_=oc_, pb=pb: e.matmul(pb.t[:, oc_:oc_ + bl], cb("ones")[0:1, :], bhi.t[0:1, g * 128:g * 128 + bl],
                                                                                     start=False, stop=False), reads=[cstb.r, bhi.r], writes=[pb.r])
                            S.op("pe", lambda e, g=g, bl=bl, oc_=oc_, pb=pb: e.matmul(pb.t[:, oc_:oc_ + bl], cb("ones")[0:1, :], blo.t[0:1, g * 128:g * 128 + bl],
                                                                                     start=False, stop=True), reads=[cstb.r, blo.r], writes=[pb.r])
                    S.op("dve", lambda e, pb=pb, half=half, tb=tb: e.tensor_tensor(
                        out=ya.t[:, half * 4:half * 4 + 4, tb * 128:(tb + 1) * 128], in0=pb.t[:, 0:512].rearrange("p (g q) -> p g q", g=4),
                        in1=uT.t[:, half * 4:half * 4 + 4, tb * 128:(tb + 1) * 128], op=ALU.mult),
                        reads=[pb.r, uT.r], writes=[ya.r])

            ln_block(0)
            for tb in range(ntb):
                if tb + 1 < ntb:
                    ln_block(tb + 1)
                spatial_block(tb)
            for mg in range(8):
                G, gr = wget(l, "G%d" % mg)
                P_, pr = wget(l, "P%d" % mg)
                for mm in range(2):
                    m = 2 * mg + mm
                    pga = gemm_fm(G, gr, mm * 128, lambda kc: hT.t[:, kc, 0:w], 16, [hT.r], w)
                    pgb = gemm_fm(G, gr, 256 + mm * 128, lambda kc: hT.t[:, kc, 0:w], 16, [hT.r], w)
                    ppa = gemm_fm(P_, pr, mm * 128, lambda kc: ya.t[:, kc, 0:w], 8, [ya.r], w)
                    ppb = gemm_fm(P_, pr, 256 + mm * 128, lambda kc: yb.t[:, kc, 0:w], 8, [yb.r], w)
                    sa, sb_ = Ct[2 + (m % 2) * 2], Ct[3 + (m % 2) * 2]
                    S.op("act", lambda e, pga=pga, sa=sa: e.activation(out=sa.t[:, 0:w], in_=pga.t[:, 0:w], func=AF.Sigmoid), reads=[pga.r], writes=[sa.r])
                    S.op("act", lambda e, pgb=pgb, sb_=sb_: e.activation(out=sb_.t[:, 0:w], in_=pgb.t[:, 0:w], func=AF.Sigmoid), reads=[pgb.r], writes=[sb_.r])
                    S.op("dve", lambda e, ppa=ppa, sa=sa: e.tensor_tensor(out=sa.t[:, 0:w], in0=ppa.t[:, 0:w], in1=sa.t[:, 0:w], op=ALU.mult),
                         reads=[ppa.r, sa.r], writes=[sa.r])
                    S.op("dve", lambda e, ppb=ppb, sb_=sb_: e.tensor_tensor(out=sb_.t[:, 0:w], in0=ppb.t[:, 0:w], in1=sb_.t[:, 0:w], op=ALU.mult),
                         reads=[ppb.r, sb_.r], writes=[sb_.r])
                    S.op("pool", lambda e, sa=sa, sb_=sb_, m=m: e.tensor_tensor(out=mixT.t[:, m, 0:w], in0=sa.t[:, 0:w], in1=sb_.t[:, 0:w], op=ALU.add),
                         reads=[sa.r, sb_.r], writes=[mixT.r])
            for j in range(4):
                blk, br = wget(l, "O%d" % j)
                for m in range(4):
                    pb = gemm_fm(blk, br, m * 128, lambda kc: mixT.t[:, kc, 0:w], 16, [mixT.r], w)
                    c = 4 * j + m
                    S.op("dve", lambda e, pb=pb, c=c: e.tensor_tensor(out=xT.t[:, c, 0:w], in0=pb.t[:, 0:w], in1=xT.t[:, c, 0:w], op=ALU.add),
                         reads=[pb.r, xTr[c]], writes=[xTr[c]])
            rmsnorm(xTr, lambda c: xT.t[:, c, 0:w], w, "norm2", l, lambda c: hT.t[:, c, 0:w], [hT.r] * 16)
            for q in range(4):
                for j in range(4):
                    blk, br = wget(l, "U%d_%d" % (q, j))
                    for m in range(4):
                        pb = gemm_fm(blk, br, m * 128, lambda kc: hT.t[:, kc, 0:w], 16, [hT.r], w)
                        t = Ct[(4 * j + m) % 2]
                        if (4 * j + m) % 2 == 0:
                            S.op("act", lambda e, pb=pb, t=t: e.activation(out=t.t[:, 0:w], in_=pb.t[:, 0:w], func=AF.Relu), reads=[pb.r], writes=[t.r])
                        else:
                            S.op("dve", lambda e, pb=pb, t=t: e.tensor_scalar(out=t.t[:, 0:w], in0=pb.t[:, 0:w], scalar1=0.0, scalar2=None, op0=ALU.max),
                                 reads=[pb.r], writes=[t.r])
                        S.op("pool", lambda e, t=t, j=j, m=m: e.tensor_tensor(out=hid.t[:, 4 * j + m, 0:w], in0=t.t[:, 0:w], in1=t.t[:, 0:w], op=ALU.mult),
                             reads=[t.r], writes=[hid.r])
                for j in range(4):
                    blk, br = wget(l, "D%d_%d" % (q, j))
                    for m in range(4):
                        pb = gemm_fm(blk, br, m * 128, lambda kc: hid.t[:, kc, 0:w], 16, [hid.r], w)
                        c = 4 * j + m
                        S.op("dve", lambda e, pb=pb, c=c: e.tensor_tensor(out=xT.t[:, c, 0:w], in0=pb.t[:, 0:w], in1=xT.t[:, c, 0:w], op=ALU.add),
                             reads=[pb.r, xTr[c]], writes=[xTr[c]])
            if not last_layer:
                dst = x1_scr.rearrange("(c p) t -> p c t", p=128)
                for c in range(16):
                    x1res[(T.idx, c)] = Res("x1")
                    S.op("act", lambda e, c=c: e.dma_start(out=dst[:, c, T.c0:T.c0 + w], in_=xT.t[:, c, 0:w]), reads=[xTr[c]], writes=[x1res[(T.idx, c)]], dma=True)
            else:
                rmsnorm(xTr, lambda c: xT.t[:, c, 0:w], w, "norm_f", 0, lambda c: xT.t[:, c, 0:w], xTr)
                dst = yT_out.rearrange("(c p) t -> p c t", p=128)
                for c in range(16):
                    S.op("act", lambda e, c=c: e.dma_start(out=dst[:, c, T.c0:T.c0 + w], in_=xT.t[:, c, 0:w]), reads=[xTr[c]], dma=True)

        for l in range(n_layers):
            phase[0] = 0
            layer_params(l)
            for T in tiles:
                phaseA(l, T)
            S.op("act", lambda e, l=l: e.dma_start(out=tsh_out[l], in_=plast.t[:]), reads=[plast.r], dma=True)
            S.barrier()
            phase[0] = 1
            layer_params_C(l)
            for T in tiles:
                phaseC(l, T, l == n_layers - 1)
            S.barrier()
        S.emit()
    return nc


def _cols(v):
    v = np.asarray(v, np.float32).reshape(-1, 128)
    return np.ascontiguousarray(v.T)


def _pvec(inp):
    pvv = np.zeros((128, NPV), np.float32)
    for l in range(NL):
        pvv[:, PV[("norm1", l)]:PV[("norm1", l)] + 16] = _cols(inp["norm1"][l])
        pvv[:, PV[("norm2", l)]:PV[("norm2", l)] + 16] = _cols(inp["norm2"][l])
        mu = np.asarray(inp["mu_shift"][l], np.float32)
        mo = PV[("mu", l)]
        pvv[:, mo:mo + 24] = _cols(mu[0:3072])
        for (cid, off, rows) in LORA:
            pvv[0:rows, mo + cid] = mu[off:off + rows]
        for nm in ("w0", "a0", "k_k", "k_a", "gn_g", "gn_b"):
            pvv[:, PV[(nm, l)]:PV[(nm, l)] + 8] = _cols(inp[nm][l])
        pvv[:, PV[("r_k", l)]:PV[("r_k", l)] + 8] = _cols(np.asarray(inp["r_k"][l]).reshape(-1))
    pvv[:, PV[("norm_f", 0)]:PV[("norm_f", 0)] + 16] = _cols(inp["norm_f"])
    return pvv


def _tsh_layout(ts):
    L, n, _ = ts.shape
    out = np.zeros((L, 128, NPCH, n), np.float32)
    out[:, :, 0:24, :] = ts[:, :, 0:3072].reshape(L, n, 24, 128).transpose(0, 3, 2, 1)
    for (cid, off, rows) in LORA:
        out[:, 0:rows, cid, :] = ts[:, :, off:off + rows].transpose(0, 2, 1)
    return out


def _tsh_unlayout(t):
    L, _, _, n = t.shape
    out = np.zeros((L, n, DSH), np.float32)
    out[:, :, 0:3072] = t[:, :, 0:24, :].transpose(0, 3, 2, 1).reshape(L, n, 3072)
    for (cid, off, rows) in LORA:
        out[:, :, off:off + rows] = t[:, 0:rows, cid, :].transpose(0, 2, 1)
    return out


def _wkv_layout(s):
    L, n = s.shape[:2]
    out = np.zeros((L, n, 8, 128, 128), np.float32)
    sT = s.transpose(0, 1, 2, 4, 3).reshape(L, n, 8, 2, 64, 64)
    out[:, :, :, 0:64, 0:64] = sT[:, :, :, 0]
    out[:, :, :, 64:128, 64:128] = sT[:, :, :, 1]
    return out


def _wkv_unlayout(o):
    L, n = o.shape[:2]
    s = np.zeros((L, n, 8, 2, 64, 64), np.float32)
    s[:, :, :, 0] = o[:, :, :, 0:64, 0:64]
    s[:, :, :, 1] = o[:, :, :, 64:128, 64:128]
    return np.ascontiguousarray(s.reshape(L, n, 16, 64, 64).transpose(0, 1, 2, 4, 3))


_PROG = {}


def _program(key, **kw):
    if key not in _PROG:
        _PROG[key] = build_program(**kw)
    return _PROG[key]


def make_in_maps(inp, seg_x, seg_prev, samp_ids):
    f = lambda a: np.ascontiguousarray(np.asarray(a, np.float32))
    shared = {
        "pvec": _pvec(inp), "cst": CST.copy(),
        "ln_v_g": f(inp["ln_v_g"]), "ln_v_b": f(inp["ln_v_b"]),
        "b_s": f(inp["b_s"]).reshape(NL, 1, 1024),
        "w_sT": np.ascontiguousarray(f(inp["w_s"]).transpose(0, 3, 1, 2)),
        "w2": f(inp["w2"]), "a2": f(inp["a2"]), "g2": f(inp["g2"]),
    }
    for k in WSHAPES:
        shared[k] = f(inp[k])
    xs = f(inp["x_sample"])
    ts = f(inp["state_tshift"])
    wk = f(inp["state_wkv"])
    maps = []
    for c in range(len(seg_x)):
        ids = samp_ids[c]
        xa = np.concatenate([seg_x[c]] + [xs[i] for i in ids], axis=0)
        m = dict(shared)
        m["xT"] = np.ascontiguousarray(xa.T)
        m["xprev"] = _cols(seg_prev[c])
        m["tsh_s"] = _tsh_layout(ts[:, ids, :])
        m["wkv_s"] = _wkv_layout(wk[:, ids])
        maps.append(m)
    return maps


def kernel(**inp):
    xp = np.asarray(inp["x_prompt"], np.float32)
    B, SEQ, _ = xp.shape
    nb = np.asarray(inp["x_sample"]).shape[0]
    ncores = 8
    NP = SEQ
    NPT = NP // 512
    NS = nb // ncores
    seg_x, seg_prev, samp = [], [], []
    zeros = np.zeros((NP, D), np.float32)
    for c in range(ncores):
        seg_x.append(xp[c] if c < B else zeros)
        seg_prev.append(np.zeros(D, np.float32))
        samp.append(list(range(c * NS, (c + 1) * NS)))
    maps = make_in_maps(inp, seg_x, seg_prev, samp)
    nc = _program(("full", NPT, NS), NPT=NPT, NS=NS)
    res = run_bass_kernel_spmd(nc, maps, core_ids=list(range(ncores))).results
    y_p = np.zeros((B, SEQ, D), np.float32)
    y_s = np.zeros((nb, 64, D), np.float32)
    tsh_p = np.zeros((NL, B, DSH), np.float32)
    wkv_p = np.zeros((NL, B, 16, 64, 64), np.float32)
    tsh_s = np.zeros((NL, nb, DSH), np.float32)
    wkv_s = np.zeros((NL, nb, 16, 64, 64), np.float32)
    vn_s = np.zeros((NL, nb, 64, DA), np.float32)
    for c in range(ncores):
        r = res[c]
        yT = r["yT"]
        ids = samp[c]
        y_s[ids] = yT[:, NP:].T.reshape(NS, 64, D)
        tso = _tsh_unlayout(r["tsh_o"])
        wko = _wkv_unlayout(r["wkv_o"])
        tsh_s[:, ids] = tso[:, 1:]
        wkv_s[:, ids] = wko[:, 1:]
        vn_s[:, ids] = r["vn_o"].reshape(NL, NS, 64, DA)
        if c < B:
            y_p[c] = yT[:, 0:NP].T
            tsh_p[:, c] = tso[:, 0]
            wkv_p[:, c] = wko[:, 0]
    return (y_p, y_s, tsh_p, wkv_p, tsh_s, wkv_s, vn_s)
```
